# Optimizing a Trainium2 kernel written in Bass

```python
import math
import jax
import jax.numpy as jnp
from jax import lax
import numpy as np

D_MODEL = 1024
BATCH = 4
SEQ = 4096
DEPTH = 2

D_INNER = 2 * D_MODEL
N_BRANCH = 4
BRANCH_W = D_INNER // N_BRANCH
HGRN_HEADS = 4
RET_HEADS = 4
RET_DECAY_EXP0 = 5.0
ROPE_BASE = 10000.0
SSM_HEAD_DIM = 64
SSM_HEADS = BRANCH_W // SSM_HEAD_DIM
SSM_GROUPS = 2
SSM_STATE = 128
SSM_CONV = 4
SSM_CONV_CH = BRANCH_W + 2 * SSM_GROUPS * SSM_STATE
SSM_DT_MIN = 0.001
SSM_DT_MAX = 0.1
GLA_HEADS = 4
GLA_KEY_W = BRANCH_W // 2
GLA_LOWRANK = 16
GLA_GATE_TEMP = 16.0
CHUNK = 64
EPS = 1e-6
IN_SPLITS = (
    BRANCH_W, BRANCH_W, BRANCH_W, BRANCH_W,
    BRANCH_W, BRANCH_W, BRANCH_W, BRANCH_W,
    BRANCH_W, SSM_CONV_CH, SSM_HEADS,
    GLA_KEY_W, GLA_KEY_W, BRANCH_W, BRANCH_W, GLA_LOWRANK,
)
IN_PROJ_W = 8 * BRANCH_W + BRANCH_W + SSM_CONV_CH + SSM_HEADS + 2 * GLA_KEY_W + 2 * BRANCH_W + GLA_LOWRANK

kernel_name = 'hybrid_parallel_heads_decoder'


def rmsnorm(x, g):
    xf = x.astype(jnp.float32)
    y = xf * lax.rsqrt(jnp.mean(xf * xf, axis=-1, keepdims=True) + EPS)
    return (y * g.astype(jnp.float32)).astype(x.dtype)


def group_rmsnorm(y, g, n_groups):
    b, l, w = y.shape
    yf = y.astype(jnp.float32).reshape(b, l, n_groups, w // n_groups)
    yf = yf * lax.rsqrt(jnp.mean(yf * yf, axis=-1, keepdims=True) + EPS)
    return (yf.reshape(b, l, w) * g.astype(jnp.float32)).astype(y.dtype)


def to_heads(t, n_heads):
    b, l, w = t.shape
    return t.reshape(b, l, n_heads, w // n_heads).transpose(0, 2, 1, 3)


def from_heads(t):
    b, h, l, d = t.shape
    return t.transpose(0, 2, 1, 3).reshape(b, l, h * d)


def to_chunks(t):
    b, h, l = t.shape[:3]
    return jnp.moveaxis(t.reshape(b, h, l // CHUNK, CHUNK, *t.shape[3:]), 2, 0)


def from_chunks(t):
    n, b, h, c, d = t.shape
    return jnp.moveaxis(t, 0, 2).reshape(b, h, n * c, d)


def chunked_scalar_decay(q, k, v, log_a):
    out_dtype = v.dtype
    q, k, v, log_a = (t.astype(jnp.float32) for t in (q, k, v, log_a))
    b, h, _, dk = q.shape
    dv = v.shape[-1]
    causal = jnp.tril(jnp.ones((CHUNK, CHUNK), dtype=bool))

    def step(state, inp):
        qc, kc, vc, ac = inp
        cum = jnp.cumsum(ac, axis=-1)
        rel = jnp.where(causal, cum[..., :, None] - cum[..., None, :], -jnp.inf)
        scores = jnp.einsum('bhtd,bhsd->bhts', qc, kc) * jnp.exp(rel)
        o = (jnp.einsum('bhts,bhsv->bhtv', scores, vc)
             + jnp.einsum('bhtd,bhdv->bhtv', qc * jnp.exp(cum)[..., None], state))
        k_to_end = kc * jnp.exp(cum[..., -1:] - cum)[..., None]
        state = (state * jnp.exp(cum[..., -1])[..., None, None]
                 + jnp.einsum('bhsd,bhsv->bhdv', k_to_end, vc))
        return state, o

    state0 = jnp.zeros((b, h, dk, dv), jnp.float32)
    _, o = lax.scan(step, state0, (to_chunks(q), to_chunks(k), to_chunks(v), to_chunks(log_a)))
    return from_chunks(o).astype(out_dtype)


def chunked_vector_decay(q, k, v, log_f):
    out_dtype = v.dtype
    q, k, v, log_f = (t.astype(jnp.float32) for t in (q, k, v, log_f))
    b, h, _, dk = q.shape
    dv = v.shape[-1]
    causal = jnp.tril(jnp.ones((CHUNK, CHUNK), dtype=bool))[..., None]

    def step(state, inp):
        qc, kc, vc, fc = inp
        cum = jnp.cumsum(fc, axis=-2)
        rel = jnp.where(causal, cum[..., :, None, :] - cum[..., None, :, :], -jnp.inf)
        scores = jnp.einsum('bhtd,bhsd,bhtsd->bhts', qc, kc, jnp.exp(rel))
        o = (jnp.einsum('bhts,bhsv->bhtv', scores, vc)
             + jnp.einsum('bhtd,bhdv->bhtv', qc * jnp.exp(cum), state))
        k_to_end = kc * jnp.exp(cum[..., -1:, :] - cum)
        state = (state * jnp.exp(cum[..., -1, :])[..., None]
                 + jnp.einsum('bhsd,bhsv->bhdv', k_to_end, vc))
        return state, o

    state0 = jnp.zeros((b, h, dk, dv), jnp.float32)
    _, o = lax.scan(step, state0, (to_chunks(q), to_chunks(k), to_chunks(v), to_chunks(log_f)))
    return from_chunks(o).astype(out_dtype)


def causal_depthwise_conv(u, w, bias):
    y = lax.conv_general_dilated(
        u, w[:, None, :].astype(u.dtype), window_strides=(1,),
        padding=[(SSM_CONV - 1, 0)], dimension_numbers=('NWC', 'WIO', 'NWC'),
        feature_group_count=u.shape[-1])
    return y + bias.astype(u.dtype)


def rotary(t):
    d = t.shape[-1]
    l = t.shape[2]
    inv_freq = ROPE_BASE ** (-jnp.arange(0, d, 2, dtype=jnp.float32) / d)
    ang = jnp.arange(l, dtype=jnp.float32)[:, None] * inv_freq[None, :]
    cos, sin = jnp.cos(ang), jnp.sin(ang)
    tf = t.astype(jnp.float32)
    t1, t2 = tf[..., : d // 2], tf[..., d // 2:]
    return jnp.concatenate([t1 * cos - t2 * sin, t1 * sin + t2 * cos], axis=-1).astype(t.dtype)


def hgrn2_branch(q_in, f_in, i_in, g_in, lower_bound, onorm_g):
    q = to_heads(jax.nn.silu(q_in), HGRN_HEADS)
    lb = lower_bound.astype(jnp.float32)
    log_f = jnp.logaddexp(jnp.log(lb), jnp.log1p(-lb) + jax.nn.log_sigmoid(f_in.astype(jnp.float32)))
    k = -jnp.expm1(log_f)
    o = chunked_vector_decay(q, to_heads(k, HGRN_HEADS), to_heads(i_in, HGRN_HEADS),
                             to_heads(log_f, HGRN_HEADS))
    return group_rmsnorm(from_heads(o), onorm_g, HGRN_HEADS) * jax.nn.silu(g_in)


def retention_branch(q_in, k_in, v_in, g_in, onorm_g):
    dk = BRANCH_W // RET_HEADS
    q = rotary(to_heads(q_in, RET_HEADS))
    k = rotary(to_heads(k_in, RET_HEADS)) * (dk ** -0.5)
    b, _, l, _ = q.shape
    log_gamma = jnp.log1p(-jnp.exp2(-(RET_DECAY_EXP0 + jnp.arange(RET_HEADS, dtype=jnp.float32))))
    log_a = jnp.broadcast_to(log_gamma[None, :, None], (b, RET_HEADS, l))
    o = chunked_scalar_decay(q, k, to_heads(v_in, RET_HEADS), log_a)
    return group_rmsnorm(from_heads(o), onorm_g, RET_HEADS) * jax.nn.silu(g_in)


def ssd_branch(z_in, xbc_in, dt_in, conv_w, conv_b, dt_bias, a_log, d_skip, norm_g):
    xbc = jax.nn.silu(causal_depthwise_conv(xbc_in, conv_w, conv_b))
    xs, bmat, cmat = jnp.split(xbc, [BRANCH_W, BRANCH_W + SSM_GROUPS * SSM_STATE], axis=-1)
    b, l, _ = xs.shape
    heads_per_group = SSM_HEADS // SSM_GROUPS

    def group_to_heads(m):
        m = m.reshape(b, l, SSM_GROUPS, SSM_STATE).transpose(0, 2, 1, 3)
        return jnp.repeat(m, heads_per_group, axis=1)

    dt = jax.nn.softplus(dt_in.astype(jnp.float32) + dt_bias.astype(jnp.float32)).transpose(0, 2, 1)
    a = -jnp.exp(a_log.astype(jnp.float32))
    xh = to_heads(xs, SSM_HEADS).astype(jnp.float32)
    y = chunked_scalar_decay(group_to_heads(cmat), group_to_heads(bmat),
                             xh * dt[..., None], dt * a[None, :, None])
    y = y + d_skip.astype(jnp.float32)[None, :, None, None] * xh
    y = from_heads(y).astype(z_in.dtype) * jax.nn.silu(z_in)
    return group_rmsnorm(y, norm_g, SSM_GROUPS)


def gla_branch(q_in, k_in, v_in, g_in, lr_in, w_gk2, b_gk2, onorm_g):
    dk = GLA_KEY_W // GLA_HEADS
    gk = jnp.einsum('blr,rk->blk', lr_in, w_gk2) + b_gk2
    log_f = jax.nn.log_sigmoid(gk.astype(jnp.float32)) / GLA_GATE_TEMP
    q = to_heads(q_in, GLA_HEADS) * (dk ** -0.5)
    o = chunked_vector_decay(q, to_heads(k_in, GLA_HEADS), to_heads(v_in, GLA_HEADS),
                             to_heads(log_f, GLA_HEADS))
    return group_rmsnorm(from_heads(o), onorm_g, GLA_HEADS) * jax.nn.silu(g_in)


def setup_inputs(seed: int = 0) -> dict:
    key = jax.random.key(seed)
    ks = jax.random.split(key, 20)
    f32 = jnp.float32

    def nrm(k, shape, scale):
        return scale * jax.random.normal(k, shape, f32)

    def gain(k, shape):
        return 1.0 + 0.01 * jax.random.normal(k, shape, f32)

    dt = jnp.exp(jax.random.uniform(ks[11], (DEPTH, SSM_HEADS), f32,
                                    math.log(SSM_DT_MIN), math.log(SSM_DT_MAX)))
    return {
        'x': jax.random.normal(ks[0], (BATCH, SEQ, D_MODEL), f32),
        'c': jax.random.normal(ks[1], (BATCH, D_MODEL), f32),
        'w_ada': nrm(ks[2], (DEPTH, D_MODEL, 3 * D_MODEL), 0.5 * D_MODEL ** -0.5),
        'b_ada': nrm(ks[3], (DEPTH, 3 * D_MODEL), 0.01),
        'norm_g': gain(ks[4], (DEPTH, D_MODEL)),
        'w_in': nrm(ks[5], (DEPTH, D_MODEL, IN_PROJ_W), D_MODEL ** -0.5),
        'hgrn_lb_logits': nrm(ks[6], (DEPTH, BRANCH_W), 0.1),
        'hgrn_onorm_g': gain(ks[7], (DEPTH, BRANCH_W)),
        'ret_onorm_g': gain(ks[8], (DEPTH, BRANCH_W)),
        'ssm_conv_w': nrm(ks[9], (DEPTH, SSM_CONV, SSM_CONV_CH), SSM_CONV ** -0.5),
        'ssm_conv_b': nrm(ks[10], (DEPTH, SSM_CONV_CH), 0.01),
        'ssm_dt_bias': dt + jnp.log(-jnp.expm1(-dt)),
        'ssm_a_log': jnp.log(jax.random.uniform(ks[12], (DEPTH, SSM_HEADS), f32, 1.0, 16.0)),
        'ssm_d': gain(ks[13], (DEPTH, SSM_HEADS)),
        'ssm_norm_g': gain(ks[14], (DEPTH, BRANCH_W)),
        'gla_w_gk2': nrm(ks[15], (DEPTH, GLA_LOWRANK, GLA_KEY_W), GLA_LOWRANK ** -0.5),
        'gla_b_gk2': nrm(ks[16], (DEPTH, GLA_KEY_W), 0.01),
        'gla_onorm_g': gain(ks[17], (DEPTH, BRANCH_W)),
        'w_out': nrm(ks[18], (DEPTH, D_INNER, D_MODEL), D_INNER ** -0.5),
        'final_g': gain(ks[19], (D_MODEL,)),
    }


def reference(x, c, w_ada, b_ada, norm_g, w_in, hgrn_lb_logits, hgrn_onorm_g, ret_onorm_g,
              ssm_conv_w, ssm_conv_b, ssm_dt_bias, ssm_a_log, ssm_d, ssm_norm_g,
              gla_w_gk2, gla_b_gk2, gla_onorm_g, w_out, final_g):
    c_act = jax.nn.silu(c)
    lb_cum = jnp.cumsum(jax.nn.softmax(hgrn_lb_logits.astype(jnp.float32), axis=0), axis=0)
    lower_bounds = lb_cum - lb_cum[0]
    split_at = [int(s) for s in np.cumsum(IN_SPLITS)[:-1]]
    for layer in range(DEPTH):
        mod = c_act @ w_ada[layer] + b_ada[layer]
        shift, scale, gate = jnp.split(mod[:, None, :], 3, axis=-1)
        h = rmsnorm(x, norm_g[layer]) * (1.0 + scale) + shift
        proj = jnp.einsum('bld,de->ble', h, w_in[layer])
        (aq, af, ai, ag, rq, rk, rv, rg, mz, mxbc, mdt,
         gq, gk, gv, gg, glr) = jnp.split(proj, split_at, axis=-1)
        y_a = hgrn2_branch(aq, af, ai, ag, lower_bounds[layer], hgrn_onorm_g[layer])
        y_b = retention_branch(rq, rk, rv, rg, ret_onorm_g[layer])
        y_c = ssd_branch(mz, mxbc, mdt, ssm_conv_w[layer], ssm_conv_b[layer], ssm_dt_bias[layer],
                         ssm_a_log[layer], ssm_d[layer], ssm_norm_g[layer])
        y_d = gla_branch(gq, gk, gv, gg, glr, gla_w_gk2[layer], gla_b_gk2[layer], gla_onorm_g[layer])
        y = jnp.concatenate([y_a, y_b, y_c, y_d], axis=-1)
        x = x + gate * jnp.einsum('ble,ed->bld', y, w_out[layer])
    return rmsnorm(x, final_g)
```

```python
import math
import os
BR = os.environ.get("BR", "hrgs")
STAGE = os.environ.get("STAGE", "")
SKIP = os.environ.get("SKIP", "")
GROUPG = (BR == "hrgs") and not os.environ.get("NOGROUPG")


class _Stop(Exception):
    pass


def chk(n):
    if STAGE == n:
        raise _Stop()
import numpy as np
from contextlib import ExitStack
import concourse.bass as bass
import concourse.mybir as mybir
from concourse.bass_utils import run_bass_kernel_spmd

F32 = mybir.dt.float32
BF16 = mybir.dt.bfloat16
AF = mybir.ActivationFunctionType
ALU = mybir.AluOpType
PE, ACT, DVE, POOL, SP = "pe", "act", "dve", "pool", "sp"
COMPUTE = (PE, ACT, DVE, POOL)

D = 1024
QS = float(2 ** 30)
QL = 30.0 * math.log(2.0)
NW = 7192
EPS = 1e-6
C_AQ, C_AF, C_AI, C_AG = 0, 512, 1024, 1536
C_RQ, C_RK, C_RV, C_RG = 2048, 2560, 3072, 3584
C_MZ, C_MX, C_DT = 4096, 4608, 5632
C_GQ, C_GK, C_GV, C_GG, C_LR = 5640, 5896, 6152, 6664, 7176
NF = 92


EXPAND = {"f%d" % i: tuple("f%d_%d" % (i, g) for g in range(4)) for i in range(5)}


class Rec:
    def __init__(self, nc, stack):
        self.nc = nc
        self.stack = stack
        self.ops = []
        self.writers = {}
        self.readers = {}
        self.dma_keys = {}

    def sb(self, name, shape, dt):
        return self.stack.enter_context(self.nc.sbuf_tensor("sb_" + name, list(shape), dt))

    def ps(self, name, shape, dt):
        return self.stack.enter_context(self.nc.psum_tensor("ps_" + name, list(shape), dt))

    def op(self, eng, fn, reads=(), writes=(), dma_key=None, dma_mode="serial"):
        idx = len(self.ops)
        deps = set()
        reads = [k for b in reads for k in EXPAND.get(b, (b,))]
        writes = [k for b in writes for k in EXPAND.get(b, (b,))]
        excl = [b for b in reads if b[0] == "P" or b == "TB"]
        if excl:
            reads = [b for b in reads if b not in excl]
            writes = list(writes) + [b for b in excl if b not in writes]
        for b in reads:
            deps.update(self.writers.get(b, {}).values())
        for b in writes:
            deps.update(self.writers.get(b, {}).values())
            deps.update(self.readers.get(b, ()))
        wk = eng if dma_key is None else ("dma", dma_key)
        for b in writes:
            self.writers.setdefault(b, {})[wk] = idx
            self.readers[b] = []
        for b in reads:
            self.readers.setdefault(b, []).append(idx)
        deps.discard(idx)
        seq = None
        if dma_key is not None:
            k = self.dma_keys.setdefault(dma_key, dict(mode=dma_mode, count=0))
            k["count"] += 1
            seq = k["count"]
        self.ops.append(dict(eng=eng, fn=fn, deps=deps, dma_key=dma_key, seq=seq, consumers=0, sig=None))
        return idx

    def emit(self):
        nc, ops = self.nc, self.ops

        def skip(p, o):
            return p["dma_key"] is None and o["dma_key"] is None and p["eng"] == PE and o["eng"] == PE

        for o in ops:
            for d in o["deps"]:
                if not skip(ops[d], o):
                    ops[d]["consumers"] += 1
        cnt = {e: 0 for e in COMPUTE}
        for o in ops:
            if o["dma_key"] is None and o["consumers"] > 0:
                cnt[o["eng"]] += 1
                o["sig"] = cnt[o["eng"]]
        sem = {}
        for e in COMPUTE:
            sem[e] = self.stack.enter_context(nc.semaphore("s_" + e))
        for k in self.dma_keys:
            sem[("dma", k)] = self.stack.enter_context(nc.semaphore("d_" + str(k)))
        known = {}
        for o in ops:
            me = o["eng"]
            need = {}
            for d in o["deps"]:
                p = ops[d]
                if p["dma_key"] is not None:
                    kk = ("dma", p["dma_key"])
                    info = self.dma_keys[p["dma_key"]]
                    val = 16 * (p["seq"] if info["mode"] == "serial" else info["count"])
                else:
                    if skip(p, o):
                        continue
                    kk, val = p["eng"], p["sig"]
                need[kk] = max(need.get(kk, 0), val)
            kn = known.setdefault(me, {})
            waits = []
            for kk, val in need.items():
                if kn.get(kk, 0) < val:
                    kn[kk] = val
                    waits.append((sem[kk], val))
            o["waits"] = waits
        by_eng = {e: [] for e in (PE, ACT, DVE, POOL, SP)}
        for o in ops:
            by_eng[o["eng"]].append(o)

        def run(eng_obj, lst):
            for o in lst:
                for (s, v) in o["waits"]:
                    eng_obj.wait_ge(s, v)
                ins = o["fn"](eng_obj)
                if ins is None:
                    continue
                if o["dma_key"] is not None:
                    ins.then_inc(sem[("dma", o["dma_key"])], 16)
                elif o["sig"] is not None:
                    ins.then_inc(sem[o["eng"]], 1)

        final_waits = [(sem[("dma", k)], 16 * v["count"]) for k, v in self.dma_keys.items()]

        with nc.Block() as block:
            @block.sync
            def _(e):
                run(e, by_eng[SP])
                for (s_, v_) in final_waits:
                    e.wait_ge(s_, v_)

            @block.tensor
            def _(e):
                run(e, by_eng[PE])

            @block.scalar
            def _(e):
                run(e, by_eng[ACT])

            @block.vector
            def _(e):
                run(e, by_eng[DVE])

            @block.gpsimd
            def _(e):
                run(e, by_eng[POOL])
                for (s_, v_) in final_waits:
                    e.wait_ge(s_, v_)
        return {e: len(by_eng[e]) for e in by_eng}


def build_program(n_tiles, n_layers, dbg=False):
    L = n_tiles * 128
    nc = bass.Bass("TRN2", target_bir_lowering=False)

    def din(name, shape):
        return nc.dram_tensor(name, list(shape), F32, kind="ExternalInput").ap()

    x_d = din("x", [L, D])
    cfm_d = din("cfm", [128, 8])
    w_in_d = din("w_in", [n_layers, D, NW])
    w_out_d = din("w_out", [n_layers, 2048, D])
    w_ada_d = din("w_ada", [n_layers, D, 3 * D])
    pfm_d = din("pfm", [n_layers, 128, NF])
    p8_d = din("p8", [n_layers, 8, 2])
    bgate_d = din("bgate", [n_layers, 128, D])
    wgk_d = din("wgk", [n_layers, 17, 256])
    finalg_d = din("finalg", [128, D])
    cmat_d = din("cmat", [128, 5 * 128])
    sel_d = din("sel", [8, 8 * 128])
    gqk_d = din("gqk", [128, 2 * 4 * 128])
    retsc_d = din("retsc", [128, 24])
    rot_d = din("rot", [n_tiles, 128, 1024])
    y_d = nc.dram_tensor("y", [L, D], F32, kind="ExternalOutput").ap()
    xs_d = nc.dram_tensor("xs", [L, D], F32, kind="Internal").ap()
    if dbg:
        dbg_d = nc.dram_tensor("dbg", [n_layers, n_tiles, 128, 16 * 128], F32, kind="ExternalOutput").ap()

    with ExitStack() as st:
        r = Rec(nc, st)
        sb, ps = r.sb, r.ps
        w_in = sb("w_in_sb", [128, 8, NW], BF16)
        w_out = sb("w_out_sb", [128, 8 if os.environ.get("SHRINK") else 16, D], BF16)
        xt = sb("xt", [128, D], F32)
        xn = sb("xn", [128, D], BF16)
        hT = sb("hT", [128, 8, 128], BF16)
        yT = sb("yT", [128, 16, 128], BF16)
        hq = sb("hq", [128, 4, 128], BF16)
        hg = sb("hg", [128, 4, 128], BF16)
        hgB = sb("hgB", [128, 4, 128], BF16)
        vv = sb("vv", [128, 512], BF16)
        qt = sb("qt", [128, 4, 128], BF16)
        kt = sb("kt", [128, 4, 128], BF16)
        ktT = sb("ktT", [128, 4, 128], BF16)
        pt = sb("pt", [128, 4, 128], BF16)
        gm = sb("gm", [128, 128], BF16)
        sp0 = sb("sp0", [128, 512], BF16)
        sp1 = sb("sp1", [128, 512], BF16)
        F = [sb("f%d" % i, [128, 4, 128], F32) for i in range(5)]
        ubuf = sb("ubuf", [128, 8, 131], BF16)
        xc = sb("xc", [128, 8, 128], BF16)
        S_hg = sb("S_hg", [128, 4, 128], F32)
        S_rt = sb("S_rt", [128, 4, 128], F32)
        S_sd = sb("S_sd", [128, 8, 64], F32)
        S_gl = sb("S_gl", [128, 2, 128], F32)
        cmat = sb("cmat", [128, 5, 128], BF16)
        ones32 = sb("ones32", [128, 128], F32)
        sel = sb("sel", [8, 8, 128], F32)
        gqk = sb("gqk", [128, 8, 128], BF16)
        retsc = sb("retsc", [128, 4, 6], F32)
        rot = sb("rot", [128, 2, 4, 128], F32)
        xt2 = sb("xt2", [128, D], F32)
        xts = [xt, xt2]
        pfm = sb("pfm", [128, NF], F32)
        p8 = sb("p8", [8, 2], F32)
        cfm = sb("cfm", [128, 8], F32)
        cact = sb("cact", [128, 8, 2], F32)
        gs = sb("gs", [128, 8], F32)
        sh = sb("sh", [128, 8], F32)
        lbt = sb("lbt", [128, 4, 3], F32)
        sm = sb("sm", [128, 16], F32)
        sc = sb("sc", [128, 4, 6], F32)
        negm = sb("negm", [128, 4, 2], F32)
        wgk = sb("wgk", [32, 256], BF16)
        lrT = sb("lrT", [32, 128], BF16)
        d8 = sb("d8", [8, 4, 128], F32)
        a8 = sb("a8", [8, 2], F32)
        ssdT = sb("ssdT", [128, 3, 8], F32)
        ss = sb("ss", [128, 4], F32)
        P = [ps("P%d" % i, [128, 512], F32) for i in range(6)]
        TB = ps("TB", [128, 1024], BF16)
        P7 = ps("P7", [128, 512], F32)
        ident, mask01, perm, ones_bf = (cmat[:, i, :] for i in range(4))

        def act(out, in_, func, reads, writes, scale=1.0, bias=0.0, accum=None):
            kw = {}
            if accum is not None:
                kw["accum_out"] = accum
            r.op(ACT, lambda e: e.activation(out=out, in_=in_, func=func, scale=scale, bias=bias, **kw), reads, writes)

        def tt(eng, out, in0, in1, op, reads, writes):
            r.op(eng, lambda e: e.tensor_tensor(out=out, in0=in0, in1=in1, op=op), reads, writes)

        def ts(eng, out, in0, s1, s2, op0, op1, reads, writes):
            if s2 is None:
                r.op(eng, lambda e: e.tensor_scalar(out=out, in0=in0, scalar1=s1, scalar2=None, op0=op0), reads, writes)
            else:
                r.op(eng, lambda e: e.tensor_scalar(out=out, in0=in0, scalar1=s1, scalar2=s2, op0=op0, op1=op1), reads, writes)

        def stt(out, in0, scalar, in1, op0, op1, reads, writes):
            r.op(DVE, lambda e: e.scalar_tensor_tensor(out=out, in0=in0, scalar=scalar, in1=in1, op0=op0, op1=op1), reads, writes)

        def cp(eng, out, in_, reads, writes):
            r.op(eng, lambda e: e.tensor_copy(out=out, in_=in_), reads, writes)

        def mm(out, lhsT, rhs, start, stop, reads, writes):
            r.op(PE, lambda e: e.matmul(out, lhsT=lhsT, rhs=rhs, start=start, stop=stop), reads, writes)

        def tr(out, in_, reads, writes):
            r.op(PE, lambda e: e.transpose(out=out, in_=in_, identity=ident), list(reads) + ["cmat"], writes)

        def dma(eng, out, in_, reads, writes, key, mode="serial"):
            if key[:3] in SKIP.split(","):
                return
            r.op(eng, lambda e: e.dma_start(out=out, in_=in_), reads, writes, dma_key=key, dma_mode=mode)

        def proj_fm(pbank, pname, col0, ngroups, width=128, pcol0=0):
            for g in range(ngroups):
                for k in range(8):
                    mm(pbank[0:width, pcol0 + g * 128: pcol0 + (g + 1) * 128],
                       w_in[:, k, col0 + g * width: col0 + (g + 1) * width], hT[:, k, :],
                       k == 0, k == 7, ["w_in%d" % k, "hT"], [pname])

        def proj_tm(pbank, pname, col0, n):
            for k in range(8):
                mm(pbank[:, 0:n], hT[:, k, :], w_in[:, k, col0:col0 + n], k == 0, k == 7, ["w_in%d" % k, "hT"], [pname])

        Ff = [f[:].rearrange("p a b -> p (a b)") for f in F]
        dma(SP, Ff[0], cmat_d[:, 0:512], [], ["f0"], "c1", "batch")
        dma(SP, Ff[3][:, 0:128], cmat_d[:, 512:640], [], ["f3"], "c1", "batch")
        dma(SP, Ff[1], gqk_d[:, 0:512], [], ["f1"], "c1", "batch")
        dma(SP, Ff[2], gqk_d[:, 512:1024], [], ["f2"], "c1", "batch")
        dma(SP, sel[:].rearrange("p a b -> p (a b)"), sel_d, [], ["sel"], "c1", "batch")
        dma(SP, retsc[:].rearrange("p a b -> p (a b)"), retsc_d, [], ["retsc"], "c1", "batch")
        dma(SP, cfm[:], cfm_d, [], ["cfm"], "c1", "batch")
        cp(DVE, cmat[:, 0:4, :].rearrange("p a b -> p (a b)"), Ff[0], ["f0"], ["cmat"])
        cp(DVE, cmat[:, 4, :], Ff[3][:, 0:128], ["f3"], ["cmat"])
        cp(DVE, gqk[:, 0:4, :].rearrange("p a b -> p (a b)"), Ff[1], ["f1"], ["gqk"])
        cp(DVE, gqk[:, 4:8, :].rearrange("p a b -> p (a b)"), Ff[2], ["f2"], ["gqk"])
        r.op(DVE, lambda e: e.memset(ones32[:], 1.0), [], ["ones32"])
        r.op(DVE, lambda e: e.memset(lrT[:], 1.0), [], ["lrT"])
        act(sm[:, 0:8], cfm[:], AF.Silu, ["cfm"], ["sm"])
        cp(DVE, cact[:], sm[:, 0:8].unsqueeze(2).to_broadcast([128, 8, 2]), ["sm"], ["cact"])
        wl = dict(ci=0, ring=[0, 1, 2])
        cast_engs = [ACT, DVE, POOL]

        def wload(dst_ap, src_ap, npart, width, bufname):
            ci = wl["ci"]
            j = wl["ring"][ci % len(wl["ring"])]
            sname = "f%d" % j
            st_ap = Ff[j][0:npart, 0:width]
            dma(SP, st_ap, src_ap, [], [sname], "wst%d" % j)
            eng = cast_engs[ci % 3]
            if eng == ACT:
                act(dst_ap, st_ap, AF.Copy, [sname], [bufname])
            else:
                cp(eng, dst_ap, st_ap, [sname], [bufname])
            wl["ci"] = ci + 1

        try:
          for l in range(n_layers):
              last = (l == n_layers - 1)
              xin = x_d if l == 0 else xs_d
              dma(SP, pfm[:], pfm_d[l], [], ["pfm"], "prm%d" % l, "batch")
              dma(SP, p8[:], p8_d[l], [], ["p8"], "prm%d" % l, "batch")
              dma(SP, xt2[:], bgate_d[l], [], ["xt1"], "prm%d" % l, "batch")
              w_in_v = w_in_d[l].rearrange("(k p) n -> p k n", p=128)
              w_ada_v = w_ada_d[l].rearrange("(k p) n -> p k n", p=128)
              cactbc = xt[:].rearrange("p (k n) -> p k n", k=8)
              cp(DVE, cactbc, sm[:, 0:8].unsqueeze(2).to_broadcast([128, 8, 128]), ["sm"], ["xt0"])
              wl["ring"] = [0, 1, 2]

              def ada_block(blk):
                  cg, half = blk // 2, blk % 2
                  j = 3 + (blk % 2)
                  sname = "f%d" % j
                  dma(SP, F[j][:], w_ada_v[:, half * 4:(half + 1) * 4, cg * 128:(cg + 1) * 128], [], [sname], "wst%d" % j)
                  for kk in range(4):
                      k = half * 4 + kk
                      if cg < 16:
                          mm(P7[:, cg * 2:cg * 2 + 2], F[j][:, kk, :], cact[:, k, :], k == 0, k == 7, [sname, "cact"], ["P7"])
                      else:
                          g = cg - 16
                          pb, pn = (P[0], "P0") if g < 4 else (P[1], "P1")
                          mm(pb[:, (g % 4) * 128:(g % 4 + 1) * 128], cactbc[:, k, :], F[j][:, kk, :], k == 0, k == 7,
                             ["xt0", sname], [pn])

              chunks = [(k, c0) for k in range(8) for c0 in range(0, NW, 512)]
              nblk = 0
              for i, (k, c0) in enumerate(chunks):
                  wd = min(512, NW - c0)
                  wload(w_in[:, k, c0:c0 + wd], w_in_v[:, k, c0:c0 + wd], 128, wd, "w_in%d" % k)
                  while nblk < 48 and nblk < (i + 1) * 48 // len(chunks):
                      ada_block(nblk)
                      nblk += 1
              while nblk < 48:
                  ada_block(nblk)
                  nblk += 1
              wload(wgk[0:17, :], wgk_d[l], 17, 256, "wgk")
              wl["ring"] = [0, 1, 2, 3, 4]
              chk("A")
              p7v = P7[:, 0:32].rearrange("p (a b) -> p a b", b=2)
              tt(DVE, sh[:], p7v[:, 0:8, 0], pfm[:, 8:16], ALU.add, ["P7", "pfm"], ["sh"])
              tt(DVE, gs[:], p7v[:, 8:16, 0], pfm[:, 16:24], ALU.add, ["P7", "pfm"], ["gs"])
              stt(gs[:], gs[:], 1.0, pfm[:, 0:8], ALU.add, ALU.mult, ["gs", "pfm"], ["gs"])
              tt(DVE, xt2[:, 0:512], P[0][:], xt2[:, 0:512], ALU.add, ["P0", "xt1"], ["xt1"])
              tt(DVE, xt2[:, 512:1024], P[1][:], xt2[:, 512:1024], ALU.add, ["P1", "xt1"], ["xt1"])
              w_out_v = w_out_d[l].rearrange("(k p) n -> p k n", p=128)
              for k in range(16):
                  for hf in range(2):
                      ci = wl["ci"]
                      j = ci % 5
                      sname = "f%d" % j
                      dma(SP, Ff[j], w_out_v[:, k, hf * 512:(hf + 1) * 512], [], [sname], "wst%d" % j)
                      tt(DVE if ci % 2 == 0 else POOL, w_out[:, k, hf * 512:(hf + 1) * 512], Ff[j], xt2[:, hf * 512:(hf + 1) * 512],
                         ALU.mult, [sname, "xt1"], ["w_out%d" % k])
                      wl["ci"] = ci + 1
              if l == 0:
                  r.op(DVE, lambda e: e.memset(lbt[:, :, 0], 0.0), [], ["lbt"])
              else:
                  act(sm[:, 8:16], pfm[:, 24:32], AF.Exp, ["pfm"], ["sm"])
                  tt(DVE, sm[:, 8:12], sm[:, 8:12], sm[:, 12:16], ALU.add, ["sm"], ["sm"])
                  r.op(DVE, lambda e: e.reciprocal(out=sm[:, 8:12], in_=sm[:, 8:12]), ["sm"], ["sm"])
                  tt(DVE, lbt[:, :, 0], sm[:, 8:12], sm[:, 12:16], ALU.mult, ["sm"], ["lbt"])
              ts(DVE, lbt[:, :, 1], lbt[:, :, 0], -1.0, 1.0, ALU.mult, ALU.add, ["lbt"], ["lbt"])
              ts(DVE, lbt[:, :, 2], lbt[:, :, 1], -1.0, None, ALU.mult, None, ["lbt"], ["lbt"])
              act(a8[:, 0:1], p8[:, 1:2], AF.Exp, ["p8"], ["a8"])
              ts(DVE, a8[:, 1:2], a8[:, 0:1], -1.0, None, ALU.mult, None, ["a8"], ["a8"])
              for (S, nm) in ((S_hg, "S_hg"), (S_rt, "S_rt"), (S_sd, "S_sd"), (S_gl, "S_gl")):
                  r.op(POOL, lambda e, S=S: e.memset(S[:], 0.0), [], [nm])
              r.op(POOL, lambda e: e.memset(ubuf[:, :, 0:3], 0.0), [], ["ubuf"])

              chk("B")
              for ti in range(n_tiles):
                  t0 = ti * 128
                  X = xts[ti % 2]
                  xk = "xt%d" % (ti % 2)

                  def load_x(tj):
                      dma(SP, xts[tj % 2][:], xin[tj * 128:(tj + 1) * 128, :], ["xs%d" % tj] if l > 0 else [], ["xt%d" % (tj % 2)],
                          "xin%d" % (tj % 2))

                  def load_rot(tj):
                      dma(SP, rot[:].rearrange("p a h b -> p (a h b)"), rot_d[tj], [], ["rot"], "rot")

                  if ti == 0:
                      load_x(0)
                      load_rot(0)
                  act(xn[:], X[:], AF.Square, [xk], ["xn", "ss"], accum=ss[:, 0:1])
                  act(ss[:, 1:2], ss[:, 0:1], AF.Ln, ["ss"], ["ss"], scale=1.0 / D, bias=EPS)
                  act(ss[:, 2:3], ss[:, 1:2], AF.Exp, ["ss"], ["ss"], scale=-0.5)
                  ts(DVE, xn[:], X[:], ss[:, 2:3], None, ALU.mult, None, [xk, "ss"], ["xn"])
                  if ti + 1 < n_tiles:
                      load_x(ti + 1)
                  for k in range(8):
                      tr(TB[:, k * 128:(k + 1) * 128], xn[:, k * 128:(k + 1) * 128], ["xn"], ["TB"])
                  for k in range(8):
                      ts(DVE, hT[:, k, :], TB[:, k * 128:(k + 1) * 128], gs[:, k:k + 1], sh[:, k:k + 1], ALU.mult, ALU.add,
                         ["TB", "gs", "sh"], ["hT"])

                  chk("C")
                  def core(groups, heads, S, Sn, scv, scn, vcol, yslot, og_col, norm_div, qz=None, qs=1.0, gb=None, gbn="hg"):
                      nh = len(heads)
                      gb = hg if gb is None else gb
                      for h in range(4):
                          ts(POOL, gb[:, h, :], gb[:, h, :], pfm[:, og_col + h: og_col + h + 1], 1.0, ALU.mult, ALU.mult, [gbn, "pfm"], [gbn])
                      for g in range(groups):
                          tr(TB[:, g * 128:(g + 1) * 128], kt[:, g, :], ["kt"], ["TB"])
                      act(ktT[:, 0:groups, :], TB[:, 0:groups * 128].rearrange("p (a b) -> p a b", b=128), AF.Copy, ["TB"], ["ktT"])
                      for hi, (g, po, dk) in enumerate(heads):
                          if qz is None:
                              mm(P[3][:, hi * 128:(hi + 1) * 128], kt[po:po + dk, g, :], qt[po:po + dk, g, :], True, True,
                                 ["kt", "qt"], ["P3"])
                          else:
                              mm(P[3][:, hi * 128:(hi + 1) * 128], kt[:, g, :], qz[:, hi, :], True, True, ["kt", "hq"], ["P3"])
                      maskt = mask01 if qs == 1.0 else cmat[:, 4, :]
                      if os.environ.get("NEWPT"):
                          stt(pt[:, 0:nh, :], P[3][:, 0:nh * 128].rearrange("p (a b) -> p a b", b=128), 1e25,
                              maskt.unsqueeze(1).to_broadcast([128, nh, 128]), ALU.min, ALU.mult, ["P3", "cmat"], ["pt"])
                      else:
                          ts(DVE, F[3][:, 0:nh, :], P[3][:, 0:nh * 128].rearrange("p (a b) -> p a b", b=128), 1e25, -1e25,
                             ALU.min, ALU.max, ["P3"], ["f3"])
                          tt(POOL, pt[:, 0:nh, :], F[3][:, 0:nh, :],
                             maskt.unsqueeze(1).to_broadcast([128, nh, 128]), ALU.mult, ["f3", "cmat"], ["pt"])
                      dv = 128
                      for g in range(groups):
                          ts(POOL, sp0[:, g * dv:(g + 1) * dv], S[:, g, :], scv[:, g, 0:1], qs, ALU.mult, ALU.mult, [Sn, scn], ["sp0"])
                      for c in range(2):
                          for hi, (g, po, dk) in enumerate(heads):
                              mm(P[5][po:po + dk, g * 128:(g + 1) * 128], ktT[c * 64:(c + 1) * 64, g, po:po + dk],
                                 vv[c * 64:(c + 1) * 64, vcol + hi * 128: vcol + (hi + 1) * 128], True, True, ["ktT", "vv"], ["P5"])
                          for g in range(groups):
                              act(F[3][:, g, :], P[5][:, g * 128:(g + 1) * 128], AF.Copy, ["P5", scn], ["f3_%d" % g],
                                  scale=scv[:, g, 4 + c:5 + c])
                          for g in range(groups):
                              stt(S[:, g, :], S[:, g, :], scv[:, g, 2 + c:3 + c], F[3][:, g, :], ALU.mult, ALU.add,
                                  [Sn, scn, "f3_%d" % g], [Sn])
                          if c == 0:
                              for g in range(groups):
                                  ts(POOL, sp1[:, g * dv:(g + 1) * dv], S[:, g, :], scv[:, g, 1:2], qs, ALU.mult, ALU.mult, [Sn, scn], ["sp1"])
                              for hi, (g, po, dk) in enumerate(heads):
                                  o = P[4][:, hi * 128:(hi + 1) * 128]
                                  mm(o, vv[:, vcol + hi * 128: vcol + (hi + 1) * 128], pt[:, hi, :], True, False, ["vv", "pt"], ["P4"])
                                  if qz is None:
                                      mm(o[:, 0:64], sp0[po:po + dk, g * dv:(g + 1) * dv], qt[po:po + dk, g, 0:64], False, False,
                                         ["sp0", "qt"], ["P4"])
                                      mm(o[:, 64:128], sp1[po:po + dk, g * dv:(g + 1) * dv], qt[po:po + dk, g, 64:128], False, True,
                                         ["sp1", "qt"], ["P4"])
                                  else:
                                      mm(o[:, 0:64], sp0[:, g * dv:(g + 1) * dv], qz[:, hi, 0:64], False, False, ["sp0", "hq"], ["P4"])
                                      mm(o[:, 64:128], sp1[:, g * dv:(g + 1) * dv], qz[:, hi, 64:128], False, True, ["sp1", "hq"], ["P4"])
                      act(xn[:, 0:512], P[4][:], AF.Square, ["P4"], ["xn"])
                      mm(P7[:, 0:512], ones_bf, xn[:, 0:512], True, True, ["xn", "cmat"], ["P7"])
                      act(F[0][:].rearrange("p a b -> p (a b)"), P7[:], AF.Ln, ["P7"], ["f0"], scale=1.0 / norm_div, bias=EPS)
                      act(F[0][:].rearrange("p a b -> p (a b)"), F[0][:].rearrange("p a b -> p (a b)"), AF.Exp, ["f0"], ["f0"], scale=-0.5)
                      tt(DVE, F[1][:].rearrange("p a b -> p (a b)"), P[4][:], F[0][:].rearrange("p a b -> p (a b)"), ALU.mult,
                         ["P4", "f0"], ["f1"])
                      tt(POOL, yT[:, yslot:yslot + 4, :], F[1][:], gb[:], ALU.mult, ["f1", gbn], ["yT"])

                  def vec_decay_prep(ngr, logf_buf, logf_name):
                      for g in range(ngr):
                          r.op(DVE, lambda e, g=g: e.tensor_tensor_scan(out=F[4][:, g, :], data0=ones32[:], data1=logf_buf[:, g, :],
                                                                         initial=0.0, op0=ALU.mult, op1=ALU.add),
                               [logf_name, "ones32"], ["f4"])
                      cum = F[4]
                      ts(DVE, negm[:, 0:ngr, :], cum[:, 0:ngr, 31:128:64], -1.0, -QL, ALU.mult, ALU.add, ["f4"], ["negm"])
                      for g in range(ngr):
                          for c in range(2):
                              act(F[2][:, g, c * 64:(c + 1) * 64], cum[:, g, c * 64:(c + 1) * 64], AF.Exp, ["f4", "negm"], ["f2"],
                                  bias=negm[:, g, c:c + 1])
                      for g in range(ngr):
                          for c in range(2):
                              act(F[3][:, g, c * 64:(c + 1) * 64], cum[:, g, c * 64:(c + 1) * 64], AF.Exp, ["f4"], ["f3_%d" % g],
                                  scale=-1.0, bias=cum[:, g, 31 + 64 * c: 32 + 64 * c])
                      cp(POOL, sc[:, 0:ngr, 0], cum[:, 0:ngr, 31], ["f4"], ["sc"])
                      tt(POOL, sc[:, 0:ngr, 1], cum[:, 0:ngr, 95], cum[:, 0:ngr, 63], ALU.subtract, ["f4"], ["sc"])
                      cp(POOL, sc[:, 0:ngr, 2], cum[:, 0:ngr, 63], ["f4"], ["sc"])
                      tt(POOL, sc[:, 0:ngr, 3], cum[:, 0:ngr, 127], cum[:, 0:ngr, 63], ALU.subtract, ["f4"], ["sc"])
                      tt(POOL, sc[:, 0:ngr, 4], cum[:, 0:ngr, 63], cum[:, 0:ngr, 31], ALU.subtract, ["f4"], ["sc"])
                      tt(POOL, sc[:, 0:ngr, 5], cum[:, 0:ngr, 127], cum[:, 0:ngr, 95], ALU.subtract, ["f4"], ["sc"])
                      act(sc[:, 0:ngr, :], sc[:, 0:ngr, :], AF.Exp, ["sc"], ["sc"])

                  flat = lambda t: t[:].rearrange("p a b -> p (a b)")

                  r.op(POOL, lambda e: e.memset(yT[:], 0.0), [], ["yT"]) if BR != "hrgs" else None
                  def br_h(before_core=None):
                      proj_fm(P[0], "P0", C_AQ, 4)
                      act(flat(hq), P[0][:], AF.Silu, ["P0"], ["hq"])
                      proj_fm(P[2], "P2", C_AG, 4)
                      act(flat(hg), P[2][:], AF.Silu, ["P2"], ["hg"])
                      if GROUPG:
                          proj_fm(P[2], "P2", C_RG, 4)
                          act(flat(hgB), P[2][:], AF.Silu, ["P2"], ["hgB"])
                      proj_fm(P[1], "P1", C_AF, 4)
                      act(flat(F[0]), P[1][:], AF.Exp, ["P1"], ["f0"], scale=-1.0)
                      act(flat(F[0]), flat(F[0]), AF.Ln, ["f0"], ["f0"], bias=1.0)
                      act(flat(F[1]), flat(F[0]), AF.Exp, ["f0"], ["f1"], scale=-1.0)
                      for h in range(4):
                          act(F[0][:, h, :], F[1][:, h, :], AF.Ln, ["f1_%d" % h, "lbt"], ["f0_%d" % h], scale=lbt[:, h, 1:2], bias=lbt[:, h, 0:1])
                          ts(POOL, F[1][:, h, :], F[1][:, h, :], lbt[:, h, 2:3], lbt[:, h, 1:2], ALU.mult, ALU.add, ["f1_%d" % h, "lbt"], ["f1_%d" % h])
                      vec_decay_prep(4, F[0], "f0")
                      tt(POOL, qt[:], hq[:], F[2][:], ALU.mult, ["hq", "f2"], ["qt"])
                      tt(POOL, kt[:], F[1][:], F[3][:], ALU.mult, ["f1", "f3"], ["kt"])
                      proj_tm(P[0], "P0", C_AI, 512)
                      act(vv[:], P[0][:], AF.Copy, ["P0"], ["vv"])
                      if before_core is not None:
                          before_core()
                      core(4, [(h, 0, 128) for h in range(4)], S_hg, "S_hg", sc, "sc", 0, 0, 32, 128.0, qs=QS)

                  def pre_r():
                      proj_fm(P[1], "P1", C_RQ, 4)
                      proj_fm(P[2], "P2", C_RG, 4)
                      proj_tm(P[0], "P0", C_RV, 512)

                  def pre_g():
                      proj_fm(P[0], "P0", C_GQ, 2)
                      proj_fm(P[2], "P2", C_GG, 4)

                  def pre_s():
                      proj_fm(P[0], "P0", C_MX, 4)
                      proj_fm(P[1], "P1", C_MX + 512, 4)
                      proj_fm(P[2], "P2", C_MZ, 4)

                  PIPE = (BR == "hrgs") and bool(os.environ.get("PIPE"))
                  if "h" in BR and not PIPE:
                      br_h()
                  def br_r(before_core=None, pre=False):
                      for (ccol, tab, dst, dn) in ((C_RQ, 0, qt, "qt"), (C_RK, 4, kt, "kt")):
                          if not (pre and ccol == C_RQ):
                              proj_fm(P[1], "P1", ccol, 4)
                          act(flat(hq), P[1][:], AF.Copy, ["P1"], ["hq"])
                          chk("Ra")
                          mm(P7[:, 0:512], perm, flat(hq), True, True, ["hq", "cmat"], ["P7"])
                          chk("Rb")
                          p1v = P[1][:].rearrange("p (a b) -> p a b", b=128)
                          p7v2 = P7[:].rearrange("p (a b) -> p a b", b=128)
                          RT = os.environ.get("RTEST", "")
                          if RT == "1":
                              tt(DVE, F[0][:], p1v, F[2][:], ALU.mult, ["P1", "f2"], ["f0"])
                          elif RT == "2":
                              cp(DVE, F[0][:], p1v, ["P1"], ["f0"])
                          elif RT == "3":
                              tt(DVE, F[0][:], F[2][:], rot[:, 0, :, :], ALU.mult, ["f2", "rot"], ["f0"])
                          else:
                              tt(DVE, F[0][:], p1v, rot[:, 0, :, :], ALU.mult, ["P1", "rot"], ["f0"])
                          chk("Rc1")
                          tt(DVE, F[1][:], p7v2, rot[:, 1, :, :], ALU.mult, ["P7", "rot"], ["f1"])
                          chk("Rc")
                          tt(POOL, F[0][:], F[0][:], F[1][:], ALU.add, ["f0", "f1"], ["f0"])
                          chk("Rd")
                          tt(POOL, dst[:], F[0][:], gqk[:, tab:tab + 4, :], ALU.mult, ["f0", "gqk"], [dn])
                      if not GROUPG:
                          if not pre:
                              proj_fm(P[2], "P2", C_RG, 4)
                          act(flat(hg), P[2][:], AF.Silu, ["P2"], ["hg"])
                      if not pre:
                          proj_tm(P[0], "P0", C_RV, 512)
                      act(vv[:], P[0][:], AF.Copy, ["P0"], ["vv"])
                      chk("R1")
                      if before_core is not None:
                          before_core()
                      if GROUPG:
                          core(4, [(h, 0, 128) for h in range(4)], S_rt, "S_rt", retsc, "retsc", 0, 4, 36, 128.0, gb=hgB, gbn="hgB")
                      else:
                          core(4, [(h, 0, 128) for h in range(4)], S_rt, "S_rt", retsc, "retsc", 0, 4, 36, 128.0)

                  if "r" in BR and not PIPE:
                      br_r()
                  if not PIPE and ti + 1 < n_tiles:
                      load_rot(ti + 1)
                  def br_g(before_core=None, pre=False):
                      proj_fm(P7, "P7", C_LR, 1, width=16)
                      cp(DVE, lrT[0:16, :], P7[0:16, 0:128], ["P7"], ["lrT"])
                      for g in range(2):
                          mm(P[1][:, g * 128:(g + 1) * 128], wgk[0:17, g * 128:(g + 1) * 128], lrT[0:17, :], True, True, ["wgk", "lrT"], ["P1"])
                      act(F[0][:, 0:2, :], P[1][:, 0:256].rearrange("p (a b) -> p a b", b=128), AF.Exp, ["P1"], ["f0"], scale=-1.0)
                      act(F[0][:, 0:2, :], F[0][:, 0:2, :], AF.Ln, ["f0"], ["f0"], bias=1.0)
                      ts(DVE, F[0][:, 0:2, :], F[0][:, 0:2, :], -1.0 / 16.0, None, ALU.mult, None, ["f0"], ["f0"])
                      vec_decay_prep(2, F[0], "f0")
                      if not pre:
                          proj_fm(P[0], "P0", C_GQ, 2)
                      stt(qt[:, 0:2, :], P[0][:, 0:256].rearrange("p (a b) -> p a b", b=128), 0.125, F[2][:, 0:2, :], ALU.mult, ALU.mult,
                          ["P0", "f2"], ["qt"])
                      proj_fm(P[1], "P1", C_GK, 2)
                      tt(DVE, kt[:, 0:2, :], P[1][:, 0:256].rearrange("p (a b) -> p a b", b=128), F[3][:, 0:2, :], ALU.mult, ["P1", "f3"], ["kt"])
                      if not pre:
                          proj_fm(P[2], "P2", C_GG, 4)
                      act(flat(hg), P[2][:], AF.Silu, ["P2"], ["hg"])
                      proj_tm(P[0], "P0", C_GV, 512)
                      act(vv[:], P[0][:], AF.Copy, ["P0"], ["vv"])
                      r.op(POOL, lambda e: e.memset(hq[:], 0.0), [], ["hq"])
                      for hi, (g, po) in enumerate(((0, 0), (0, 64), (1, 0), (1, 64))):
                          cp(POOL, hq[po:po + 64, hi, :], qt[po:po + 64, g, :], ["qt"], ["hq"])
                      if before_core is not None:
                          before_core()
                      core(2, [(0, 0, 64), (0, 64, 64), (1, 0, 64), (1, 64, 64)], S_gl, "S_gl", sc, "sc", 0, 12, 44, 128.0, qz=hq, qs=QS)

                  if "g" in BR and not PIPE:
                      br_g()
                  def br_s(pre=False):
                      for half in range(2):
                          if not pre:
                              proj_fm(P[half], "P%d" % half, C_MX + half * 512, 4)
                      cp(DVE, ubuf[:, 0:4, 3:131], P[0][:].rearrange("p (a b) -> p a b", b=128), ["P0"], ["ubuf"])
                      act(ubuf[:, 4:8, 3:131], P[1][:].rearrange("p (a b) -> p a b", b=128), AF.Copy, ["P1"], ["ubuf"])
                      cacc = [F[0], F[1]]
                      for g in range(8):
                          ca = cacc[g // 4][:, g % 4, :]
                          cn = "f%d_%d" % (g // 4, g % 4)
                          ts(POOL, ca, ubuf[:, g, 0:128], pfm[:, 56 + g * 4: 57 + g * 4], pfm[:, 48 + g: 49 + g], ALU.mult, ALU.add,
                             ["ubuf", "pfm"], [cn])
                          for j in range(1, 4):
                              stt(ca, ubuf[:, g, j:j + 128], pfm[:, 56 + g * 4 + j: 57 + g * 4 + j], ca, ALU.mult, ALU.add,
                                  ["ubuf", "pfm", cn], [cn])
                      cp(POOL, ubuf[:, :, 0:3], ubuf[:, :, 128:131], ["f0", "f1", "ubuf"], ["ubuf"])
                      act(xc[:, 0:4, :], F[0][:], AF.Silu, ["f0"], ["xc"])
                      act(xc[:, 4:8, :], F[1][:], AF.Silu, ["f1"], ["xc"])
                      if not pre:
                          proj_fm(P[2], "P2", C_MZ, 4)
                      act(flat(hg), P[2][:], AF.Silu, ["P2"], ["hg"])
                      proj_fm(P7, "P7", C_DT, 1, width=8)
                      act(d8[:, 0, :], P7[0:8, 0:128], AF.Exp, ["P7", "p8"], ["d8"], bias=p8[:, 0:1])
                      act(d8[:, 1, :], d8[:, 0, :], AF.Ln, ["d8"], ["d8"], bias=1.0)
                      ts(DVE, d8[:, 0, :], d8[:, 1, :], a8[:, 1:2], None, ALU.mult, None, ["d8", "a8"], ["d8"])
                      for c in range(2):
                          r.op(DVE, lambda e, c=c: e.tensor_tensor_scan(out=d8[:, 2, c * 64:(c + 1) * 64], data0=ones32[0:8, 0:64],
                                                                         data1=d8[:, 0, c * 64:(c + 1) * 64], initial=0.0,
                                                                         op0=ALU.mult, op1=ALU.add), ["d8", "ones32"], ["d8"])
                      for c in range(2):
                          act(d8[:, 3, c * 64:(c + 1) * 64], d8[:, 2, c * 64:(c + 1) * 64], AF.Exp, ["d8"], ["d8"], scale=-1.0,
                              bias=d8[:, 2, c * 64 + 63: c * 64 + 64])
                      tt(DVE, d8[:, 3, :], d8[:, 3, :], d8[:, 1, :], ALU.mult, ["d8"], ["d8"])
                      for i, row in enumerate((1, 3, 2)):
                          r.op(PE, lambda e, i=i, row=row: e.matmul(P7[:, 256 + i * 8: 256 + (i + 1) * 8], lhsT=d8[:, row, :],
                                                                    rhs=sel[:, :, 0], start=True, stop=True), ["d8", "sel"], ["P7"])
                      cp(DVE, ssdT[:].rearrange("p a b -> p (a b)"), P7[:, 256:280], ["P7"], ["ssdT"])
                      for g in range(4):
                          tr(TB[:, g * 128:(g + 1) * 128], xc[:, g, :], ["xc"], ["TB"])
                      tbv = TB[:, 0:512].rearrange("p (a b) -> p a b", b=64)
                      tt(DVE, vv[:].rearrange("p (a b) -> p a b", b=64), tbv, ssdT[:, 0, :].unsqueeze(2).to_broadcast([128, 8, 64]), ALU.mult,
                         ["TB", "ssdT"], ["vv"])
                      tt(DVE, flat(hq).rearrange("p (a b) -> p a b", b=64), tbv, ssdT[:, 1, :].unsqueeze(2).to_broadcast([128, 8, 64]), ALU.mult,
                         ["TB", "ssdT"], ["hq"])
                      v2 = flat(hq)
                      for gi in range(2):
                          tr(TB[:, 512 + gi * 128: 512 + (gi + 1) * 128], xc[:, 4 + gi, :], ["xc"], ["TB"])
                      act(ktT[:, 0:2, :], TB[:, 512:768].rearrange("p (a b) -> p a b", b=128), AF.Copy, ["TB"], ["ktT"])
                      cp(POOL, sp0[:], S_sd[:].rearrange("p a b -> p (a b)"), ["S_sd"], ["sp0"])
                      for gi in range(2):
                          for hh in range(4):
                              h = gi * 4 + hh
                              mm(P7[:, hh * 128:(hh + 1) * 128], sel[:, h, :], d8[:, 2, :], True, True, ["sel", "d8"], ["P7"])
                          for hh in range(4):
                              h = gi * 4 + hh
                              ts(DVE, F[2][:, hh, :], P7[:, hh * 128:(hh + 1) * 128], ssdT[:, 2, h:h + 1], 0.0, ALU.subtract, ALU.min,
                                 ["P7", "ssdT"], ["f2"])
                          act(flat(kt), flat(F[2]), AF.Exp, ["f2"], ["kt"])
                          act(flat(F[3]), P7[:], AF.Exp, ["P7"], ["f3"])
                          mm(P[3][:, 0:128], xc[:, 4 + gi, :], xc[:, 6 + gi, :], True, True, ["xc"], ["P3"])
                          tt(DVE, gm[:], P[3][:, 0:128], mask01, ALU.mult, ["P3", "cmat"], ["gm"])
                          tt(POOL, pt[:], kt[:], gm[:].unsqueeze(1).to_broadcast([128, 4, 128]), ALU.mult, ["kt", "gm"], ["pt"])
                          tt(POOL, qt[:], F[3][:], xc[:, 6 + gi, :].unsqueeze(1).to_broadcast([128, 4, 128]), ALU.mult, ["f3", "xc"], ["qt"])
                          for c in range(2):
                              mm(P[5][:, 0:256], ktT[c * 64:(c + 1) * 64, gi, :], v2[c * 64:(c + 1) * 64, gi * 256:(gi + 1) * 256], True, True,
                                 ["ktT", "hq"], ["P5"])
                              for hh in range(4):
                                  h = gi * 4 + hh
                                  stt(S_sd[:, h, :], S_sd[:, h, :], F[3][:, hh, c * 64 + 63: c * 64 + 64], P[5][:, hh * 64:(hh + 1) * 64],
                                      ALU.mult, ALU.add, ["S_sd", "f3", "P5"], ["S_sd"])
                              if c == 0:
                                  cp(POOL, sp1[:, gi * 256:(gi + 1) * 256], S_sd[:, gi * 4:(gi + 1) * 4, :].rearrange("p a b -> p (a b)"),
                                     ["S_sd"], ["sp1"])
                                  for hh in range(4):
                                      h = gi * 4 + hh
                                      o = P[4][(h % 2) * 64:(h % 2) * 64 + 64, (h // 2) * 128:(h // 2 + 1) * 128]
                                      mm(o, vv[:, h * 64:(h + 1) * 64], pt[:, hh, :], True, False, ["vv", "pt"], ["P4"])
                                      mm(o[:, 0:64], sp0[:, h * 64:(h + 1) * 64], qt[:, hh, 0:64], False, False, ["sp0", "qt"], ["P4"])
                                      mm(o[:, 64:128], sp1[:, h * 64:(h + 1) * 64], qt[:, hh, 64:128], False, True, ["sp1", "qt"], ["P4"])
                      for g in range(4):
                          stt(F[0][:, g, :], xc[:, g, :], pfm[:, 88 + g: 89 + g], P[4][:, g * 128:(g + 1) * 128], ALU.mult, ALU.add,
                              ["xc", "pfm", "P4"], ["f0"])
                      tt(POOL, F[0][:], F[0][:], hg[:], ALU.mult, ["f0", "hg"], ["f0"])
                      act(xn[:, 0:512], flat(F[0]), AF.Square, ["f0"], ["xn"])
                      for gi in range(2):
                          for j in range(2):
                              mm(P7[:, gi * 128:(gi + 1) * 128], ones_bf, xn[:, (2 * gi + j) * 128:(2 * gi + j + 1) * 128], j == 0, j == 1,
                                 ["xn", "cmat"], ["P7"])
                      act(F[1][:, 0:2, :], P7[:, 0:256].rearrange("p (a b) -> p a b", b=128), AF.Ln, ["P7"], ["f1"], scale=1.0 / 256.0, bias=EPS)
                      act(F[1][:, 0:2, :], F[1][:, 0:2, :], AF.Exp, ["f1"], ["f1"], scale=-0.5)
                      for g in range(4):
                          stt(yT[:, 8 + g, :], F[0][:, g, :], pfm[:, 40 + g: 41 + g], F[1][:, g // 2, :], ALU.mult, ALU.mult,
                              ["f0", "pfm", "f1"], ["yT"])

                  if "s" in BR and not PIPE:
                      br_s()
                  if PIPE:
                      br_h(before_core=pre_r)
                      br_r(before_core=pre_g, pre=True)
                      if ti + 1 < n_tiles:
                          load_rot(ti + 1)
                      br_g(before_core=pre_s, pre=True)
                      br_s(pre=True)
                  if dbg:
                      DUMP = os.environ.get("DUMP", "")
                      if DUMP:
                          bufs = dict(qt=qt, kt=kt, hq=hq, hg=hg, pt=pt, ktT=ktT)
                          for hf, nm in enumerate(DUMP.split(",")):
                              if nm.startswith("f"):
                                  j = int(nm[1:])
                                  dma(SP, dbg_d[l, ti, :, hf * 512:(hf + 1) * 512], Ff[j], [nm], ["dbgout"], "dbg")
                              elif nm in ("vv", "sp0", "sp1"):
                                  src = dict(vv=vv, sp0=sp0, sp1=sp1)[nm]
                                  cp(POOL, Ff[hf], src[:], [nm], ["f%d" % hf])
                                  dma(SP, dbg_d[l, ti, :, hf * 512:(hf + 1) * 512], Ff[hf], ["f%d" % hf], ["dbgout"], "dbg")
                              elif nm.startswith("S"):
                                  src = dict(S_hg=S_hg, S_rt=S_rt)[nm]
                                  dma(SP, dbg_d[l, ti, :, hf * 512:(hf + 1) * 512], src[:].rearrange("p a b -> p (a b)"), [nm], ["dbgout"], "dbg")
                              else:
                                  cp(POOL, F[hf][:], bufs[nm][:], [nm], ["f%d" % hf])
                                  dma(SP, dbg_d[l, ti, :, hf * 512:(hf + 1) * 512], Ff[hf], ["f%d" % hf], ["dbgout"], "dbg")
                      else:
                          for hf in range(4):
                              cp(POOL, F[2][:], yT[:, hf * 4:(hf + 1) * 4, :], ["yT"], ["f2"])
                              dma(SP, dbg_d[l, ti, :, hf * 512:(hf + 1) * 512], flat(F[2]), ["f2"], ["dbgout"], "dbg")

                  if last:
                      for half in range(2):
                          dma(SP, flat(F[2 + half]), finalg_d[:, half * 512:(half + 1) * 512], [], ["f%d" % (2 + half)], "fg%d" % half)
                  for half in range(2):
                      for kc in range(16):
                          mm(P[half][:], yT[:, kc, :], w_out[:, kc, half * 512:(half + 1) * 512], kc == 0, kc == 15,
                             ["yT", "w_out%d" % kc], ["P%d" % half])
                      tt(DVE, X[:, half * 512:(half + 1) * 512], P[half][:], X[:, half * 512:(half + 1) * 512], ALU.add,
                         ["P%d" % half, xk], [xk])
                  if last:
                      act(xn[:], X[:], AF.Square, [xk], ["xn", "ss"], accum=ss[:, 0:1])
                      act(ss[:, 1:2], ss[:, 0:1], AF.Ln, ["ss"], ["ss"], scale=1.0 / D, bias=EPS)
                      act(ss[:, 2:3], ss[:, 1:2], AF.Exp, ["ss"], ["ss"], scale=-0.5)
                      for half in range(2):
                          stt(X[:, half * 512:(half + 1) * 512], X[:, half * 512:(half + 1) * 512], ss[:, 2:3], flat(F[2 + half]),
                              ALU.mult, ALU.mult, [xk, "ss", "f%d" % (2 + half)], [xk])
                      dma(SP, y_d[t0:t0 + 128, :], X[:], [xk], ["yout"], "xout%d" % (ti % 2))
                  else:
                      dma(SP, xs_d[t0:t0 + 128, :], X[:], [xk], ["xs%d" % ti], "xout%d" % (ti % 2))
        except _Stop:
            dma(SP, y_d[0:128, :], xt[:], ["xt0"], ["yout"], "xout0")
        r.op(SP, lambda e: None, ["yout"] + (["dbgout"] if dbg else []), [])
        stats = r.emit()
    return nc, stats


def _const_tables(n_tiles):
    L = n_tiles * 128
    ident = np.eye(128, dtype=np.float32)
    s = np.arange(128)[:, None]
    t = np.arange(128)[None, :]
    mask01 = ((s // 64 == t // 64) & (s <= t)).astype(np.float32)
    perm = np.zeros((128, 128), np.float32)
    perm[(np.arange(128) + 64) % 128, np.arange(128)] = 1.0
    ones = np.ones((128, 128), np.float32)
    cmat = np.concatenate([ident, mask01, perm, ones, mask01 * QS], axis=1)
    sel = np.zeros((8, 8, 128), np.float32)
    for h in range(8):
        sel[h, h, :] = 1.0
    gam = 1.0 - np.exp2(-(5.0 + np.arange(4, dtype=np.float64)))
    tp = (np.arange(128) % 64 + 1).astype(np.float64)
    gq = np.stack([gam[h] ** tp for h in range(4)], 0)
    gk = np.stack([gam[h] ** (-tp) * (128.0 ** -0.5) for h in range(4)], 0)
    gqk = np.concatenate([np.broadcast_to(gq[None], (128, 4, 128)), np.broadcast_to(gk[None], (128, 4, 128))], axis=1)
    gqk = np.ascontiguousarray(gqk, dtype=np.float32).reshape(128, 1024)
    retsc = np.zeros((128, 4, 6), np.float32)
    for h in range(4):
        g64 = gam[h] ** 64
        retsc[:, h, :] = [1.0, 1.0, g64, g64, g64, g64]
    retsc = retsc.reshape(128, 24)
    inv_freq = (10000.0 ** (-np.arange(0, 128, 2, dtype=np.float32) / 128)).astype(np.float32)
    ang = np.arange(L, dtype=np.float32)[:, None] * inv_freq[None, :]
    cos = np.cos(ang).astype(np.float32).T
    sin = np.sin(ang).astype(np.float32).T
    cosf = np.concatenate([cos, cos], 0)
    sinf = np.concatenate([-sin, sin], 0)
    rot = np.stack([cosf.reshape(128, n_tiles, 128), sinf.reshape(128, n_tiles, 128)], axis=2)
    rot = np.broadcast_to(rot.transpose(1, 0, 2, 3)[:, :, :, None, :], (n_tiles, 128, 2, 4, 128))
    rot = np.ascontiguousarray(rot).reshape(n_tiles, 128, 1024)
    return dict(cmat=cmat, sel=sel.reshape(8, 1024), gqk=gqk, retsc=retsc, rot=rot)


def _fm(v, n):
    return np.ascontiguousarray(np.asarray(v, np.float32).reshape(n, 128).T)


def _pack_params(inp, n_layers):
    pfm = np.zeros((n_layers, 128, NF), np.float32)
    p8 = np.zeros((n_layers, 8, 2), np.float32)
    bgate = np.zeros((n_layers, 128, D), np.float32)
    wgk = np.zeros((n_layers, 17, 256), np.float32)
    for l in range(n_layers):
        pfm[l, :, 0:8] = _fm(inp["norm_g"][l], 8)
        pfm[l, :, 8:16] = _fm(inp["b_ada"][l][0:D], 8)
        pfm[l, :, 16:24] = _fm(inp["b_ada"][l][D:2 * D], 8)
        pfm[l, :, 24:28] = _fm(inp["hgrn_lb_logits"][0], 4)
        pfm[l, :, 28:32] = _fm(inp["hgrn_lb_logits"][min(1, inp["hgrn_lb_logits"].shape[0] - 1)], 4)
        pfm[l, :, 32:36] = _fm(inp["hgrn_onorm_g"][l], 4)
        pfm[l, :, 36:40] = _fm(inp["ret_onorm_g"][l], 4)
        pfm[l, :, 40:44] = _fm(inp["ssm_norm_g"][l], 4)
        pfm[l, :, 44:48] = _fm(inp["gla_onorm_g"][l], 4)
        pfm[l, :, 48:56] = _fm(inp["ssm_conv_b"][l], 8)
        cw = np.asarray(inp["ssm_conv_w"][l], np.float32)
        pfm[l, :, 56:88] = cw.reshape(4, 8, 128).transpose(2, 1, 0).reshape(128, 32)
        pfm[l, :, 88:92] = _fm(np.repeat(np.asarray(inp["ssm_d"][l], np.float32), 64), 4)
        p8[l, :, 0] = inp["ssm_dt_bias"][l]
        p8[l, :, 1] = inp["ssm_a_log"][l]
        bgate[l] = np.broadcast_to(np.asarray(inp["b_ada"][l][2 * D:3 * D], np.float32)[None, :], (128, D))
        wgk[l, 0:16] = inp["gla_w_gk2"][l]
        wgk[l, 16] = inp["gla_b_gk2"][l]
    return pfm, p8, bgate, wgk


_CACHE = {}


def run(inputs, n_tiles, n_layers, core_batches, dbg=False):
    inp = {k: np.asarray(v) for k, v in inputs.items()}
    key = (n_tiles, n_layers, dbg)
    if key not in _CACHE:
        _CACHE[key] = build_program(n_tiles, n_layers, dbg)
    nc, stats = _CACHE[key]
    L = n_tiles * 128
    consts = _const_tables(n_tiles)
    pfm, p8, bgate, wgk = _pack_params(inp, n_layers)
    finalg = np.ascontiguousarray(np.broadcast_to(np.asarray(inp["final_g"], np.float32)[None, :], (128, D)))
    shared = dict(w_in=np.ascontiguousarray(inp["w_in"][:n_layers], dtype=np.float32),
                  w_out=np.ascontiguousarray(inp["w_out"][:n_layers], dtype=np.float32),
                  w_ada=np.ascontiguousarray(inp["w_ada"][:n_layers], dtype=np.float32),
                  pfm=pfm, p8=p8, bgate=bgate, wgk=wgk, finalg=finalg, **consts)
    in_maps = []
    for b in core_batches:
        m = dict(shared)
        m["x"] = np.ascontiguousarray(inp["x"][b, :L], dtype=np.float32)
        m["cfm"] = _fm(inp["c"][b], 8)
        in_maps.append(m)
    res = run_bass_kernel_spmd(nc, in_maps, core_ids=list(range(len(core_batches))))
    return res, stats


def kernel(**inputs):
    B, Lfull, _ = inputs["x"].shape
    n_tiles = Lfull // 128
    core_batches = [i % B for i in range(8)]
    res, _ = run(inputs, n_tiles, 2, core_batches)
    out = np.stack([np.asarray(res.results[b]["y"], dtype=np.float32) for b in range(B)], axis=0)
    return out
```

```python
import math
import os
BR = os.environ.get("BR", "hrgs")
STAGE = os.environ.get("STAGE", "")
SKIP = os.environ.get("SKIP", "")
GROUPG = (BR == "hrgs") and not os.environ.get("NOGROUPG")


class _Stop(Exception):
    pass


def chk(n):
    if STAGE == n:
        raise _Stop()
import numpy as np
from contextlib import ExitStack
import concourse.bass as bass
import concourse.mybir as mybir
from concourse.bass_utils import run_bass_kernel_spmd

F32 = mybir.dt.float32
BF16 = mybir.dt.bfloat16
AF = mybir.ActivationFunctionType
ALU = mybir.AluOpType
PE, ACT, DVE, POOL, SP = "pe", "act", "dve", "pool", "sp"
COMPUTE = (PE, ACT, DVE, POOL)

D = 1024
QS = float(2 ** 30)
QL = 30.0 * math.log(2.0)
NW = 7192
EPS = 1e-6
C_AQ, C_AF, C_AI, C_AG = 0, 512, 1024, 1536
C_RQ, C_RK, C_RV, C_RG = 2048, 2560, 3072, 3584
C_MZ, C_MX, C_DT = 4096, 4608, 5632
C_GQ, C_GK, C_GV, C_GG, C_LR = 5640, 5896, 6152, 6664, 7176
NF = 92


EXPAND = {"f%d" % i: tuple("f%d_%d" % (i, g) for g in range(4)) for i in range(5)}


class Rec:
    def __init__(self, nc, stack):
        self.nc = nc
        self.stack = stack
        self.ops = []
        self.writers = {}
        self.readers = {}
        self.dma_keys = {}

    def sb(self, name, shape, dt):
        return self.stack.enter_context(self.nc.sbuf_tensor("sb_" + name, list(shape), dt))

    def ps(self, name, shape, dt):
        return self.stack.enter_context(self.nc.psum_tensor("ps_" + name, list(shape), dt))

    def op(self, eng, fn, reads=(), writes=(), dma_key=None, dma_mode="serial"):
        idx = len(self.ops)
        deps = set()
        reads = [k for b in reads for k in EXPAND.get(b, (b,))]
        writes = [k for b in writes for k in EXPAND.get(b, (b,))]
        excl = [b for b in reads if b[0] == "P" or b == "TB"]
        if excl:
            reads = [b for b in reads if b not in excl]
            writes = list(writes) + [b for b in excl if b not in writes]
        for b in reads:
            deps.update(self.writers.get(b, {}).values())
        for b in writes:
            deps.update(self.writers.get(b, {}).values())
            deps.update(self.readers.get(b, ()))
        wk = eng if dma_key is None else ("dma", dma_key)
        for b in writes:
            self.writers.setdefault(b, {})[wk] = idx
            self.readers[b] = []
        for b in reads:
            self.readers.setdefault(b, []).append(idx)
        deps.discard(idx)
        seq = None
        if dma_key is not None:
            k = self.dma_keys.setdefault(dma_key, dict(mode=dma_mode, count=0))
            k["count"] += 1
            seq = k["count"]
        self.ops.append(dict(eng=eng, fn=fn, deps=deps, dma_key=dma_key, seq=seq, consumers=0, sig=None))
        return idx

    def emit(self):
        nc, ops = self.nc, self.ops

        def skip(p, o):
            return p["dma_key"] is None and o["dma_key"] is None and p["eng"] == PE and o["eng"] == PE

        for o in ops:
            for d in o["deps"]:
                if not skip(ops[d], o):
                    ops[d]["consumers"] += 1
        cnt = {e: 0 for e in COMPUTE}
        for o in ops:
            if o["dma_key"] is None and o["consumers"] > 0:
                cnt[o["eng"]] += 1
                o["sig"] = cnt[o["eng"]]
        sem = {}
        for e in COMPUTE:
            sem[e] = self.stack.enter_context(nc.semaphore("s_" + e))
        for k in self.dma_keys:
            sem[("dma", k)] = self.stack.enter_context(nc.semaphore("d_" + str(k)))
        known = {}
        for o in ops:
            me = o["eng"]
            need = {}
            for d in o["deps"]:
                p = ops[d]
                if p["dma_key"] is not None:
                    kk = ("dma", p["dma_key"])
                    info = self.dma_keys[p["dma_key"]]
                    val = 16 * (p["seq"] if info["mode"] == "serial" else info["count"])
                else:
                    if skip(p, o):
                        continue
                    kk, val = p["eng"], p["sig"]
                need[kk] = max(need.get(kk, 0), val)
            kn = known.setdefault(me, {})
            waits = []
            for kk, val in need.items():
                if kn.get(kk, 0) < val:
                    kn[kk] = val
                    waits.append((sem[kk], val))
            o["waits"] = waits
        by_eng = {e: [] for e in (PE, ACT, DVE, POOL, SP)}
        for o in ops:
            by_eng[o["eng"]].append(o)

        def run(eng_obj, lst):
            for o in lst:
                for (s, v) in o["waits"]:
                    eng_obj.wait_ge(s, v)
                ins = o["fn"](eng_obj)
                if ins is None:
                    continue
                if o["dma_key"] is not None:
                    ins.then_inc(sem[("dma", o["dma_key"])], 16)
                elif o["sig"] is not None:
                    ins.then_inc(sem[o["eng"]], 1)

        final_waits = [(sem[("dma", k)], 16 * v["count"]) for k, v in self.dma_keys.items()]

        with nc.Block() as block:
            @block.sync
            def _(e):
                run(e, by_eng[SP])
                for (s_, v_) in final_waits:
                    e.wait_ge(s_, v_)

            @block.tensor
            def _(e):
                run(e, by_eng[PE])

            @block.scalar
            def _(e):
                run(e, by_eng[ACT])

            @block.vector
            def _(e):
                run(e, by_eng[DVE])

            @block.gpsimd
            def _(e):
                run(e, by_eng[POOL])
                for (s_, v_) in final_waits:
                    e.wait_ge(s_, v_)
        return {e: len(by_eng[e]) for e in by_eng}


def build_program(n_tiles, n_layers, dbg=False):
    L = n_tiles * 128
    nc = bass.Bass("TRN2", target_bir_lowering=False)

    def din(name, shape):
        return nc.dram_tensor(name, list(shape), F32, kind="ExternalInput").ap()

    x_d = din("x", [L, D])
    cfm_d = din("cfm", [128, 8])
    w_in_d = din("w_in", [n_layers, D, NW])
    w_out_d = din("w_out", [n_layers, 2048, D])
    w_ada_d = din("w_ada", [n_layers, D, 3 * D])
    pfm_d = din("pfm", [n_layers, 128, NF])
    p8_d = din("p8", [n_layers, 8, 2])
    bgate_d = din("bgate", [n_layers, 128, D])
    wgk_d = din("wgk", [n_layers, 17, 256])
    finalg_d = din("finalg", [128, D])
    cmat_d = din("cmat", [128, 5 * 128])
    sel_d = din("sel", [8, 8 * 128])
    gqk_d = din("gqk", [128, 2 * 4 * 128])
    retsc_d = din("retsc", [128, 24])
    rot_d = din("rot", [n_tiles, 128, 1024])
    y_d = nc.dram_tensor("y", [L, D], F32, kind="ExternalOutput").ap()
    xs_d = nc.dram_tensor("xs", [L, D], F32, kind="Internal").ap()
    if dbg:
        dbg_d = nc.dram_tensor("dbg", [n_layers, n_tiles, 128, 16 * 128], F32, kind="ExternalOutput").ap()

    with ExitStack() as st:
        r = Rec(nc, st)
        sb, ps = r.sb, r.ps
        w_in = sb("w_in_sb", [128, 8, NW], BF16)
        w_out = sb("w_out_sb", [128, 8 if os.environ.get("SHRINK") else 16, D], BF16)
        xt = sb("xt", [128, D], F32)
        xn = sb("xn", [128, D], BF16)
        hT = sb("hT", [128, 8, 128], BF16)
        yT = sb("yT", [128, 16, 128], BF16)
        hq = sb("hq", [128, 4, 128], BF16)
        hg = sb("hg", [128, 4, 128], BF16)
        hgB = sb("hgB", [128, 4, 128], BF16)
        vv = sb("vv", [128, 512], BF16)
        qt = sb("qt", [128, 4, 128], BF16)
        kt = sb("kt", [128, 4, 128], BF16)
        ktT = sb("ktT", [128, 4, 128], BF16)
        pt = sb("pt", [128, 4, 128], BF16)
        gm = sb("gm", [128, 128], BF16)
        sp0 = sb("sp0", [128, 512], BF16)
        sp1 = sb("sp1", [128, 512], BF16)
        F = [sb("f%d" % i, [128, 4, 128], F32) for i in range(5)]
        ubuf = sb("ubuf", [128, 8, 131], BF16)
        xc = sb("xc", [128, 8, 128], BF16)
        S_hg = sb("S_hg", [128, 4, 128], F32)
        S_rt = sb("S_rt", [128, 4, 128], F32)
        S_sd = sb("S_sd", [128, 8, 64], F32)
        S_gl = sb("S_gl", [128, 2, 128], F32)
        cmat = sb("cmat", [128, 5, 128], BF16)
        ones32 = sb("ones32", [128, 128], F32)
        sel = sb("sel", [8, 8, 128], F32)
        gqk = sb("gqk", [128, 8, 128], BF16)
        retsc = sb("retsc", [128, 4, 6], F32)
        rot = sb("rot", [128, 2, 4, 128], F32)
        xt2 = sb("xt2", [128, D], F32)
        xts = [xt, xt2]
        pfm = sb("pfm", [128, NF], F32)
        p8 = sb("p8", [8, 2], F32)
        cfm = sb("cfm", [128, 8], F32)
        cact = sb("cact", [128, 8, 2], F32)
        gs = sb("gs", [128, 8], F32)
        sh = sb("sh", [128, 8], F32)
        lbt = sb("lbt", [128, 4, 3], F32)
        sm = sb("sm", [128, 16], F32)
        sc = sb("sc", [128, 4, 6], F32)
        negm = sb("negm", [128, 4, 2], F32)
        wgk = sb("wgk", [32, 256], BF16)
        lrT = sb("lrT", [32, 128], BF16)
        d8 = sb("d8", [8, 4, 128], F32)
        a8 = sb("a8", [8, 2], F32)
        ssdT = sb("ssdT", [128, 3, 8], F32)
        ss = sb("ss", [128, 4], F32)
        P = [ps("P%d" % i, [128, 512], F32) for i in range(6)]
        TB = ps("TB", [128, 1024], BF16)
        P7 = ps("P7", [128, 512], F32)
        ident, mask01, perm, ones_bf = (cmat[:, i, :] for i in range(4))

        def act(out, in_, func, reads, writes, scale=1.0, bias=0.0, accum=None):
            kw = {}
            if accum is not None:
                kw["accum_out"] = accum
            r.op(ACT, lambda e: e.activation(out=out, in_=in_, func=func, scale=scale, bias=bias, **kw), reads, writes)

        def tt(eng, out, in0, in1, op, reads, writes):
            r.op(eng, lambda e: e.tensor_tensor(out=out, in0=in0, in1=in1, op=op), reads, writes)

        def ts(eng, out, in0, s1, s2, op0, op1, reads, writes):
            if s2 is None:
                r.op(eng, lambda e: e.tensor_scalar(out=out, in0=in0, scalar1=s1, scalar2=None, op0=op0), reads, writes)
            else:
                r.op(eng, lambda e: e.tensor_scalar(out=out, in0=in0, scalar1=s1, scalar2=s2, op0=op0, op1=op1), reads, writes)

        def stt(out, in0, scalar, in1, op0, op1, reads, writes):
            r.op(DVE, lambda e: e.scalar_tensor_tensor(out=out, in0=in0, scalar=scalar, in1=in1, op0=op0, op1=op1), reads, writes)

        def cp(eng, out, in_, reads, writes):
            r.op(eng, lambda e: e.tensor_copy(out=out, in_=in_), reads, writes)

        def mm(out, lhsT, rhs, start, stop, reads, writes):
            r.op(PE, lambda e: e.matmul(out, lhsT=lhsT, rhs=rhs, start=start, stop=stop), reads, writes)

        def tr(out, in_, reads, writes):
            r.op(PE, lambda e: e.transpose(out=out, in_=in_, identity=ident), list(reads) + ["cmat"], writes)

        def dma(eng, out, in_, reads, writes, key, mode="serial"):
            if key[:3] in SKIP.split(","):
                return
            r.op(eng, lambda e: e.dma_start(out=out, in_=in_), reads, writes, dma_key=key, dma_mode=mode)

        def proj_fm(pbank, pname, col0, ngroups, width=128, pcol0=0):
            for g in range(ngroups):
                for k in range(8):
                    mm(pbank[0:width, pcol0 + g * 128: pcol0 + (g + 1) * 128],
                       w_in[:, k, col0 + g * width: col0 + (g + 1) * width], hT[:, k, :],
                       k == 0, k == 7, ["w_in%d" % k, "hT"], [pname])

        def proj_tm(pbank, pname, col0, n):
            for k in range(8):
                mm(pbank[:, 0:n], hT[:, k, :], w_in[:, k, col0:col0 + n], k == 0, k == 7, ["w_in%d" % k, "hT"], [pname])

        Ff = [f[:].rearrange("p a b -> p (a b)") for f in F]
        dma(SP, Ff[0], cmat_d[:, 0:512], [], ["f0"], "c1", "batch")
        dma(SP, Ff[3][:, 0:128], cmat_d[:, 512:640], [], ["f3"], "c1", "batch")
        dma(SP, Ff[1], gqk_d[:, 0:512], [], ["f1"], "c1", "batch")
        dma(SP, Ff[2], gqk_d[:, 512:1024], [], ["f2"], "c1", "batch")
        dma(SP, sel[:].rearrange("p a b -> p (a b)"), sel_d, [], ["sel"], "c1", "batch")
        dma(SP, retsc[:].rearrange("p a b -> p (a b)"), retsc_d, [], ["retsc"], "c1", "batch")
        dma(SP, cfm[:], cfm_d, [], ["cfm"], "c1", "batch")
        cp(DVE, cmat[:, 0:4, :].rearrange("p a b -> p (a b)"), Ff[0], ["f0"], ["cmat"])
        cp(DVE, cmat[:, 4, :], Ff[3][:, 0:128], ["f3"], ["cmat"])
        cp(DVE, gqk[:, 0:4, :].rearrange("p a b -> p (a b)"), Ff[1], ["f1"], ["gqk"])
        cp(DVE, gqk[:, 4:8, :].rearrange("p a b -> p (a b)"), Ff[2], ["f2"], ["gqk"])
        r.op(DVE, lambda e: e.memset(ones32[:], 1.0), [], ["ones32"])
        r.op(DVE, lambda e: e.memset(lrT[:], 1.0), [], ["lrT"])
        act(sm[:, 0:8], cfm[:], AF.Silu, ["cfm"], ["sm"])
        cp(DVE, cact[:], sm[:, 0:8].unsqueeze(2).to_broadcast([128, 8, 2]), ["sm"], ["cact"])
        wl = dict(ci=0, ring=[0, 1, 2])
        cast_engs = [ACT, DVE, POOL]

        def wload(dst_ap, src_ap, npart, width, bufname):
            ci = wl["ci"]
            j = wl["ring"][ci % len(wl["ring"])]
            sname = "f%d" % j
            st_ap = Ff[j][0:npart, 0:width]
            dma(SP, st_ap, src_ap, [], [sname], "wst%d" % j)
            eng = cast_engs[ci % 3]
            if eng == ACT:
                act(dst_ap, st_ap, AF.Copy, [sname], [bufname])
            else:
                cp(eng, dst_ap, st_ap, [sname], [bufname])
            wl["ci"] = ci + 1

        try:
          for l in range(n_layers):
              last = (l == n_layers - 1)
              xin = x_d if l == 0 else xs_d
              dma(SP, pfm[:], pfm_d[l], [], ["pfm"], "prm%d" % l, "batch")
              dma(SP, p8[:], p8_d[l], [], ["p8"], "prm%d" % l, "batch")
              dma(SP, xt2[:], bgate_d[l], [], ["xt1"], "prm%d" % l, "batch")
              w_in_v = w_in_d[l].rearrange("(k p) n -> p k n", p=128)
              w_ada_v = w_ada_d[l].rearrange("(k p) n -> p k n", p=128)
              cactbc = xt[:].rearrange("p (k n) -> p k n", k=8)
              cp(DVE, cactbc, sm[:, 0:8].unsqueeze(2).to_broadcast([128, 8, 128]), ["sm"], ["xt0"])
              wl["ring"] = [0, 1, 2]

              def ada_block(blk):
                  cg, half = blk // 2, blk % 2
                  j = 3 + (blk % 2)
                  sname = "f%d" % j
                  dma(SP, F[j][:], w_ada_v[:, half * 4:(half + 1) * 4, cg * 128:(cg + 1) * 128], [], [sname], "wst%d" % j)
                  for kk in range(4):
                      k = half * 4 + kk
                      if cg < 16:
                          mm(P7[:, cg * 2:cg * 2 + 2], F[j][:, kk, :], cact[:, k, :], k == 0, k == 7, [sname, "cact"], ["P7"])
                      else:
                          g = cg - 16
                          pb, pn = (P[0], "P0") if g < 4 else (P[1], "P1")
                          mm(pb[:, (g % 4) * 128:(g % 4 + 1) * 128], cactbc[:, k, :], F[j][:, kk, :], k == 0, k == 7,
                             ["xt0", sname], [pn])

              chunks = [(k, c0) for k in range(8) for c0 in range(0, NW, 512)]
              nblk = 0
              for i, (k, c0) in enumerate(chunks):
                  wd = min(512, NW - c0)
                  wload(w_in[:, k, c0:c0 + wd], w_in_v[:, k, c0:c0 + wd], 128, wd, "w_in%d" % k)
                  while nblk < 48 and nblk < (i + 1) * 48 // len(chunks):
                      ada_block(nblk)
                      nblk += 1
              while nblk < 48:
                  ada_block(nblk)
                  nblk += 1
              wload(wgk[0:17, :], wgk_d[l], 17, 256, "wgk")
              wl["ring"] = [0, 1, 2, 3, 4]
              chk("A")
              p7v = P7[:, 0:32].rearrange("p (a b) -> p a b", b=2)
              tt(DVE, sh[:], p7v[:, 0:8, 0], pfm[:, 8:16], ALU.add, ["P7", "pfm"], ["sh"])
              tt(DVE, gs[:], p7v[:, 8:16, 0], pfm[:, 16:24], ALU.add, ["P7", "pfm"], ["gs"])
              stt(gs[:], gs[:], 1.0, pfm[:, 0:8], ALU.add, ALU.mult, ["gs", "pfm"], ["gs"])
              tt(DVE, xt2[:, 0:512], P[0][:], xt2[:, 0:512], ALU.add, ["P0", "xt1"], ["xt1"])
              tt(DVE, xt2[:, 512:1024], P[1][:], xt2[:, 512:1024], ALU.add, ["P1", "xt1"], ["xt1"])
              w_out_v = w_out_d[l].rearrange("(k p) n -> p k n", p=128)
              for k in range(16):
                  for hf in range(2):
                      ci = wl["ci"]
                      j = ci % 5
                      sname = "f%d" % j
                      dma(SP, Ff[j], w_out_v[:, k, hf * 512:(hf + 1) * 512], [], [sname], "wst%d" % j)
                      tt(DVE if ci % 2 == 0 else POOL, w_out[:, k, hf * 512:(hf + 1) * 512], Ff[j], xt2[:, hf * 512:(hf + 1) * 512],
                         ALU.mult, [sname, "xt1"], ["w_out%d" % k])
                      wl["ci"] = ci + 1
              if l == 0:
                  r.op(DVE, lambda e: e.memset(lbt[:, :, 0], 0.0), [], ["lbt"])
              else:
                  act(sm[:, 8:16], pfm[:, 24:32], AF.Exp, ["pfm"], ["sm"])
                  tt(DVE, sm[:, 8:12], sm[:, 8:12], sm[:, 12:16], ALU.add, ["sm"], ["sm"])
                  r.op(DVE, lambda e: e.reciprocal(out=sm[:, 8:12], in_=sm[:, 8:12]), ["sm"], ["sm"])
                  tt(DVE, lbt[:, :, 0], sm[:, 8:12], sm[:, 12:16], ALU.mult, ["sm"], ["lbt"])
              ts(DVE, lbt[:, :, 1], lbt[:, :, 0], -1.0, 1.0, ALU.mult, ALU.add, ["lbt"], ["lbt"])
              ts(DVE, lbt[:, :, 2], lbt[:, :, 1], -1.0, None, ALU.mult, None, ["lbt"], ["lbt"])
              act(a8[:, 0:1], p8[:, 1:2], AF.Exp, ["p8"], ["a8"])
              ts(DVE, a8[:, 1:2], a8[:, 0:1], -1.0, None, ALU.mult, None, ["a8"], ["a8"])
              for (S, nm) in ((S_hg, "S_hg"), (S_rt, "S_rt"), (S_sd, "S_sd"), (S_gl, "S_gl")):
                  r.op(POOL, lambda e, S=S: e.memset(S[:], 0.0), [], [nm])
              r.op(POOL, lambda e: e.memset(ubuf[:, :, 0:3], 0.0), [], ["ubuf"])

              chk("B")
              for ti in range(n_tiles):
                  t0 = ti * 128
                  X = xts[ti % 2]
                  xk = "xt%d" % (ti % 2)

                  def load_x(tj):
                      dma(SP, xts[tj % 2][:], xin[tj * 128:(tj + 1) * 128, :], ["xs%d" % tj] if l > 0 else [], ["xt%d" % (tj % 2)],
                          "xin%d" % (tj % 2))

                  def load_rot(tj):
                      dma(SP, rot[:].rearrange("p a h b -> p (a h b)"), rot_d[tj], [], ["rot"], "rot")

                  if ti == 0:
                      load_x(0)
                      load_rot(0)
                  act(xn[:], X[:], AF.Square, [xk], ["xn", "ss"], accum=ss[:, 0:1])
                  act(ss[:, 1:2], ss[:, 0:1], AF.Ln, ["ss"], ["ss"], scale=1.0 / D, bias=EPS)
                  act(ss[:, 2:3], ss[:, 1:2], AF.Exp, ["ss"], ["ss"], scale=-0.5)
                  ts(DVE, xn[:], X[:], ss[:, 2:3], None, ALU.mult, None, [xk, "ss"], ["xn"])
                  if ti + 1 < n_tiles:
                      load_x(ti + 1)
                  for k in range(8):
                      tr(TB[:, k * 128:(k + 1) * 128], xn[:, k * 128:(k + 1) * 128], ["xn"], ["TB"])
                  for k in range(8):
                      ts(DVE, hT[:, k, :], TB[:, k * 128:(k + 1) * 128], gs[:, k:k + 1], sh[:, k:k + 1], ALU.mult, ALU.add,
                         ["TB", "gs", "sh"], ["hT"])

                  chk("C")
                  def core(groups, heads, S, Sn, scv, scn, vcol, yslot, og_col, norm_div, qz=None, qs=1.0, gb=None, gbn="hg"):
                      nh = len(heads)
                      gb = hg if gb is None else gb
                      for h in range(4):
                          ts(POOL, gb[:, h, :], gb[:, h, :], pfm[:, og_col + h: og_col + h + 1], 1.0, ALU.mult, ALU.mult, [gbn, "pfm"], [gbn])
                      for g in range(groups):
                          tr(TB[:, g * 128:(g + 1) * 128], kt[:, g, :], ["kt"], ["TB"])
                      act(ktT[:, 0:groups, :], TB[:, 0:groups * 128].rearrange("p (a b) -> p a b", b=128), AF.Copy, ["TB"], ["ktT"])
                      for hi, (g, po, dk) in enumerate(heads):
                          if qz is None:
                              mm(P[3][:, hi * 128:(hi + 1) * 128], kt[po:po + dk, g, :], qt[po:po + dk, g, :], True, True,
                                 ["kt", "qt"], ["P3"])
                          else:
                              mm(P[3][:, hi * 128:(hi + 1) * 128], kt[:, g, :], qz[:, hi, :], True, True, ["kt", "hq"], ["P3"])
                      maskt = mask01 if qs == 1.0 else cmat[:, 4, :]
                      if os.environ.get("NEWPT"):
                          stt(pt[:, 0:nh, :], P[3][:, 0:nh * 128].rearrange("p (a b) -> p a b", b=128), 1e25,
                              maskt.unsqueeze(1).to_broadcast([128, nh, 128]), ALU.min, ALU.mult, ["P3", "cmat"], ["pt"])
                      else:
                          ts(DVE, pt[:, 0:nh, :], P[3][:, 0:nh * 128].rearrange("p (a b) -> p a b", b=128), 1e25, -1e25,
                             ALU.min, ALU.max, ["P3"], ["pt"])
                          tt(DVE, pt[:, 0:nh, :], pt[:, 0:nh, :],
                             maskt.unsqueeze(1).to_broadcast([128, nh, 128]), ALU.mult, ["pt", "cmat"], ["pt"])
                      dv = 128
                      for g in range(groups):
                          ts(POOL, sp0[:, g * dv:(g + 1) * dv], S[:, g, :], scv[:, g, 0:1], qs, ALU.mult, ALU.mult, [Sn, scn], ["sp0"])
                      for c in range(2):
                          for hi, (g, po, dk) in enumerate(heads):
                              mm(P[5][po:po + dk, g * 128:(g + 1) * 128], ktT[c * 64:(c + 1) * 64, g, po:po + dk],
                                 vv[c * 64:(c + 1) * 64, vcol + hi * 128: vcol + (hi + 1) * 128], True, True, ["ktT", "vv"], ["P5"])
                          for g in range(groups):
                              act(F[3][:, g, :], P[5][:, g * 128:(g + 1) * 128], AF.Copy, ["P5", scn], ["f3_%d" % g],
                                  scale=scv[:, g, 4 + c:5 + c])
                          for g in range(groups):
                              stt(S[:, g, :], S[:, g, :], scv[:, g, 2 + c:3 + c], F[3][:, g, :], ALU.mult, ALU.add,
                                  [Sn, scn, "f3_%d" % g], [Sn])
                          if c == 0:
                              for g in range(groups):
                                  ts(POOL, sp1[:, g * dv:(g + 1) * dv], S[:, g, :], scv[:, g, 1:2], qs, ALU.mult, ALU.mult, [Sn, scn], ["sp1"])
                              for hi, (g, po, dk) in enumerate(heads):
                                  o = P[4][:, hi * 128:(hi + 1) * 128]
                                  mm(o, vv[:, vcol + hi * 128: vcol + (hi + 1) * 128], pt[:, hi, :], True, False, ["vv", "pt"], ["P4"])
                                  if qz is None:
                                      mm(o[:, 0:64], sp0[po:po + dk, g * dv:(g + 1) * dv], qt[po:po + dk, g, 0:64], False, False,
                                         ["sp0", "qt"], ["P4"])
                                      mm(o[:, 64:128], sp1[po:po + dk, g * dv:(g + 1) * dv], qt[po:po + dk, g, 64:128], False, True,
                                         ["sp1", "qt"], ["P4"])
                                  else:
                                      mm(o[:, 0:64], sp0[:, g * dv:(g + 1) * dv], qz[:, hi, 0:64], False, False, ["sp0", "hq"], ["P4"])
                                      mm(o[:, 64:128], sp1[:, g * dv:(g + 1) * dv], qz[:, hi, 64:128], False, True, ["sp1", "hq"], ["P4"])
                      act(xn[:, 0:512], P[4][:], AF.Square, ["P4"], ["xn"])
                      mm(P7[:, 0:512], ones_bf, xn[:, 0:512], True, True, ["xn", "cmat"], ["P7"])
                      act(F[0][:].rearrange("p a b -> p (a b)"), P7[:], AF.Ln, ["P7"], ["f0"], scale=1.0 / norm_div, bias=EPS)
                      act(F[0][:].rearrange("p a b -> p (a b)"), F[0][:].rearrange("p a b -> p (a b)"), AF.Exp, ["f0"], ["f0"], scale=-0.5)
                      tt(DVE, F[1][:].rearrange("p a b -> p (a b)"), P[4][:], F[0][:].rearrange("p a b -> p (a b)"), ALU.mult,
                         ["P4", "f0"], ["f1"])
                      tt(POOL, yT[:, yslot:yslot + 4, :], F[1][:], gb[:], ALU.mult, ["f1", gbn], ["yT"])

                  def vec_decay_prep(ngr, logf_buf, logf_name):
                      for g in range(ngr):
                          r.op(DVE, lambda e, g=g: e.tensor_tensor_scan(out=F[4][:, g, :], data0=ones32[:], data1=logf_buf[:, g, :],
                                                                         initial=0.0, op0=ALU.mult, op1=ALU.add),
                               [logf_name, "ones32"], ["f4"])
                      cum = F[4]
                      ts(DVE, negm[:, 0:ngr, :], cum[:, 0:ngr, 31:128:64], -1.0, -QL, ALU.mult, ALU.add, ["f4"], ["negm"])
                      for g in range(ngr):
                          for c in range(2):
                              act(F[2][:, g, c * 64:(c + 1) * 64], cum[:, g, c * 64:(c + 1) * 64], AF.Exp, ["f4", "negm"], ["f2"],
                                  bias=negm[:, g, c:c + 1])
                      for g in range(ngr):
                          for c in range(2):
                              act(F[3][:, g, c * 64:(c + 1) * 64], cum[:, g, c * 64:(c + 1) * 64], AF.Exp, ["f4"], ["f3_%d" % g],
                                  scale=-1.0, bias=cum[:, g, 31 + 64 * c: 32 + 64 * c])
                      cp(POOL, sc[:, 0:ngr, 0], cum[:, 0:ngr, 31], ["f4"], ["sc"])
                      tt(POOL, sc[:, 0:ngr, 1], cum[:, 0:ngr, 95], cum[:, 0:ngr, 63], ALU.subtract, ["f4"], ["sc"])
                      cp(POOL, sc[:, 0:ngr, 2], cum[:, 0:ngr, 63], ["f4"], ["sc"])
                      tt(POOL, sc[:, 0:ngr, 3], cum[:, 0:ngr, 127], cum[:, 0:ngr, 63], ALU.subtract, ["f4"], ["sc"])
                      tt(POOL, sc[:, 0:ngr, 4], cum[:, 0:ngr, 63], cum[:, 0:ngr, 31], ALU.subtract, ["f4"], ["sc"])
                      tt(POOL, sc[:, 0:ngr, 5], cum[:, 0:ngr, 127], cum[:, 0:ngr, 95], ALU.subtract, ["f4"], ["sc"])
                      act(sc[:, 0:ngr, :], sc[:, 0:ngr, :], AF.Exp, ["sc"], ["sc"])

                  flat = lambda t: t[:].rearrange("p a b -> p (a b)")

                  r.op(POOL, lambda e: e.memset(yT[:], 0.0), [], ["yT"]) if BR != "hrgs" else None
                  def br_h(before_core=None):
                      proj_fm(P[0], "P0", C_AQ, 4)
                      act(flat(hq), P[0][:], AF.Silu, ["P0"], ["hq"])
                      proj_fm(P[2], "P2", C_AG, 4)
                      act(flat(hg), P[2][:], AF.Silu, ["P2"], ["hg"])
                      if GROUPG:
                          proj_fm(P[2], "P2", C_RG, 4)
                          act(flat(hgB), P[2][:], AF.Silu, ["P2"], ["hgB"])
                      proj_fm(P[1], "P1", C_AF, 4)
                      act(flat(F[0]), P[1][:], AF.Exp, ["P1"], ["f0"], scale=-1.0)
                      act(flat(F[0]), flat(F[0]), AF.Ln, ["f0"], ["f0"], bias=1.0)
                      act(flat(F[1]), flat(F[0]), AF.Exp, ["f0"], ["f1"], scale=-1.0)
                      for h in range(4):
                          act(F[0][:, h, :], F[1][:, h, :], AF.Ln, ["f1_%d" % h, "lbt"], ["f0_%d" % h], scale=lbt[:, h, 1:2], bias=lbt[:, h, 0:1])
                          ts(POOL, F[1][:, h, :], F[1][:, h, :], lbt[:, h, 2:3], lbt[:, h, 1:2], ALU.mult, ALU.add, ["f1_%d" % h, "lbt"], ["f1_%d" % h])
                      vec_decay_prep(4, F[0], "f0")
                      tt(POOL, qt[:], hq[:], F[2][:], ALU.mult, ["hq", "f2"], ["qt"])
                      tt(POOL, kt[:], F[1][:], F[3][:], ALU.mult, ["f1", "f3"], ["kt"])
                      proj_tm(P[0], "P0", C_AI, 512)
                      act(vv[:], P[0][:], AF.Copy, ["P0"], ["vv"])
                      if before_core is not None:
                          before_core()
                      core(4, [(h, 0, 128) for h in range(4)], S_hg, "S_hg", sc, "sc", 0, 0, 32, 128.0, qs=QS)

                  def pre_r():
                      proj_fm(P[1], "P1", C_RQ, 4)
                      proj_fm(P[2], "P2", C_RG, 4)
                      proj_tm(P[0], "P0", C_RV, 512)

                  def pre_g():
                      proj_fm(P[0], "P0", C_GQ, 2)
                      proj_fm(P[2], "P2", C_GG, 4)

                  def pre_s():
                      proj_fm(P[0], "P0", C_MX, 4)
                      proj_fm(P[1], "P1", C_MX + 512, 4)
                      proj_fm(P[2], "P2", C_MZ, 4)

                  PIPE = (BR == "hrgs") and bool(os.environ.get("PIPE"))
                  if "h" in BR and not PIPE:
                      br_h()
                  def br_r(before_core=None, pre=False):
                      for (ccol, tab, dst, dn) in ((C_RQ, 0, qt, "qt"), (C_RK, 4, kt, "kt")):
                          if not (pre and ccol == C_RQ):
                              proj_fm(P[1], "P1", ccol, 4)
                          act(flat(hq), P[1][:], AF.Copy, ["P1"], ["hq"])
                          chk("Ra")
                          mm(P7[:, 0:512], perm, flat(hq), True, True, ["hq", "cmat"], ["P7"])
                          chk("Rb")
                          p1v = P[1][:].rearrange("p (a b) -> p a b", b=128)
                          p7v2 = P7[:].rearrange("p (a b) -> p a b", b=128)
                          RT = os.environ.get("RTEST", "")
                          if RT == "1":
                              tt(DVE, F[0][:], p1v, F[2][:], ALU.mult, ["P1", "f2"], ["f0"])
                          elif RT == "2":
                              cp(DVE, F[0][:], p1v, ["P1"], ["f0"])
                          elif RT == "3":
                              tt(DVE, F[0][:], F[2][:], rot[:, 0, :, :], ALU.mult, ["f2", "rot"], ["f0"])
                          else:
                              tt(DVE, F[0][:], p1v, rot[:, 0, :, :], ALU.mult, ["P1", "rot"], ["f0"])
                          chk("Rc1")
                          tt(DVE, F[1][:], p7v2, rot[:, 1, :, :], ALU.mult, ["P7", "rot"], ["f1"])
                          chk("Rc")
                          tt(POOL, F[0][:], F[0][:], F[1][:], ALU.add, ["f0", "f1"], ["f0"])
                          chk("Rd")
                          tt(POOL, dst[:], F[0][:], gqk[:, tab:tab + 4, :], ALU.mult, ["f0", "gqk"], [dn])
                      if not GROUPG:
                          if not pre:
                              proj_fm(P[2], "P2", C_RG, 4)
                          act(flat(hg), P[2][:], AF.Silu, ["P2"], ["hg"])
                      if not pre:
                          proj_tm(P[0], "P0", C_RV, 512)
                      act(vv[:], P[0][:], AF.Copy, ["P0"], ["vv"])
                      chk("R1")
                      if before_core is not None:
                          before_core()
                      if GROUPG:
                          core(4, [(h, 0, 128) for h in range(4)], S_rt, "S_rt", retsc, "retsc", 0, 4, 36, 128.0, gb=hgB, gbn="hgB")
                      else:
                          core(4, [(h, 0, 128) for h in range(4)], S_rt, "S_rt", retsc, "retsc", 0, 4, 36, 128.0)

                  if "r" in BR and not PIPE:
                      br_r()
                  if not PIPE and ti + 1 < n_tiles:
                      load_rot(ti + 1)
                  def br_g(before_core=None, pre=False):
                      proj_fm(P7, "P7", C_LR, 1, width=16)
                      cp(DVE, lrT[0:16, :], P7[0:16, 0:128], ["P7"], ["lrT"])
                      for g in range(2):
                          mm(P[1][:, g * 128:(g + 1) * 128], wgk[0:17, g * 128:(g + 1) * 128], lrT[0:17, :], True, True, ["wgk", "lrT"], ["P1"])
                      act(F[0][:, 0:2, :], P[1][:, 0:256].rearrange("p (a b) -> p a b", b=128), AF.Exp, ["P1"], ["f0"], scale=-1.0)
                      act(F[0][:, 0:2, :], F[0][:, 0:2, :], AF.Ln, ["f0"], ["f0"], bias=1.0)
                      ts(DVE, F[0][:, 0:2, :], F[0][:, 0:2, :], -1.0 / 16.0, None, ALU.mult, None, ["f0"], ["f0"])
                      vec_decay_prep(2, F[0], "f0")
                      if not pre:
                          proj_fm(P[0], "P0", C_GQ, 2)
                      stt(qt[:, 0:2, :], P[0][:, 0:256].rearrange("p (a b) -> p a b", b=128), 0.125, F[2][:, 0:2, :], ALU.mult, ALU.mult,
                          ["P0", "f2"], ["qt"])
                      proj_fm(P[1], "P1", C_GK, 2)
                      tt(DVE, kt[:, 0:2, :], P[1][:, 0:256].rearrange("p (a b) -> p a b", b=128), F[3][:, 0:2, :], ALU.mult, ["P1", "f3"], ["kt"])
                      if not pre:
                          proj_fm(P[2], "P2", C_GG, 4)
                      act(flat(hg), P[2][:], AF.Silu, ["P2"], ["hg"])
                      proj_tm(P[0], "P0", C_GV, 512)
                      act(vv[:], P[0][:], AF.Copy, ["P0"], ["vv"])
                      r.op(POOL, lambda e: e.memset(hq[:], 0.0), [], ["hq"])
                      for hi, (g, po) in enumerate(((0, 0), (0, 64), (1, 0), (1, 64))):
                          cp(POOL, hq[po:po + 64, hi, :], qt[po:po + 64, g, :], ["qt"], ["hq"])
                      if before_core is not None:
                          before_core()
                      core(2, [(0, 0, 64), (0, 64, 64), (1, 0, 64), (1, 64, 64)], S_gl, "S_gl", sc, "sc", 0, 12, 44, 128.0, qz=hq, qs=QS)

                  if "g" in BR and not PIPE:
                      br_g()
                  def br_s(pre=False):
                      for half in range(2):
                          if not pre:
                              proj_fm(P[half], "P%d" % half, C_MX + half * 512, 4)
                      cp(DVE, ubuf[:, 0:4, 3:131], P[0][:].rearrange("p (a b) -> p a b", b=128), ["P0"], ["ubuf"])
                      act(ubuf[:, 4:8, 3:131], P[1][:].rearrange("p (a b) -> p a b", b=128), AF.Copy, ["P1"], ["ubuf"])
                      cacc = [F[0], F[1]]
                      for g in range(8):
                          ca = cacc[g // 4][:, g % 4, :]
                          cn = "f%d_%d" % (g // 4, g % 4)
                          ts(POOL, ca, ubuf[:, g, 0:128], pfm[:, 56 + g * 4: 57 + g * 4], pfm[:, 48 + g: 49 + g], ALU.mult, ALU.add,
                             ["ubuf", "pfm"], [cn])
                          for j in range(1, 4):
                              stt(ca, ubuf[:, g, j:j + 128], pfm[:, 56 + g * 4 + j: 57 + g * 4 + j], ca, ALU.mult, ALU.add,
                                  ["ubuf", "pfm", cn], [cn])
                      cp(POOL, ubuf[:, :, 0:3], ubuf[:, :, 128:131], ["f0", "f1", "ubuf"], ["ubuf"])
                      act(xc[:, 0:4, :], F[0][:], AF.Silu, ["f0"], ["xc"])
                      act(xc[:, 4:8, :], F[1][:], AF.Silu, ["f1"], ["xc"])
                      if not pre:
                          proj_fm(P[2], "P2", C_MZ, 4)
                      act(flat(hg), P[2][:], AF.Silu, ["P2"], ["hg"])
                      proj_fm(P7, "P7", C_DT, 1, width=8)
                      act(d8[:, 0, :], P7[0:8, 0:128], AF.Exp, ["P7", "p8"], ["d8"], bias=p8[:, 0:1])
                      act(d8[:, 1, :], d8[:, 0, :], AF.Ln, ["d8"], ["d8"], bias=1.0)
                      ts(DVE, d8[:, 0, :], d8[:, 1, :], a8[:, 1:2], None, ALU.mult, None, ["d8", "a8"], ["d8"])
                      for c in range(2):
                          r.op(DVE, lambda e, c=c: e.tensor_tensor_scan(out=d8[:, 2, c * 64:(c + 1) * 64], data0=ones32[0:8, 0:64],
                                                                         data1=d8[:, 0, c * 64:(c + 1) * 64], initial=0.0,
                                                                         op0=ALU.mult, op1=ALU.add), ["d8", "ones32"], ["d8"])
                      for c in range(2):
                          act(d8[:, 3, c * 64:(c + 1) * 64], d8[:, 2, c * 64:(c + 1) * 64], AF.Exp, ["d8"], ["d8"], scale=-1.0,
                              bias=d8[:, 2, c * 64 + 63: c * 64 + 64])
                      tt(DVE, d8[:, 3, :], d8[:, 3, :], d8[:, 1, :], ALU.mult, ["d8"], ["d8"])
                      for i, row in enumerate((1, 3, 2)):
                          r.op(PE, lambda e, i=i, row=row: e.matmul(P7[:, 256 + i * 8: 256 + (i + 1) * 8], lhsT=d8[:, row, :],
                                                                    rhs=sel[:, :, 0], start=True, stop=True), ["d8", "sel"], ["P7"])
                      cp(DVE, ssdT[:].rearrange("p a b -> p (a b)"), P7[:, 256:280], ["P7"], ["ssdT"])
                      for g in range(4):
                          tr(TB[:, g * 128:(g + 1) * 128], xc[:, g, :], ["xc"], ["TB"])
                      tbv = TB[:, 0:512].rearrange("p (a b) -> p a b", b=64)
                      tt(DVE, vv[:].rearrange("p (a b) -> p a b", b=64), tbv, ssdT[:, 0, :].unsqueeze(2).to_broadcast([128, 8, 64]), ALU.mult,
                         ["TB", "ssdT"], ["vv"])
                      tt(DVE, flat(hq).rearrange("p (a b) -> p a b", b=64), tbv, ssdT[:, 1, :].unsqueeze(2).to_broadcast([128, 8, 64]), ALU.mult,
                         ["TB", "ssdT"], ["hq"])
                      v2 = flat(hq)
                      for gi in range(2):
                          tr(TB[:, 512 + gi * 128: 512 + (gi + 1) * 128], xc[:, 4 + gi, :], ["xc"], ["TB"])
                      act(ktT[:, 0:2, :], TB[:, 512:768].rearrange("p (a b) -> p a b", b=128), AF.Copy, ["TB"], ["ktT"])
                      cp(POOL, sp0[:], S_sd[:].rearrange("p a b -> p (a b)"), ["S_sd"], ["sp0"])
                      for gi in range(2):
                          for hh in range(4):
                              h = gi * 4 + hh
                              mm(P7[:, hh * 128:(hh + 1) * 128], sel[:, h, :], d8[:, 2, :], True, True, ["sel", "d8"], ["P7"])
                          for hh in range(4):
                              h = gi * 4 + hh
                              ts(DVE, F[2][:, hh, :], P7[:, hh * 128:(hh + 1) * 128], ssdT[:, 2, h:h + 1], 0.0, ALU.subtract, ALU.min,
                                 ["P7", "ssdT"], ["f2"])
                          act(flat(kt), flat(F[2]), AF.Exp, ["f2"], ["kt"])
                          act(flat(F[3]), P7[:], AF.Exp, ["P7"], ["f3"])
                          mm(P[3][:, 0:128], xc[:, 4 + gi, :], xc[:, 6 + gi, :], True, True, ["xc"], ["P3"])
                          tt(DVE, gm[:], P[3][:, 0:128], mask01, ALU.mult, ["P3", "cmat"], ["gm"])
                          tt(POOL, pt[:], kt[:], gm[:].unsqueeze(1).to_broadcast([128, 4, 128]), ALU.mult, ["kt", "gm"], ["pt"])
                          tt(POOL, qt[:], F[3][:], xc[:, 6 + gi, :].unsqueeze(1).to_broadcast([128, 4, 128]), ALU.mult, ["f3", "xc"], ["qt"])
                          for c in range(2):
                              mm(P[5][:, 0:256], ktT[c * 64:(c + 1) * 64, gi, :], v2[c * 64:(c + 1) * 64, gi * 256:(gi + 1) * 256], True, True,
                                 ["ktT", "hq"], ["P5"])
                              for hh in range(4):
                                  h = gi * 4 + hh
                                  stt(S_sd[:, h, :], S_sd[:, h, :], F[3][:, hh, c * 64 + 63: c * 64 + 64], P[5][:, hh * 64:(hh + 1) * 64],
                                      ALU.mult, ALU.add, ["S_sd", "f3", "P5"], ["S_sd"])
                              if c == 0:
                                  cp(POOL, sp1[:, gi * 256:(gi + 1) * 256], S_sd[:, gi * 4:(gi + 1) * 4, :].rearrange("p a b -> p (a b)"),
                                     ["S_sd"], ["sp1"])
                                  for hh in range(4):
                                      h = gi * 4 + hh
                                      o = P[4][(h % 2) * 64:(h % 2) * 64 + 64, (h // 2) * 128:(h // 2 + 1) * 128]
                                      mm(o, vv[:, h * 64:(h + 1) * 64], pt[:, hh, :], True, False, ["vv", "pt"], ["P4"])
                                      mm(o[:, 0:64], sp0[:, h * 64:(h + 1) * 64], qt[:, hh, 0:64], False, False, ["sp0", "qt"], ["P4"])
                                      mm(o[:, 64:128], sp1[:, h * 64:(h + 1) * 64], qt[:, hh, 64:128], False, True, ["sp1", "qt"], ["P4"])
                      for g in range(4):
                          stt(F[0][:, g, :], xc[:, g, :], pfm[:, 88 + g: 89 + g], P[4][:, g * 128:(g + 1) * 128], ALU.mult, ALU.add,
                              ["xc", "pfm", "P4"], ["f0"])
                      tt(POOL, F[0][:], F[0][:], hg[:], ALU.mult, ["f0", "hg"], ["f0"])
                      act(xn[:, 0:512], flat(F[0]), AF.Square, ["f0"], ["xn"])
                      for gi in range(2):
                          for j in range(2):
                              mm(P7[:, gi * 128:(gi + 1) * 128], ones_bf, xn[:, (2 * gi + j) * 128:(2 * gi + j + 1) * 128], j == 0, j == 1,
                                 ["xn", "cmat"], ["P7"])
                      act(F[1][:, 0:2, :], P7[:, 0:256].rearrange("p (a b) -> p a b", b=128), AF.Ln, ["P7"], ["f1"], scale=1.0 / 256.0, bias=EPS)
                      act(F[1][:, 0:2, :], F[1][:, 0:2, :], AF.Exp, ["f1"], ["f1"], scale=-0.5)
                      for g in range(4):
                          stt(yT[:, 8 + g, :], F[0][:, g, :], pfm[:, 40 + g: 41 + g], F[1][:, g // 2, :], ALU.mult, ALU.mult,
                              ["f0", "pfm", "f1"], ["yT"])

                  if "s" in BR and not PIPE:
                      br_s()
                  if PIPE:
                      br_h(before_core=pre_r)
                      br_r(before_core=pre_g, pre=True)
                      if ti + 1 < n_tiles:
                          load_rot(ti + 1)
                      br_g(before_core=pre_s, pre=True)
                      br_s(pre=True)
                  if dbg:
                      DUMP = os.environ.get("DUMP", "")
                      if DUMP:
                          bufs = dict(qt=qt, kt=kt, hq=hq, hg=hg, pt=pt, ktT=ktT)
                          for hf, nm in enumerate(DUMP.split(",")):
                              if nm.startswith("f"):
                                  j = int(nm[1:])
                                  dma(SP, dbg_d[l, ti, :, hf * 512:(hf + 1) * 512], Ff[j], [nm], ["dbgout"], "dbg")
                              elif nm in ("vv", "sp0", "sp1"):
                                  src = dict(vv=vv, sp0=sp0, sp1=sp1)[nm]
                                  cp(POOL, Ff[hf], src[:], [nm], ["f%d" % hf])
                                  dma(SP, dbg_d[l, ti, :, hf * 512:(hf + 1) * 512], Ff[hf], ["f%d" % hf], ["dbgout"], "dbg")
                              elif nm.startswith("S"):
                                  src = dict(S_hg=S_hg, S_rt=S_rt)[nm]
                                  dma(SP, dbg_d[l, ti, :, hf * 512:(hf + 1) * 512], src[:].rearrange("p a b -> p (a b)"), [nm], ["dbgout"], "dbg")
                              else:
                                  cp(POOL, F[hf][:], bufs[nm][:], [nm], ["f%d" % hf])
                                  dma(SP, dbg_d[l, ti, :, hf * 512:(hf + 1) * 512], Ff[hf], ["f%d" % hf], ["dbgout"], "dbg")
                      else:
                          for hf in range(4):
                              cp(POOL, F[2][:], yT[:, hf * 4:(hf + 1) * 4, :], ["yT"], ["f2"])
                              dma(SP, dbg_d[l, ti, :, hf * 512:(hf + 1) * 512], flat(F[2]), ["f2"], ["dbgout"], "dbg")

                  if last:
                      for half in range(2):
                          dma(SP, flat(F[2 + half]), finalg_d[:, half * 512:(half + 1) * 512], [], ["f%d" % (2 + half)], "fg%d" % half)
                  for half in range(2):
                      for kc in range(16):
                          mm(P[half][:], yT[:, kc, :], w_out[:, kc, half * 512:(half + 1) * 512], kc == 0, kc == 15,
                             ["yT", "w_out%d" % kc], ["P%d" % half])
                      tt(DVE, X[:, half * 512:(half + 1) * 512], P[half][:], X[:, half * 512:(half + 1) * 512], ALU.add,
                         ["P%d" % half, xk], [xk])
                  if last:
                      act(xn[:], X[:], AF.Square, [xk], ["xn", "ss"], accum=ss[:, 0:1])
                      act(ss[:, 1:2], ss[:, 0:1], AF.Ln, ["ss"], ["ss"], scale=1.0 / D, bias=EPS)
                      act(ss[:, 2:3], ss[:, 1:2], AF.Exp, ["ss"], ["ss"], scale=-0.5)
                      for half in range(2):
                          stt(X[:, half * 512:(half + 1) * 512], X[:, half * 512:(half + 1) * 512], ss[:, 2:3], flat(F[2 + half]),
                              ALU.mult, ALU.mult, [xk, "ss", "f%d" % (2 + half)], [xk])
                      dma(SP, y_d[t0:t0 + 128, :], X[:], [xk], ["yout"], "xout%d" % (ti % 2))
                  else:
                      dma(SP, xs_d[t0:t0 + 128, :], X[:], [xk], ["xs%d" % ti], "xout%d" % (ti % 2))
        except _Stop:
            dma(SP, y_d[0:128, :], xt[:], ["xt0"], ["yout"], "xout0")
        r.op(SP, lambda e: None, ["yout"] + (["dbgout"] if dbg else []), [])
        stats = r.emit()
    return nc, stats


def _const_tables(n_tiles):
    L = n_tiles * 128
    ident = np.eye(128, dtype=np.float32)
    s = np.arange(128)[:, None]
    t = np.arange(128)[None, :]
    mask01 = ((s // 64 == t // 64) & (s <= t)).astype(np.float32)
    perm = np.zeros((128, 128), np.float32)
    perm[(np.arange(128) + 64) % 128, np.arange(128)] = 1.0
    ones = np.ones((128, 128), np.float32)
    cmat = np.concatenate([ident, mask01, perm, ones, mask01 * QS], axis=1)
    sel = np.zeros((8, 8, 128), np.float32)
    for h in range(8):
        sel[h, h, :] = 1.0
    gam = 1.0 - np.exp2(-(5.0 + np.arange(4, dtype=np.float64)))
    tp = (np.arange(128) % 64 + 1).astype(np.float64)
    gq = np.stack([gam[h] ** tp for h in range(4)], 0)
    gk = np.stack([gam[h] ** (-tp) * (128.0 ** -0.5) for h in range(4)], 0)
    gqk = np.concatenate([np.broadcast_to(gq[None], (128, 4, 128)), np.broadcast_to(gk[None], (128, 4, 128))], axis=1)
    gqk = np.ascontiguousarray(gqk, dtype=np.float32).reshape(128, 1024)
    retsc = np.zeros((128, 4, 6), np.float32)
    for h in range(4):
        g64 = gam[h] ** 64
        retsc[:, h, :] = [1.0, 1.0, g64, g64, g64, g64]
    retsc = retsc.reshape(128, 24)
    inv_freq = (10000.0 ** (-np.arange(0, 128, 2, dtype=np.float32) / 128)).astype(np.float32)
    ang = np.arange(L, dtype=np.float32)[:, None] * inv_freq[None, :]
    cos = np.cos(ang).astype(np.float32).T
    sin = np.sin(ang).astype(np.float32).T
    cosf = np.concatenate([cos, cos], 0)
    sinf = np.concatenate([-sin, sin], 0)
    rot = np.stack([cosf.reshape(128, n_tiles, 128), sinf.reshape(128, n_tiles, 128)], axis=2)
    rot = np.broadcast_to(rot.transpose(1, 0, 2, 3)[:, :, :, None, :], (n_tiles, 128, 2, 4, 128))
    rot = np.ascontiguousarray(rot).reshape(n_tiles, 128, 1024)
    return dict(cmat=cmat, sel=sel.reshape(8, 1024), gqk=gqk, retsc=retsc, rot=rot)


def _fm(v, n):
    return np.ascontiguousarray(np.asarray(v, np.float32).reshape(n, 128).T)


def _pack_params(inp, n_layers):
    pfm = np.zeros((n_layers, 128, NF), np.float32)
    p8 = np.zeros((n_layers, 8, 2), np.float32)
    bgate = np.zeros((n_layers, 128, D), np.float32)
    wgk = np.zeros((n_layers, 17, 256), np.float32)
    for l in range(n_layers):
        pfm[l, :, 0:8] = _fm(inp["norm_g"][l], 8)
        pfm[l, :, 8:16] = _fm(inp["b_ada"][l][0:D], 8)
        pfm[l, :, 16:24] = _fm(inp["b_ada"][l][D:2 * D], 8)
        pfm[l, :, 24:28] = _fm(inp["hgrn_lb_logits"][0], 4)
        pfm[l, :, 28:32] = _fm(inp["hgrn_lb_logits"][min(1, inp["hgrn_lb_logits"].shape[0] - 1)], 4)
        pfm[l, :, 32:36] = _fm(inp["hgrn_onorm_g"][l], 4)
        pfm[l, :, 36:40] = _fm(inp["ret_onorm_g"][l], 4)
        pfm[l, :, 40:44] = _fm(inp["ssm_norm_g"][l], 4)
        pfm[l, :, 44:48] = _fm(inp["gla_onorm_g"][l], 4)
        pfm[l, :, 48:56] = _fm(inp["ssm_conv_b"][l], 8)
        cw = np.asarray(inp["ssm_conv_w"][l], np.float32)
        pfm[l, :, 56:88] = cw.reshape(4, 8, 128).transpose(2, 1, 0).reshape(128, 32)
        pfm[l, :, 88:92] = _fm(np.repeat(np.asarray(inp["ssm_d"][l], np.float32), 64), 4)
        p8[l, :, 0] = inp["ssm_dt_bias"][l]
        p8[l, :, 1] = inp["ssm_a_log"][l]
        bgate[l] = np.broadcast_to(np.asarray(inp["b_ada"][l][2 * D:3 * D], np.float32)[None, :], (128, D))
        wgk[l, 0:16] = inp["gla_w_gk2"][l]
        wgk[l, 16] = inp["gla_b_gk2"][l]
    return pfm, p8, bgate, wgk


_CACHE = {}


def run(inputs, n_tiles, n_layers, core_batches, dbg=False):
    inp = {k: np.asarray(v) for k, v in inputs.items()}
    key = (n_tiles, n_layers, dbg)
    if key not in _CACHE:
        _CACHE[key] = build_program(n_tiles, n_layers, dbg)
    nc, stats = _CACHE[key]
    L = n_tiles * 128
    consts = _const_tables(n_tiles)
    pfm, p8, bgate, wgk = _pack_params(inp, n_layers)
    finalg = np.ascontiguousarray(np.broadcast_to(np.asarray(inp["final_g"], np.float32)[None, :], (128, D)))
    shared = dict(w_in=np.ascontiguousarray(inp["w_in"][:n_layers], dtype=np.float32),
                  w_out=np.ascontiguousarray(inp["w_out"][:n_layers], dtype=np.float32),
                  w_ada=np.ascontiguousarray(inp["w_ada"][:n_layers], dtype=np.float32),
                  pfm=pfm, p8=p8, bgate=bgate, wgk=wgk, finalg=finalg, **consts)
    in_maps = []
    for b in core_batches:
        m = dict(shared)
        m["x"] = np.ascontiguousarray(inp["x"][b, :L], dtype=np.float32)
        m["cfm"] = _fm(inp["c"][b], 8)
        in_maps.append(m)
    res = run_bass_kernel_spmd(nc, in_maps, core_ids=list(range(len(core_batches))))
    return res, stats


def kernel(**inputs):
    B, Lfull, _ = inputs["x"].shape
    n_tiles = Lfull // 128
    core_batches = [i % B for i in range(8)]
    res, _ = run(inputs, n_tiles, 2, core_batches)
    out = np.stack([np.asarray(res.results[b]["y"], dtype=np.float32) for b in range(B)], axis=0)
    return out
```

```python
import math
import os
BR = os.environ.get("BR", "hrgs")
STAGE = os.environ.get("STAGE", "")
SKIP = os.environ.get("SKIP", "")
GROUPG = (BR == "hrgs") and not os.environ.get("NOGROUPG")


class _Stop(Exception):
    pass


def chk(n):
    if STAGE == n:
        raise _Stop()
import numpy as np
from contextlib import ExitStack
import concourse.bass as bass
import concourse.mybir as mybir
from concourse.bass_utils import run_bass_kernel_spmd

F32 = mybir.dt.float32
BF16 = mybir.dt.bfloat16
AF = mybir.ActivationFunctionType
ALU = mybir.AluOpType
PE, ACT, DVE, POOL, SP = "pe", "act", "dve", "pool", "sp"
COMPUTE = (PE, ACT, DVE, POOL)

D = 1024
QS = float(2 ** 30)
QL = 30.0 * math.log(2.0)
NW = 7192
EPS = 1e-6
C_AQ, C_AF, C_AI, C_AG = 0, 512, 1024, 1536
C_RQ, C_RK, C_RV, C_RG = 2048, 2560, 3072, 3584
C_MZ, C_MX, C_DT = 4096, 4608, 5632
C_GQ, C_GK, C_GV, C_GG, C_LR = 5640, 5896, 6152, 6664, 7176
NF = 92


EXPAND = {"f%d" % i: tuple("f%d_%d" % (i, g) for g in range(4)) for i in range(5)}


class Rec:
    def __init__(self, nc, stack):
        self.nc = nc
        self.stack = stack
        self.ops = []
        self.writers = {}
        self.readers = {}
        self.dma_keys = {}

    def sb(self, name, shape, dt):
        return self.stack.enter_context(self.nc.sbuf_tensor("sb_" + name, list(shape), dt))

    def ps(self, name, shape, dt):
        return self.stack.enter_context(self.nc.psum_tensor("ps_" + name, list(shape), dt))

    def op(self, eng, fn, reads=(), writes=(), dma_key=None, dma_mode="serial"):
        idx = len(self.ops)
        deps = set()
        reads = [k for b in reads for k in EXPAND.get(b, (b,))]
        writes = [k for b in writes for k in EXPAND.get(b, (b,))]
        excl = [b for b in reads if b[0] == "P" or b == "TB"]
        if excl:
            reads = [b for b in reads if b not in excl]
            writes = list(writes) + [b for b in excl if b not in writes]
        for b in reads:
            deps.update(self.writers.get(b, {}).values())
        for b in writes:
            deps.update(self.writers.get(b, {}).values())
            deps.update(self.readers.get(b, ()))
        wk = eng if dma_key is None else ("dma", dma_key)
        for b in writes:
            self.writers.setdefault(b, {})[wk] = idx
            self.readers[b] = []
        for b in reads:
            self.readers.setdefault(b, []).append(idx)
        deps.discard(idx)
        seq = None
        if dma_key is not None:
            k = self.dma_keys.setdefault(dma_key, dict(mode=dma_mode, count=0))
            k["count"] += 1
            seq = k["count"]
        self.ops.append(dict(eng=eng, fn=fn, deps=deps, dma_key=dma_key, seq=seq, consumers=0, sig=None))
        return idx

    def emit(self):
        nc, ops = self.nc, self.ops

        def skip(p, o):
            return p["dma_key"] is None and o["dma_key"] is None and p["eng"] == PE and o["eng"] == PE

        for o in ops:
            for d in o["deps"]:
                if not skip(ops[d], o):
                    ops[d]["consumers"] += 1
        cnt = {e: 0 for e in COMPUTE}
        for o in ops:
            if o["dma_key"] is None and o["consumers"] > 0:
                cnt[o["eng"]] += 1
                o["sig"] = cnt[o["eng"]]
        sem = {}
        for e in COMPUTE:
            sem[e] = self.stack.enter_context(nc.semaphore("s_" + e))
        for k in self.dma_keys:
            sem[("dma", k)] = self.stack.enter_context(nc.semaphore("d_" + str(k)))
        known = {}
        for o in ops:
            me = o["eng"]
            need = {}
            for d in o["deps"]:
                p = ops[d]
                if p["dma_key"] is not None:
                    kk = ("dma", p["dma_key"])
                    info = self.dma_keys[p["dma_key"]]
                    val = 16 * (p["seq"] if info["mode"] == "serial" else info["count"])
                else:
                    if skip(p, o):
                        continue
                    kk, val = p["eng"], p["sig"]
                need[kk] = max(need.get(kk, 0), val)
            kn = known.setdefault(me, {})
            waits = []
            for kk, val in need.items():
                if kn.get(kk, 0) < val:
                    kn[kk] = val
                    waits.append((sem[kk], val))
            o["waits"] = waits
        by_eng = {e: [] for e in (PE, ACT, DVE, POOL, SP)}
        for o in ops:
            by_eng[o["eng"]].append(o)

        def run(eng_obj, lst):
            for o in lst:
                for (s, v) in o["waits"]:
                    eng_obj.wait_ge(s, v)
                ins = o["fn"](eng_obj)
                if ins is None:
                    continue
                if o["dma_key"] is not None:
                    ins.then_inc(sem[("dma", o["dma_key"])], 16)
                elif o["sig"] is not None:
                    ins.then_inc(sem[o["eng"]], 1)

        final_waits = [(sem[("dma", k)], 16 * v["count"]) for k, v in self.dma_keys.items()]

        with nc.Block() as block:
            @block.sync
            def _(e):
                run(e, by_eng[SP])
                for (s_, v_) in final_waits:
                    e.wait_ge(s_, v_)

            @block.tensor
            def _(e):
                run(e, by_eng[PE])

            @block.scalar
            def _(e):
                run(e, by_eng[ACT])

            @block.vector
            def _(e):
                run(e, by_eng[DVE])

            @block.gpsimd
            def _(e):
                run(e, by_eng[POOL])
                for (s_, v_) in final_waits:
                    e.wait_ge(s_, v_)
        return {e: len(by_eng[e]) for e in by_eng}


def build_program(n_tiles, n_layers, dbg=False):
    L = n_tiles * 128
    nc = bass.Bass("TRN2", target_bir_lowering=False)

    def din(name, shape):
        return nc.dram_tensor(name, list(shape), F32, kind="ExternalInput").ap()

    x_d = din("x", [L, D])
    cfm_d = din("cfm", [128, 8])
    w_in_d = din("w_in", [n_layers, D, NW])
    w_out_d = din("w_out", [n_layers, 2048, D])
    w_ada_d = din("w_ada", [n_layers, D, 3 * D])
    pfm_d = din("pfm", [n_layers, 128, NF])
    p8_d = din("p8", [n_layers, 8, 2])
    bgate_d = din("bgate", [n_layers, 128, D])
    wgk_d = din("wgk", [n_layers, 17, 256])
    finalg_d = din("finalg", [128, D])
    cmat_d = din("cmat", [128, 5 * 128])
    sel_d = din("sel", [8, 8 * 128])
    gqk_d = din("gqk", [128, 2 * 4 * 128])
    retsc_d = din("retsc", [128, 24])
    rot_d = din("rot", [n_tiles, 128, 1024])
    y_d = nc.dram_tensor("y", [L, D], F32, kind="ExternalOutput").ap()
    xs_d = nc.dram_tensor("xs", [L, D], F32, kind="Internal").ap()
    if dbg:
        dbg_d = nc.dram_tensor("dbg", [n_layers, n_tiles, 128, 16 * 128], F32, kind="ExternalOutput").ap()

    with ExitStack() as st:
        r = Rec(nc, st)
        sb, ps = r.sb, r.ps
        w_in = sb("w_in_sb", [128, 8, NW], BF16)
        w_out = sb("w_out_sb", [128, 8 if os.environ.get("SHRINK") else 16, D], BF16)
        xt = sb("xt", [128, D], F32)
        xn = sb("xn", [128, D], BF16)
        hT = sb("hT", [128, 8, 128], BF16)
        yT = sb("yT", [128, 16, 128], BF16)
        hq = sb("hq", [128, 4, 128], BF16)
        hg = sb("hg", [128, 4, 128], BF16)
        hgB = sb("hgB", [128, 4, 128], BF16)
        vv = sb("vv", [128, 512], BF16)
        qt = sb("qt", [128, 4, 128], BF16)
        kt = sb("kt", [128, 4, 128], BF16)
        ktT = sb("ktT", [128, 4, 128], BF16)
        pt = sb("pt", [128, 4, 128], BF16)
        gm = sb("gm", [128, 128], BF16)
        sp0 = sb("sp0", [128, 512], BF16)
        sp1 = sb("sp1", [128, 512], BF16)
        F = [sb("f%d" % i, [128, 4, 128], F32) for i in range(5)]
        ubuf = sb("ubuf", [128, 8, 131], BF16)
        xc = sb("xc", [128, 8, 128], BF16)
        S_hg = sb("S_hg", [128, 4, 128], F32)
        S_rt = sb("S_rt", [128, 4, 128], F32)
        S_sd = sb("S_sd", [128, 8, 64], F32)
        S_gl = sb("S_gl", [128, 2, 128], F32)
        cmat = sb("cmat", [128, 5, 128], BF16)
        ones32 = sb("ones32", [128, 128], F32)
        sel = sb("sel", [8, 8, 128], F32)
        gqk = sb("gqk", [128, 8, 128], BF16)
        retsc = sb("retsc", [128, 4, 6], F32)
        rot = sb("rot", [128, 2, 4, 128], F32)
        xt2 = sb("xt2", [128, D], F32)
        xts = [xt, xt2]
        pfm = sb("pfm", [128, NF], F32)
        p8 = sb("p8", [8, 2], F32)
        cfm = sb("cfm", [128, 8], F32)
        cact = sb("cact", [128, 8, 2], F32)
        gs = sb("gs", [128, 8], F32)
        sh = sb("sh", [128, 8], F32)
        lbt = sb("lbt", [128, 4, 3], F32)
        sm = sb("sm", [128, 16], F32)
        sc = sb("sc", [128, 4, 6], F32)
        negm = sb("negm", [128, 4, 2], F32)
        wgk = sb("wgk", [32, 256], BF16)
        lrT = sb("lrT", [32, 128], BF16)
        d8 = sb("d8", [8, 4, 128], F32)
        a8 = sb("a8", [8, 2], F32)
        ssdT = sb("ssdT", [128, 3, 8], F32)
        ss = sb("ss", [128, 4], F32)
        P = [ps("P%d" % i, [128, 512], F32) for i in range(6)]
        TB = ps("TB", [128, 1024], BF16)
        P7 = ps("P7", [128, 512], F32)
        ident, mask01, perm, ones_bf = (cmat[:, i, :] for i in range(4))

        def act(out, in_, func, reads, writes, scale=1.0, bias=0.0, accum=None):
            kw = {}
            if accum is not None:
                kw["accum_out"] = accum
            r.op(ACT, lambda e: e.activation(out=out, in_=in_, func=func, scale=scale, bias=bias, **kw), reads, writes)

        def tt(eng, out, in0, in1, op, reads, writes):
            r.op(eng, lambda e: e.tensor_tensor(out=out, in0=in0, in1=in1, op=op), reads, writes)

        def ts(eng, out, in0, s1, s2, op0, op1, reads, writes):
            if s2 is None:
                r.op(eng, lambda e: e.tensor_scalar(out=out, in0=in0, scalar1=s1, scalar2=None, op0=op0), reads, writes)
            else:
                r.op(eng, lambda e: e.tensor_scalar(out=out, in0=in0, scalar1=s1, scalar2=s2, op0=op0, op1=op1), reads, writes)

        def stt(out, in0, scalar, in1, op0, op1, reads, writes):
            r.op(DVE, lambda e: e.scalar_tensor_tensor(out=out, in0=in0, scalar=scalar, in1=in1, op0=op0, op1=op1), reads, writes)

        def cp(eng, out, in_, reads, writes):
            r.op(eng, lambda e: e.tensor_copy(out=out, in_=in_), reads, writes)

        def mm(out, lhsT, rhs, start, stop, reads, writes):
            r.op(PE, lambda e: e.matmul(out, lhsT=lhsT, rhs=rhs, start=start, stop=stop), reads, writes)

        def tr(out, in_, reads, writes):
            r.op(PE, lambda e: e.transpose(out=out, in_=in_, identity=ident), list(reads) + ["cmat"], writes)

        def dma(eng, out, in_, reads, writes, key, mode="serial"):
            if key[:3] in SKIP.split(","):
                return
            r.op(eng, lambda e: e.dma_start(out=out, in_=in_), reads, writes, dma_key=key, dma_mode=mode)

        def proj_fm(pbank, pname, col0, ngroups, width=128, pcol0=0):
            for g in range(ngroups):
                for k in range(8):
                    mm(pbank[0:width, pcol0 + g * 128: pcol0 + (g + 1) * 128],
                       w_in[:, k, col0 + g * width: col0 + (g + 1) * width], hT[:, k, :],
                       k == 0, k == 7, ["w_in%d" % k, "hT"], [pname])

        def proj_tm(pbank, pname, col0, n):
            for k in range(8):
                mm(pbank[:, 0:n], hT[:, k, :], w_in[:, k, col0:col0 + n], k == 0, k == 7, ["w_in%d" % k, "hT"], [pname])

        Ff = [f[:].rearrange("p a b -> p (a b)") for f in F]
        dma(SP, Ff[0], cmat_d[:, 0:512], [], ["f0"], "c1", "batch")
        dma(SP, Ff[3][:, 0:128], cmat_d[:, 512:640], [], ["f3"], "c1", "batch")
        dma(SP, Ff[1], gqk_d[:, 0:512], [], ["f1"], "c1", "batch")
        dma(SP, Ff[2], gqk_d[:, 512:1024], [], ["f2"], "c1", "batch")
        dma(SP, sel[:].rearrange("p a b -> p (a b)"), sel_d, [], ["sel"], "c1", "batch")
        dma(SP, retsc[:].rearrange("p a b -> p (a b)"), retsc_d, [], ["retsc"], "c1", "batch")
        dma(SP, cfm[:], cfm_d, [], ["cfm"], "c1", "batch")
        cp(DVE, cmat[:, 0:4, :].rearrange("p a b -> p (a b)"), Ff[0], ["f0"], ["cmat"])
        cp(DVE, cmat[:, 4, :], Ff[3][:, 0:128], ["f3"], ["cmat"])
        cp(DVE, gqk[:, 0:4, :].rearrange("p a b -> p (a b)"), Ff[1], ["f1"], ["gqk"])
        cp(DVE, gqk[:, 4:8, :].rearrange("p a b -> p (a b)"), Ff[2], ["f2"], ["gqk"])
        r.op(DVE, lambda e: e.memset(ones32[:], 1.0), [], ["ones32"])
        r.op(DVE, lambda e: e.memset(lrT[:], 1.0), [], ["lrT"])
        act(sm[:, 0:8], cfm[:], AF.Silu, ["cfm"], ["sm"])
        cp(DVE, cact[:], sm[:, 0:8].unsqueeze(2).to_broadcast([128, 8, 2]), ["sm"], ["cact"])
        wl = dict(ci=0, ring=[0, 1, 2])
        cast_engs = [ACT, DVE, POOL]

        def wload(dst_ap, src_ap, npart, width, bufname):
            ci = wl["ci"]
            j = wl["ring"][ci % len(wl["ring"])]
            sname = "f%d" % j
            st_ap = Ff[j][0:npart, 0:width]
            dma(SP, st_ap, src_ap, [], [sname], "wst%d" % j)
            eng = cast_engs[ci % 3]
            if eng == ACT:
                act(dst_ap, st_ap, AF.Copy, [sname], [bufname])
            else:
                cp(eng, dst_ap, st_ap, [sname], [bufname])
            wl["ci"] = ci + 1

        try:
          for l in range(n_layers):
              last = (l == n_layers - 1)
              xin = x_d if l == 0 else xs_d
              dma(SP, pfm[:], pfm_d[l], [], ["pfm"], "prm%d" % l, "batch")
              dma(SP, p8[:], p8_d[l], [], ["p8"], "prm%d" % l, "batch")
              dma(SP, xt2[:], bgate_d[l], [], ["xt1"], "prm%d" % l, "batch")
              w_in_v = w_in_d[l].rearrange("(k p) n -> p k n", p=128)
              w_ada_v = w_ada_d[l].rearrange("(k p) n -> p k n", p=128)
              cactbc = xt[:].rearrange("p (k n) -> p k n", k=8)
              cp(DVE, cactbc, sm[:, 0:8].unsqueeze(2).to_broadcast([128, 8, 128]), ["sm"], ["xt0"])
              wl["ring"] = [0, 1, 2]

              def ada_block(blk):
                  cg, half = blk // 2, blk % 2
                  j = 3 + (blk % 2)
                  sname = "f%d" % j
                  dma(SP, F[j][:], w_ada_v[:, half * 4:(half + 1) * 4, cg * 128:(cg + 1) * 128], [], [sname], "wst%d" % j)
                  for kk in range(4):
                      k = half * 4 + kk
                      if cg < 16:
                          mm(P7[:, cg * 2:cg * 2 + 2], F[j][:, kk, :], cact[:, k, :], k == 0, k == 7, [sname, "cact"], ["P7"])
                      else:
                          g = cg - 16
                          pb, pn = (P[0], "P0") if g < 4 else (P[1], "P1")
                          mm(pb[:, (g % 4) * 128:(g % 4 + 1) * 128], cactbc[:, k, :], F[j][:, kk, :], k == 0, k == 7,
                             ["xt0", sname], [pn])

              chunks = [(k, c0) for k in range(8) for c0 in range(0, NW, 512)]
              nblk = 0
              for i, (k, c0) in enumerate(chunks):
                  wd = min(512, NW - c0)
                  wload(w_in[:, k, c0:c0 + wd], w_in_v[:, k, c0:c0 + wd], 128, wd, "w_in%d" % k)
                  while nblk < 48 and nblk < (i + 1) * 48 // len(chunks):
                      ada_block(nblk)
                      nblk += 1
              while nblk < 48:
                  ada_block(nblk)
                  nblk += 1
              wload(wgk[0:17, :], wgk_d[l], 17, 256, "wgk")
              wl["ring"] = [0, 1, 2, 3, 4]
              chk("A")
              p7v = P7[:, 0:32].rearrange("p (a b) -> p a b", b=2)
              tt(DVE, sh[:], p7v[:, 0:8, 0], pfm[:, 8:16], ALU.add, ["P7", "pfm"], ["sh"])
              tt(DVE, gs[:], p7v[:, 8:16, 0], pfm[:, 16:24], ALU.add, ["P7", "pfm"], ["gs"])
              stt(gs[:], gs[:], 1.0, pfm[:, 0:8], ALU.add, ALU.mult, ["gs", "pfm"], ["gs"])
              tt(DVE, xt2[:, 0:512], P[0][:], xt2[:, 0:512], ALU.add, ["P0", "xt1"], ["xt1"])
              tt(DVE, xt2[:, 512:1024], P[1][:], xt2[:, 512:1024], ALU.add, ["P1", "xt1"], ["xt1"])
              w_out_v = w_out_d[l].rearrange("(k p) n -> p k n", p=128)
              for k in range(16):
                  for hf in range(2):
                      ci = wl["ci"]
                      j = ci % 5
                      sname = "f%d" % j
                      dma(SP, Ff[j], w_out_v[:, k, hf * 512:(hf + 1) * 512], [], [sname], "wst%d" % j)
                      tt(DVE if ci % 2 == 0 else POOL, w_out[:, k, hf * 512:(hf + 1) * 512], Ff[j], xt2[:, hf * 512:(hf + 1) * 512],
                         ALU.mult, [sname, "xt1"], ["w_out%d" % k])
                      wl["ci"] = ci + 1
              if l == 0:
                  r.op(DVE, lambda e: e.memset(lbt[:, :, 0], 0.0), [], ["lbt"])
              else:
                  act(sm[:, 8:16], pfm[:, 24:32], AF.Exp, ["pfm"], ["sm"])
                  tt(DVE, sm[:, 8:12], sm[:, 8:12], sm[:, 12:16], ALU.add, ["sm"], ["sm"])
                  r.op(DVE, lambda e: e.reciprocal(out=sm[:, 8:12], in_=sm[:, 8:12]), ["sm"], ["sm"])
                  tt(DVE, lbt[:, :, 0], sm[:, 8:12], sm[:, 12:16], ALU.mult, ["sm"], ["lbt"])
              ts(DVE, lbt[:, :, 1], lbt[:, :, 0], -1.0, 1.0, ALU.mult, ALU.add, ["lbt"], ["lbt"])
              ts(DVE, lbt[:, :, 2], lbt[:, :, 1], -1.0, None, ALU.mult, None, ["lbt"], ["lbt"])
              act(a8[:, 0:1], p8[:, 1:2], AF.Exp, ["p8"], ["a8"])
              ts(DVE, a8[:, 1:2], a8[:, 0:1], -1.0, None, ALU.mult, None, ["a8"], ["a8"])
              for (S, nm) in ((S_hg, "S_hg"), (S_rt, "S_rt"), (S_sd, "S_sd"), (S_gl, "S_gl")):
                  r.op(POOL, lambda e, S=S: e.memset(S[:], 0.0), [], [nm])
              r.op(POOL, lambda e: e.memset(ubuf[:, :, 0:3], 0.0), [], ["ubuf"])

              chk("B")
              for ti in range(n_tiles):
                  t0 = ti * 128
                  X = xts[ti % 2]
                  xk = "xt%d" % (ti % 2)

                  def load_x(tj):
                      dma(SP, xts[tj % 2][:], xin[tj * 128:(tj + 1) * 128, :], ["xs%d" % tj] if l > 0 else [], ["xt%d" % (tj % 2)],
                          "xin%d" % (tj % 2))

                  def load_rot(tj):
                      dma(SP, rot[:].rearrange("p a h b -> p (a h b)"), rot_d[tj], [], ["rot"], "rot")

                  if ti == 0:
                      load_x(0)
                      load_rot(0)
                  act(xn[:], X[:], AF.Square, [xk], ["xn", "ss"], accum=ss[:, 0:1])
                  act(ss[:, 1:2], ss[:, 0:1], AF.Ln, ["ss"], ["ss"], scale=1.0 / D, bias=EPS)
                  act(ss[:, 2:3], ss[:, 1:2], AF.Exp, ["ss"], ["ss"], scale=-0.5)
                  ts(DVE, xn[:], X[:], ss[:, 2:3], None, ALU.mult, None, [xk, "ss"], ["xn"])
                  if ti + 1 < n_tiles:
                      load_x(ti + 1)
                  for k in range(8):
                      tr(TB[:, k * 128:(k + 1) * 128], xn[:, k * 128:(k + 1) * 128], ["xn"], ["TB"])
                  for k in range(8):
                      ts(DVE, hT[:, k, :], TB[:, k * 128:(k + 1) * 128], gs[:, k:k + 1], sh[:, k:k + 1], ALU.mult, ALU.add,
                         ["TB", "gs", "sh"], ["hT"])

                  chk("C")
                  def core(groups, heads, S, Sn, scv, scn, vcol, yslot, og_col, norm_div, qz=None, qs=1.0, gb=None, gbn="hg"):
                      nh = len(heads)
                      gb = hg if gb is None else gb
                      for h in range(4):
                          ts(POOL, gb[:, h, :], gb[:, h, :], pfm[:, og_col + h: og_col + h + 1], 1.0, ALU.mult, ALU.mult, [gbn, "pfm"], [gbn])
                      for g in range(groups):
                          tr(TB[:, g * 128:(g + 1) * 128], kt[:, g, :], ["kt"], ["TB"])
                      act(ktT[:, 0:groups, :], TB[:, 0:groups * 128].rearrange("p (a b) -> p a b", b=128), AF.Copy, ["TB"], ["ktT"])
                      for hi, (g, po, dk) in enumerate(heads):
                          if qz is None:
                              mm(P[3][:, hi * 128:(hi + 1) * 128], kt[po:po + dk, g, :], qt[po:po + dk, g, :], True, True,
                                 ["kt", "qt"], ["P3"])
                          else:
                              mm(P[3][:, hi * 128:(hi + 1) * 128], kt[:, g, :], qz[:, hi, :], True, True, ["kt", "hq"], ["P3"])
                      maskt = mask01 if qs == 1.0 else cmat[:, 4, :]
                      if os.environ.get("NEWPT"):
                          stt(pt[:, 0:nh, :], P[3][:, 0:nh * 128].rearrange("p (a b) -> p a b", b=128), 1e25,
                              maskt.unsqueeze(1).to_broadcast([128, nh, 128]), ALU.min, ALU.mult, ["P3", "cmat"], ["pt"])
                      else:
                          ts(DVE, pt[:, 0:nh, :], P[3][:, 0:nh * 128].rearrange("p (a b) -> p a b", b=128), 1e25, -1e25,
                             ALU.min, ALU.max, ["P3"], ["pt"])
                          tt(DVE, pt[:, 0:nh, :], pt[:, 0:nh, :],
                             maskt.unsqueeze(1).to_broadcast([128, nh, 128]), ALU.mult, ["pt", "cmat"], ["pt"])
                      dv = 128
                      for g in range(groups):
                          ts(POOL, sp0[:, g * dv:(g + 1) * dv], S[:, g, :], scv[:, g, 0:1], qs, ALU.mult, ALU.mult, [Sn, scn], ["sp0"])
                      for c in range(2):
                          for hi, (g, po, dk) in enumerate(heads):
                              mm(P[5][po:po + dk, g * 128:(g + 1) * 128], ktT[c * 64:(c + 1) * 64, g, po:po + dk],
                                 vv[c * 64:(c + 1) * 64, vcol + hi * 128: vcol + (hi + 1) * 128], True, True, ["ktT", "vv"], ["P5"])
                          for g in range(groups):
                              act(F[3][:, g, :], P[5][:, g * 128:(g + 1) * 128], AF.Copy, ["P5", scn], ["f3_%d" % g],
                                  scale=scv[:, g, 4 + c:5 + c])
                          for g in range(groups):
                              stt(S[:, g, :], S[:, g, :], scv[:, g, 2 + c:3 + c], F[3][:, g, :], ALU.mult, ALU.add,
                                  [Sn, scn, "f3_%d" % g], [Sn])
                          if c == 0:
                              for g in range(groups):
                                  ts(POOL, sp1[:, g * dv:(g + 1) * dv], S[:, g, :], scv[:, g, 1:2], qs, ALU.mult, ALU.mult, [Sn, scn], ["sp1"])
                              for hi, (g, po, dk) in enumerate(heads):
                                  o = P[4][:, hi * 128:(hi + 1) * 128]
                                  mm(o, vv[:, vcol + hi * 128: vcol + (hi + 1) * 128], pt[:, hi, :], True, False, ["vv", "pt"], ["P4"])
                                  if qz is None:
                                      mm(o[:, 0:64], sp0[po:po + dk, g * dv:(g + 1) * dv], qt[po:po + dk, g, 0:64], False, False,
                                         ["sp0", "qt"], ["P4"])
                                      mm(o[:, 64:128], sp1[po:po + dk, g * dv:(g + 1) * dv], qt[po:po + dk, g, 64:128], False, True,
                                         ["sp1", "qt"], ["P4"])
                                  else:
                                      mm(o[:, 0:64], sp0[:, g * dv:(g + 1) * dv], qz[:, hi, 0:64], False, False, ["sp0", "hq"], ["P4"])
                                      mm(o[:, 64:128], sp1[:, g * dv:(g + 1) * dv], qz[:, hi, 64:128], False, True, ["sp1", "hq"], ["P4"])
                      act(xn[:, 0:512], P[4][:], AF.Square, ["P4"], ["xn"])
                      mm(P7[:, 0:512], ones_bf, xn[:, 0:512], True, True, ["xn", "cmat"], ["P7"])
                      act(F[0][:].rearrange("p a b -> p (a b)"), P7[:], AF.Ln, ["P7"], ["f0"], scale=1.0 / norm_div, bias=EPS)
                      act(F[0][:].rearrange("p a b -> p (a b)"), F[0][:].rearrange("p a b -> p (a b)"), AF.Exp, ["f0"], ["f0"], scale=-0.5)
                      tt(DVE, yT[:, yslot:yslot + 4, :].rearrange("p a b -> p (a b)"), P[4][:], F[0][:].rearrange("p a b -> p (a b)"), ALU.mult,
                         ["P4", "f0"], ["yT"])
                      tt(DVE, yT[:, yslot:yslot + 4, :], yT[:, yslot:yslot + 4, :], gb[:], ALU.mult, ["yT", gbn], ["yT"])

                  def vec_decay_prep(ngr, logf_buf, logf_name):
                      for g in range(ngr):
                          r.op(DVE, lambda e, g=g: e.tensor_tensor_scan(out=F[4][:, g, :], data0=ones32[:], data1=logf_buf[:, g, :],
                                                                         initial=0.0, op0=ALU.mult, op1=ALU.add),
                               [logf_name, "ones32"], ["f4"])
                      cum = F[4]
                      ts(DVE, negm[:, 0:ngr, :], cum[:, 0:ngr, 31:128:64], -1.0, -QL, ALU.mult, ALU.add, ["f4"], ["negm"])
                      for g in range(ngr):
                          for c in range(2):
                              act(F[2][:, g, c * 64:(c + 1) * 64], cum[:, g, c * 64:(c + 1) * 64], AF.Exp, ["f4", "negm"], ["f2"],
                                  bias=negm[:, g, c:c + 1])
                      for g in range(ngr):
                          for c in range(2):
                              act(F[3][:, g, c * 64:(c + 1) * 64], cum[:, g, c * 64:(c + 1) * 64], AF.Exp, ["f4"], ["f3_%d" % g],
                                  scale=-1.0, bias=cum[:, g, 31 + 64 * c: 32 + 64 * c])
                      cp(POOL, sc[:, 0:ngr, 0], cum[:, 0:ngr, 31], ["f4"], ["sc"])
                      tt(POOL, sc[:, 0:ngr, 1], cum[:, 0:ngr, 95], cum[:, 0:ngr, 63], ALU.subtract, ["f4"], ["sc"])
                      cp(POOL, sc[:, 0:ngr, 2], cum[:, 0:ngr, 63], ["f4"], ["sc"])
                      tt(POOL, sc[:, 0:ngr, 3], cum[:, 0:ngr, 127], cum[:, 0:ngr, 63], ALU.subtract, ["f4"], ["sc"])
                      tt(POOL, sc[:, 0:ngr, 4], cum[:, 0:ngr, 63], cum[:, 0:ngr, 31], ALU.subtract, ["f4"], ["sc"])
                      tt(POOL, sc[:, 0:ngr, 5], cum[:, 0:ngr, 127], cum[:, 0:ngr, 95], ALU.subtract, ["f4"], ["sc"])
                      act(sc[:, 0:ngr, :], sc[:, 0:ngr, :], AF.Exp, ["sc"], ["sc"])

                  flat = lambda t: t[:].rearrange("p a b -> p (a b)")

                  r.op(POOL, lambda e: e.memset(yT[:], 0.0), [], ["yT"]) if BR != "hrgs" else None
                  def br_h(before_core=None):
                      proj_fm(P[0], "P0", C_AQ, 4)
                      act(flat(hq), P[0][:], AF.Silu, ["P0"], ["hq"])
                      proj_fm(P[2], "P2", C_AG, 4)
                      act(flat(hg), P[2][:], AF.Silu, ["P2"], ["hg"])
                      if GROUPG:
                          proj_fm(P[2], "P2", C_RG, 4)
                          act(flat(hgB), P[2][:], AF.Silu, ["P2"], ["hgB"])
                      proj_fm(P[1], "P1", C_AF, 4)
                      act(flat(F[0]), P[1][:], AF.Exp, ["P1"], ["f0"], scale=-1.0)
                      act(flat(F[0]), flat(F[0]), AF.Ln, ["f0"], ["f0"], bias=1.0)
                      act(flat(F[1]), flat(F[0]), AF.Exp, ["f0"], ["f1"], scale=-1.0)
                      for h in range(4):
                          act(F[0][:, h, :], F[1][:, h, :], AF.Ln, ["f1_%d" % h, "lbt"], ["f0_%d" % h], scale=lbt[:, h, 1:2], bias=lbt[:, h, 0:1])
                          ts(POOL, F[1][:, h, :], F[1][:, h, :], lbt[:, h, 2:3], lbt[:, h, 1:2], ALU.mult, ALU.add, ["f1_%d" % h, "lbt"], ["f1_%d" % h])
                      vec_decay_prep(4, F[0], "f0")
                      tt(POOL, qt[:], hq[:], F[2][:], ALU.mult, ["hq", "f2"], ["qt"])
                      tt(DVE, kt[:], F[1][:], F[3][:], ALU.mult, ["f1", "f3"], ["kt"])
                      proj_tm(P[0], "P0", C_AI, 512)
                      act(vv[:], P[0][:], AF.Copy, ["P0"], ["vv"])
                      if before_core is not None:
                          before_core()
                      core(4, [(h, 0, 128) for h in range(4)], S_hg, "S_hg", sc, "sc", 0, 0, 32, 128.0, qs=QS)

                  def pre_r():
                      proj_fm(P[1], "P1", C_RQ, 4)
                      proj_fm(P[2], "P2", C_RG, 4)
                      proj_tm(P[0], "P0", C_RV, 512)

                  def pre_g():
                      proj_fm(P[0], "P0", C_GQ, 2)
                      proj_fm(P[2], "P2", C_GG, 4)

                  def pre_s():
                      proj_fm(P[0], "P0", C_MX, 4)
                      proj_fm(P[1], "P1", C_MX + 512, 4)
                      proj_fm(P[2], "P2", C_MZ, 4)

                  PIPE = (BR == "hrgs") and bool(os.environ.get("PIPE"))
                  if "h" in BR and not PIPE:
                      br_h()
                  def br_r(before_core=None, pre=False):
                      for (ccol, tab, dst, dn) in ((C_RQ, 0, qt, "qt"), (C_RK, 4, kt, "kt")):
                          if not (pre and ccol == C_RQ):
                              proj_fm(P[1], "P1", ccol, 4)
                          act(flat(hq), P[1][:], AF.Copy, ["P1"], ["hq"])
                          chk("Ra")
                          mm(P7[:, 0:512], perm, flat(hq), True, True, ["hq", "cmat"], ["P7"])
                          chk("Rb")
                          p1v = P[1][:].rearrange("p (a b) -> p a b", b=128)
                          p7v2 = P7[:].rearrange("p (a b) -> p a b", b=128)
                          RT = os.environ.get("RTEST", "")
                          if RT == "1":
                              tt(DVE, F[0][:], p1v, F[2][:], ALU.mult, ["P1", "f2"], ["f0"])
                          elif RT == "2":
                              cp(DVE, F[0][:], p1v, ["P1"], ["f0"])
                          elif RT == "3":
                              tt(DVE, F[0][:], F[2][:], rot[:, 0, :, :], ALU.mult, ["f2", "rot"], ["f0"])
                          else:
                              tt(DVE, F[0][:], p1v, rot[:, 0, :, :], ALU.mult, ["P1", "rot"], ["f0"])
                          chk("Rc1")
                          tt(DVE, F[1][:], p7v2, rot[:, 1, :, :], ALU.mult, ["P7", "rot"], ["f1"])
                          chk("Rc")
                          tt(POOL, F[0][:], F[0][:], F[1][:], ALU.add, ["f0", "f1"], ["f0"])
                          chk("Rd")
                          tt(POOL, dst[:], F[0][:], gqk[:, tab:tab + 4, :], ALU.mult, ["f0", "gqk"], [dn])
                      if not GROUPG:
                          if not pre:
                              proj_fm(P[2], "P2", C_RG, 4)
                          act(flat(hg), P[2][:], AF.Silu, ["P2"], ["hg"])
                      if not pre:
                          proj_tm(P[0], "P0", C_RV, 512)
                      act(vv[:], P[0][:], AF.Copy, ["P0"], ["vv"])
                      chk("R1")
                      if before_core is not None:
                          before_core()
                      if GROUPG:
                          core(4, [(h, 0, 128) for h in range(4)], S_rt, "S_rt", retsc, "retsc", 0, 4, 36, 128.0, gb=hgB, gbn="hgB")
                      else:
                          core(4, [(h, 0, 128) for h in range(4)], S_rt, "S_rt", retsc, "retsc", 0, 4, 36, 128.0)

                  if "r" in BR and not PIPE:
                      br_r()
                  if not PIPE and ti + 1 < n_tiles:
                      load_rot(ti + 1)
                  def br_g(before_core=None, pre=False):
                      proj_fm(P7, "P7", C_LR, 1, width=16)
                      cp(DVE, lrT[0:16, :], P7[0:16, 0:128], ["P7"], ["lrT"])
                      for g in range(2):
                          mm(P[1][:, g * 128:(g + 1) * 128], wgk[0:17, g * 128:(g + 1) * 128], lrT[0:17, :], True, True, ["wgk", "lrT"], ["P1"])
                      act(F[0][:, 0:2, :], P[1][:, 0:256].rearrange("p (a b) -> p a b", b=128), AF.Exp, ["P1"], ["f0"], scale=-1.0)
                      act(F[0][:, 0:2, :], F[0][:, 0:2, :], AF.Ln, ["f0"], ["f0"], bias=1.0)
                      ts(DVE, F[0][:, 0:2, :], F[0][:, 0:2, :], -1.0 / 16.0, None, ALU.mult, None, ["f0"], ["f0"])
                      vec_decay_prep(2, F[0], "f0")
                      if not pre:
                          proj_fm(P[0], "P0", C_GQ, 2)
                      stt(qt[:, 0:2, :], P[0][:, 0:256].rearrange("p (a b) -> p a b", b=128), 0.125, F[2][:, 0:2, :], ALU.mult, ALU.mult,
                          ["P0", "f2"], ["qt"])
                      proj_fm(P[1], "P1", C_GK, 2)
                      tt(DVE, kt[:, 0:2, :], P[1][:, 0:256].rearrange("p (a b) -> p a b", b=128), F[3][:, 0:2, :], ALU.mult, ["P1", "f3"], ["kt"])
                      if not pre:
                          proj_fm(P[2], "P2", C_GG, 4)
                      act(flat(hg), P[2][:], AF.Silu, ["P2"], ["hg"])
                      proj_tm(P[0], "P0", C_GV, 512)
                      act(vv[:], P[0][:], AF.Copy, ["P0"], ["vv"])
                      r.op(POOL, lambda e: e.memset(hq[:], 0.0), [], ["hq"])
                      for hi, (g, po) in enumerate(((0, 0), (0, 64), (1, 0), (1, 64))):
                          cp(POOL, hq[po:po + 64, hi, :], qt[po:po + 64, g, :], ["qt"], ["hq"])
                      if before_core is not None:
                          before_core()
                      core(2, [(0, 0, 64), (0, 64, 64), (1, 0, 64), (1, 64, 64)], S_gl, "S_gl", sc, "sc", 0, 12, 44, 128.0, qz=hq, qs=QS)

                  if "g" in BR and not PIPE:
                      br_g()
                  def br_s(pre=False):
                      for half in range(2):
                          if not pre:
                              proj_fm(P[half], "P%d" % half, C_MX + half * 512, 4)
                      cp(DVE, ubuf[:, 0:4, 3:131], P[0][:].rearrange("p (a b) -> p a b", b=128), ["P0"], ["ubuf"])
                      act(ubuf[:, 4:8, 3:131], P[1][:].rearrange("p (a b) -> p a b", b=128), AF.Copy, ["P1"], ["ubuf"])
                      cacc = [F[0], F[1]]
                      for g in range(8):
                          ca = cacc[g // 4][:, g % 4, :]
                          cn = "f%d_%d" % (g // 4, g % 4)
                          ts(POOL, ca, ubuf[:, g, 0:128], pfm[:, 56 + g * 4: 57 + g * 4], pfm[:, 48 + g: 49 + g], ALU.mult, ALU.add,
                             ["ubuf", "pfm"], [cn])
                          for j in range(1, 4):
                              stt(ca, ubuf[:, g, j:j + 128], pfm[:, 56 + g * 4 + j: 57 + g * 4 + j], ca, ALU.mult, ALU.add,
                                  ["ubuf", "pfm", cn], [cn])
                      cp(POOL, ubuf[:, :, 0:3], ubuf[:, :, 128:131], ["f0", "f1", "ubuf"], ["ubuf"])
                      act(xc[:, 0:4, :], F[0][:], AF.Silu, ["f0"], ["xc"])
                      act(xc[:, 4:8, :], F[1][:], AF.Silu, ["f1"], ["xc"])
                      if not pre:
                          proj_fm(P[2], "P2", C_MZ, 4)
                      act(flat(hg), P[2][:], AF.Silu, ["P2"], ["hg"])
                      proj_fm(P7, "P7", C_DT, 1, width=8)
                      act(d8[:, 0, :], P7[0:8, 0:128], AF.Exp, ["P7", "p8"], ["d8"], bias=p8[:, 0:1])
                      act(d8[:, 1, :], d8[:, 0, :], AF.Ln, ["d8"], ["d8"], bias=1.0)
                      ts(DVE, d8[:, 0, :], d8[:, 1, :], a8[:, 1:2], None, ALU.mult, None, ["d8", "a8"], ["d8"])
                      for c in range(2):
                          r.op(DVE, lambda e, c=c: e.tensor_tensor_scan(out=d8[:, 2, c * 64:(c + 1) * 64], data0=ones32[0:8, 0:64],
                                                                         data1=d8[:, 0, c * 64:(c + 1) * 64], initial=0.0,
                                                                         op0=ALU.mult, op1=ALU.add), ["d8", "ones32"], ["d8"])
                      for c in range(2):
                          act(d8[:, 3, c * 64:(c + 1) * 64], d8[:, 2, c * 64:(c + 1) * 64], AF.Exp, ["d8"], ["d8"], scale=-1.0,
                              bias=d8[:, 2, c * 64 + 63: c * 64 + 64])
                      tt(DVE, d8[:, 3, :], d8[:, 3, :], d8[:, 1, :], ALU.mult, ["d8"], ["d8"])
                      for i, row in enumerate((1, 3, 2)):
                          r.op(PE, lambda e, i=i, row=row: e.matmul(P7[:, 256 + i * 8: 256 + (i + 1) * 8], lhsT=d8[:, row, :],
                                                                    rhs=sel[:, :, 0], start=True, stop=True), ["d8", "sel"], ["P7"])
                      cp(DVE, ssdT[:].rearrange("p a b -> p (a b)"), P7[:, 256:280], ["P7"], ["ssdT"])
                      for g in range(4):
                          tr(TB[:, g * 128:(g + 1) * 128], xc[:, g, :], ["xc"], ["TB"])
                      tbv = TB[:, 0:512].rearrange("p (a b) -> p a b", b=64)
                      tt(DVE, vv[:].rearrange("p (a b) -> p a b", b=64), tbv, ssdT[:, 0, :].unsqueeze(2).to_broadcast([128, 8, 64]), ALU.mult,
                         ["TB", "ssdT"], ["vv"])
                      tt(DVE, flat(hq).rearrange("p (a b) -> p a b", b=64), tbv, ssdT[:, 1, :].unsqueeze(2).to_broadcast([128, 8, 64]), ALU.mult,
                         ["TB", "ssdT"], ["hq"])
                      v2 = flat(hq)
                      for gi in range(2):
                          tr(TB[:, 512 + gi * 128: 512 + (gi + 1) * 128], xc[:, 4 + gi, :], ["xc"], ["TB"])
                      act(ktT[:, 0:2, :], TB[:, 512:768].rearrange("p (a b) -> p a b", b=128), AF.Copy, ["TB"], ["ktT"])
                      cp(POOL, sp0[:], S_sd[:].rearrange("p a b -> p (a b)"), ["S_sd"], ["sp0"])
                      for gi in range(2):
                          for hh in range(4):
                              h = gi * 4 + hh
                              mm(P7[:, hh * 128:(hh + 1) * 128], sel[:, h, :], d8[:, 2, :], True, True, ["sel", "d8"], ["P7"])
                          for hh in range(4):
                              h = gi * 4 + hh
                              ts(DVE, F[2][:, hh, :], P7[:, hh * 128:(hh + 1) * 128], ssdT[:, 2, h:h + 1], 0.0, ALU.subtract, ALU.min,
                                 ["P7", "ssdT"], ["f2"])
                          act(flat(kt), flat(F[2]), AF.Exp, ["f2"], ["kt"])
                          act(flat(F[3]), P7[:], AF.Exp, ["P7"], ["f3"])
                          mm(P[3][:, 0:128], xc[:, 4 + gi, :], xc[:, 6 + gi, :], True, True, ["xc"], ["P3"])
                          tt(DVE, gm[:], P[3][:, 0:128], mask01, ALU.mult, ["P3", "cmat"], ["gm"])
                          tt(POOL, pt[:], kt[:], gm[:].unsqueeze(1).to_broadcast([128, 4, 128]), ALU.mult, ["kt", "gm"], ["pt"])
                          tt(POOL, qt[:], F[3][:], xc[:, 6 + gi, :].unsqueeze(1).to_broadcast([128, 4, 128]), ALU.mult, ["f3", "xc"], ["qt"])
                          for c in range(2):
                              mm(P[5][:, 0:256], ktT[c * 64:(c + 1) * 64, gi, :], v2[c * 64:(c + 1) * 64, gi * 256:(gi + 1) * 256], True, True,
                                 ["ktT", "hq"], ["P5"])
                              for hh in range(4):
                                  h = gi * 4 + hh
                                  stt(S_sd[:, h, :], S_sd[:, h, :], F[3][:, hh, c * 64 + 63: c * 64 + 64], P[5][:, hh * 64:(hh + 1) * 64],
                                      ALU.mult, ALU.add, ["S_sd", "f3", "P5"], ["S_sd"])
                              if c == 0:
                                  cp(POOL, sp1[:, gi * 256:(gi + 1) * 256], S_sd[:, gi * 4:(gi + 1) * 4, :].rearrange("p a b -> p (a b)"),
                                     ["S_sd"], ["sp1"])
                                  for hh in range(4):
                                      h = gi * 4 + hh
                                      o = P[4][(h % 2) * 64:(h % 2) * 64 + 64, (h // 2) * 128:(h // 2 + 1) * 128]
                                      mm(o, vv[:, h * 64:(h + 1) * 64], pt[:, hh, :], True, False, ["vv", "pt"], ["P4"])
                                      mm(o[:, 0:64], sp0[:, h * 64:(h + 1) * 64], qt[:, hh, 0:64], False, False, ["sp0", "qt"], ["P4"])
                                      mm(o[:, 64:128], sp1[:, h * 64:(h + 1) * 64], qt[:, hh, 64:128], False, True, ["sp1", "qt"], ["P4"])
                      for g in range(4):
                          stt(F[0][:, g, :], xc[:, g, :], pfm[:, 88 + g: 89 + g], P[4][:, g * 128:(g + 1) * 128], ALU.mult, ALU.add,
                              ["xc", "pfm", "P4"], ["f0"])
                      tt(POOL, F[0][:], F[0][:], hg[:], ALU.mult, ["f0", "hg"], ["f0"])
                      act(xn[:, 0:512], flat(F[0]), AF.Square, ["f0"], ["xn"])
                      for gi in range(2):
                          for j in range(2):
                              mm(P7[:, gi * 128:(gi + 1) * 128], ones_bf, xn[:, (2 * gi + j) * 128:(2 * gi + j + 1) * 128], j == 0, j == 1,
                                 ["xn", "cmat"], ["P7"])
                      act(F[1][:, 0:2, :], P7[:, 0:256].rearrange("p (a b) -> p a b", b=128), AF.Ln, ["P7"], ["f1"], scale=1.0 / 256.0, bias=EPS)
                      act(F[1][:, 0:2, :], F[1][:, 0:2, :], AF.Exp, ["f1"], ["f1"], scale=-0.5)
                      for g in range(4):
                          stt(yT[:, 8 + g, :], F[0][:, g, :], pfm[:, 40 + g: 41 + g], F[1][:, g // 2, :], ALU.mult, ALU.mult,
                              ["f0", "pfm", "f1"], ["yT"])

                  if "s" in BR and not PIPE:
                      br_s()
                  if PIPE:
                      br_h(before_core=pre_r)
                      br_r(before_core=pre_g, pre=True)
                      if ti + 1 < n_tiles:
                          load_rot(ti + 1)
                      br_g(before_core=pre_s, pre=True)
                      br_s(pre=True)
                  if dbg:
                      DUMP = os.environ.get("DUMP", "")
                      if DUMP:
                          bufs = dict(qt=qt, kt=kt, hq=hq, hg=hg, pt=pt, ktT=ktT)
                          for hf, nm in enumerate(DUMP.split(",")):
                              if nm.startswith("f"):
                                  j = int(nm[1:])
                                  dma(SP, dbg_d[l, ti, :, hf * 512:(hf + 1) * 512], Ff[j], [nm], ["dbgout"], "dbg")
                              elif nm in ("vv", "sp0", "sp1"):
                                  src = dict(vv=vv, sp0=sp0, sp1=sp1)[nm]
                                  cp(POOL, Ff[hf], src[:], [nm], ["f%d" % hf])
                                  dma(SP, dbg_d[l, ti, :, hf * 512:(hf + 1) * 512], Ff[hf], ["f%d" % hf], ["dbgout"], "dbg")
                              elif nm.startswith("S"):
                                  src = dict(S_hg=S_hg, S_rt=S_rt)[nm]
                                  dma(SP, dbg_d[l, ti, :, hf * 512:(hf + 1) * 512], src[:].rearrange("p a b -> p (a b)"), [nm], ["dbgout"], "dbg")
                              else:
                                  cp(POOL, F[hf][:], bufs[nm][:], [nm], ["f%d" % hf])
                                  dma(SP, dbg_d[l, ti, :, hf * 512:(hf + 1) * 512], Ff[hf], ["f%d" % hf], ["dbgout"], "dbg")
                      else:
                          for hf in range(4):
                              cp(POOL, F[2][:], yT[:, hf * 4:(hf + 1) * 4, :], ["yT"], ["f2"])
                              dma(SP, dbg_d[l, ti, :, hf * 512:(hf + 1) * 512], flat(F[2]), ["f2"], ["dbgout"], "dbg")

                  if last:
                      for half in range(2):
                          dma(SP, flat(F[2 + half]), finalg_d[:, half * 512:(half + 1) * 512], [], ["f%d" % (2 + half)], "fg%d" % half)
                  for half in range(2):
                      for kc in range(16):
                          mm(P[half][:], yT[:, kc, :], w_out[:, kc, half * 512:(half + 1) * 512], kc == 0, kc == 15,
                             ["yT", "w_out%d" % kc], ["P%d" % half])
                      tt(DVE, X[:, half * 512:(half + 1) * 512], P[half][:], X[:, half * 512:(half + 1) * 512], ALU.add,
                         ["P%d" % half, xk], [xk])
                  if last:
                      act(xn[:], X[:], AF.Square, [xk], ["xn", "ss"], accum=ss[:, 0:1])
                      act(ss[:, 1:2], ss[:, 0:1], AF.Ln, ["ss"], ["ss"], scale=1.0 / D, bias=EPS)
                      act(ss[:, 2:3], ss[:, 1:2], AF.Exp, ["ss"], ["ss"], scale=-0.5)
                      for half in range(2):
                          stt(X[:, half * 512:(half + 1) * 512], X[:, half * 512:(half + 1) * 512], ss[:, 2:3], flat(F[2 + half]),
                              ALU.mult, ALU.mult, [xk, "ss", "f%d" % (2 + half)], [xk])
                      dma(SP, y_d[t0:t0 + 128, :], X[:], [xk], ["yout"], "xout%d" % (ti % 2))
                  else:
                      dma(SP, xs_d[t0:t0 + 128, :], X[:], [xk], ["xs%d" % ti], "xout%d" % (ti % 2))
        except _Stop:
            dma(SP, y_d[0:128, :], xt[:], ["xt0"], ["yout"], "xout0")
        r.op(SP, lambda e: None, ["yout"] + (["dbgout"] if dbg else []), [])
        stats = r.emit()
    return nc, stats


def _const_tables(n_tiles):
    L = n_tiles * 128
    ident = np.eye(128, dtype=np.float32)
    s = np.arange(128)[:, None]
    t = np.arange(128)[None, :]
    mask01 = ((s // 64 == t // 64) & (s <= t)).astype(np.float32)
    perm = np.zeros((128, 128), np.float32)
    perm[(np.arange(128) + 64) % 128, np.arange(128)] = 1.0
    ones = np.ones((128, 128), np.float32)
    cmat = np.concatenate([ident, mask01, perm, ones, mask01 * QS], axis=1)
    sel = np.zeros((8, 8, 128), np.float32)
    for h in range(8):
        sel[h, h, :] = 1.0
    gam = 1.0 - np.exp2(-(5.0 + np.arange(4, dtype=np.float64)))
    tp = (np.arange(128) % 64 + 1).astype(np.float64)
    gq = np.stack([gam[h] ** tp for h in range(4)], 0)
    gk = np.stack([gam[h] ** (-tp) * (128.0 ** -0.5) for h in range(4)], 0)
    gqk = np.concatenate([np.broadcast_to(gq[None], (128, 4, 128)), np.broadcast_to(gk[None], (128, 4, 128))], axis=1)
    gqk = np.ascontiguousarray(gqk, dtype=np.float32).reshape(128, 1024)
    retsc = np.zeros((128, 4, 6), np.float32)
    for h in range(4):
        g64 = gam[h] ** 64
        retsc[:, h, :] = [1.0, 1.0, g64, g64, g64, g64]
    retsc = retsc.reshape(128, 24)
    inv_freq = (10000.0 ** (-np.arange(0, 128, 2, dtype=np.float32) / 128)).astype(np.float32)
    ang = np.arange(L, dtype=np.float32)[:, None] * inv_freq[None, :]
    cos = np.cos(ang).astype(np.float32).T
    sin = np.sin(ang).astype(np.float32).T
    cosf = np.concatenate([cos, cos], 0)
    sinf = np.concatenate([-sin, sin], 0)
    rot = np.stack([cosf.reshape(128, n_tiles, 128), sinf.reshape(128, n_tiles, 128)], axis=2)
    rot = np.broadcast_to(rot.transpose(1, 0, 2, 3)[:, :, :, None, :], (n_tiles, 128, 2, 4, 128))
    rot = np.ascontiguousarray(rot).reshape(n_tiles, 128, 1024)
    return dict(cmat=cmat, sel=sel.reshape(8, 1024), gqk=gqk, retsc=retsc, rot=rot)


def _fm(v, n):
    return np.ascontiguousarray(np.asarray(v, np.float32).reshape(n, 128).T)


def _pack_params(inp, n_layers):
    pfm = np.zeros((n_layers, 128, NF), np.float32)
    p8 = np.zeros((n_layers, 8, 2), np.float32)
    bgate = np.zeros((n_layers, 128, D), np.float32)
    wgk = np.zeros((n_layers, 17, 256), np.float32)
    for l in range(n_layers):
        pfm[l, :, 0:8] = _fm(inp["norm_g"][l], 8)
        pfm[l, :, 8:16] = _fm(inp["b_ada"][l][0:D], 8)
        pfm[l, :, 16:24] = _fm(inp["b_ada"][l][D:2 * D], 8)
        pfm[l, :, 24:28] = _fm(inp["hgrn_lb_logits"][0], 4)
        pfm[l, :, 28:32] = _fm(inp["hgrn_lb_logits"][min(1, inp["hgrn_lb_logits"].shape[0] - 1)], 4)
        pfm[l, :, 32:36] = _fm(inp["hgrn_onorm_g"][l], 4)
        pfm[l, :, 36:40] = _fm(inp["ret_onorm_g"][l], 4)
        pfm[l, :, 40:44] = _fm(inp["ssm_norm_g"][l], 4)
        pfm[l, :, 44:48] = _fm(inp["gla_onorm_g"][l], 4)
        pfm[l, :, 48:56] = _fm(inp["ssm_conv_b"][l], 8)
        cw = np.asarray(inp["ssm_conv_w"][l], np.float32)
        pfm[l, :, 56:88] = cw.reshape(4, 8, 128).transpose(2, 1, 0).reshape(128, 32)
        pfm[l, :, 88:92] = _fm(np.repeat(np.asarray(inp["ssm_d"][l], np.float32), 64), 4)
        p8[l, :, 0] = inp["ssm_dt_bias"][l]
        p8[l, :, 1] = inp["ssm_a_log"][l]
        bgate[l] = np.broadcast_to(np.asarray(inp["b_ada"][l][2 * D:3 * D], np.float32)[None, :], (128, D))
        wgk[l, 0:16] = inp["gla_w_gk2"][l]
        wgk[l, 16] = inp["gla_b_gk2"][l]
    return pfm, p8, bgate, wgk


_CACHE = {}


def run(inputs, n_tiles, n_layers, core_batches, dbg=False):
    inp = {k: np.asarray(v) for k, v in inputs.items()}
    key = (n_tiles, n_layers, dbg)
    if key not in _CACHE:
        _CACHE[key] = build_program(n_tiles, n_layers, dbg)
    nc, stats = _CACHE[key]
    L = n_tiles * 128
    consts = _const_tables(n_tiles)
    pfm, p8, bgate, wgk = _pack_params(inp, n_layers)
    finalg = np.ascontiguousarray(np.broadcast_to(np.asarray(inp["final_g"], np.float32)[None, :], (128, D)))
    shared = dict(w_in=np.ascontiguousarray(inp["w_in"][:n_layers], dtype=np.float32),
                  w_out=np.ascontiguousarray(inp["w_out"][:n_layers], dtype=np.float32),
                  w_ada=np.ascontiguousarray(inp["w_ada"][:n_layers], dtype=np.float32),
                  pfm=pfm, p8=p8, bgate=bgate, wgk=wgk, finalg=finalg, **consts)
    in_maps = []
    for b in core_batches:
        m = dict(shared)
        m["x"] = np.ascontiguousarray(inp["x"][b, :L], dtype=np.float32)
        m["cfm"] = _fm(inp["c"][b], 8)
        in_maps.append(m)
    res = run_bass_kernel_spmd(nc, in_maps, core_ids=list(range(len(core_batches))))
    return res, stats


def kernel(**inputs):
    B, Lfull, _ = inputs["x"].shape
    n_tiles = Lfull // 128
    core_batches = [i % B for i in range(8)]
    res, _ = run(inputs, n_tiles, 2, core_batches)
    out = np.stack([np.asarray(res.results[b]["y"], dtype=np.float32) for b in range(B)], axis=0)
    return out
```

```python
import math
import os
BR = os.environ.get("BR", "hrgs")
STAGE = os.environ.get("STAGE", "")
SKIP = os.environ.get("SKIP", "")
GROUPG = (BR == "hrgs") and not os.environ.get("NOGROUPG")


class _Stop(Exception):
    pass


def chk(n):
    if STAGE == n:
        raise _Stop()
import numpy as np
from contextlib import ExitStack
import concourse.bass as bass
import concourse.mybir as mybir
from concourse.bass_utils import run_bass_kernel_spmd

F32 = mybir.dt.float32
BF16 = mybir.dt.bfloat16
AF = mybir.ActivationFunctionType
ALU = mybir.AluOpType
PE, ACT, DVE, POOL, SP = "pe", "act", "dve", "pool", "sp"
COMPUTE = (PE, ACT, DVE, POOL)

D = 1024
QS = float(2 ** 30)
QL = 30.0 * math.log(2.0)
NW = 7192
EPS = 1e-6
C_AQ, C_AF, C_AI, C_AG = 0, 512, 1024, 1536
C_RQ, C_RK, C_RV, C_RG = 2048, 2560, 3072, 3584
C_MZ, C_MX, C_DT = 4096, 4608, 5632
C_GQ, C_GK, C_GV, C_GG, C_LR = 5640, 5896, 6152, 6664, 7176
NF = 92


EXPAND = {"f%d" % i: tuple("f%d_%d" % (i, g) for g in range(4)) for i in range(5)}


class Rec:
    def __init__(self, nc, stack):
        self.nc = nc
        self.stack = stack
        self.ops = []
        self.writers = {}
        self.readers = {}
        self.dma_keys = {}

    def sb(self, name, shape, dt):
        return self.stack.enter_context(self.nc.sbuf_tensor("sb_" + name, list(shape), dt))

    def ps(self, name, shape, dt):
        return self.stack.enter_context(self.nc.psum_tensor("ps_" + name, list(shape), dt))

    def op(self, eng, fn, reads=(), writes=(), dma_key=None, dma_mode="serial"):
        idx = len(self.ops)
        deps = set()
        reads = [k for b in reads for k in EXPAND.get(b, (b,))]
        writes = [k for b in writes for k in EXPAND.get(b, (b,))]
        excl = [b for b in reads if b[0] == "P" or b == "TB"]
        if excl:
            reads = [b for b in reads if b not in excl]
            writes = list(writes) + [b for b in excl if b not in writes]
        for b in reads:
            deps.update(self.writers.get(b, {}).values())
        for b in writes:
            deps.update(self.writers.get(b, {}).values())
            deps.update(self.readers.get(b, ()))
        wk = eng if dma_key is None else ("dma", dma_key)
        for b in writes:
            self.writers.setdefault(b, {})[wk] = idx
            self.readers[b] = []
        for b in reads:
            self.readers.setdefault(b, []).append(idx)
        deps.discard(idx)
        seq = None
        if dma_key is not None:
            k = self.dma_keys.setdefault(dma_key, dict(mode=dma_mode, count=0))
            k["count"] += 1
            seq = k["count"]
        self.ops.append(dict(eng=eng, fn=fn, deps=deps, dma_key=dma_key, seq=seq, consumers=0, sig=None))
        return idx

    def emit(self):
        nc, ops = self.nc, self.ops

        def skip(p, o):
            return p["dma_key"] is None and o["dma_key"] is None and p["eng"] == PE and o["eng"] == PE

        for o in ops:
            for d in o["deps"]:
                if not skip(ops[d], o):
                    ops[d]["consumers"] += 1
        cnt = {e: 0 for e in COMPUTE}
        for o in ops:
            if o["dma_key"] is None and o["consumers"] > 0:
                cnt[o["eng"]] += 1
                o["sig"] = cnt[o["eng"]]
        sem = {}
        for e in COMPUTE:
            sem[e] = self.stack.enter_context(nc.semaphore("s_" + e))
        for k in self.dma_keys:
            sem[("dma", k)] = self.stack.enter_context(nc.semaphore("d_" + str(k)))
        known = {}
        for o in ops:
            me = o["eng"]
            need = {}
            for d in o["deps"]:
                p = ops[d]
                if p["dma_key"] is not None:
                    kk = ("dma", p["dma_key"])
                    info = self.dma_keys[p["dma_key"]]
                    val = 16 * (p["seq"] if info["mode"] == "serial" else info["count"])
                else:
                    if skip(p, o):
                        continue
                    kk, val = p["eng"], p["sig"]
                need[kk] = max(need.get(kk, 0), val)
            kn = known.setdefault(me, {})
            waits = []
            for kk, val in need.items():
                if kn.get(kk, 0) < val:
                    kn[kk] = val
                    waits.append((sem[kk], val))
            o["waits"] = waits
        by_eng = {e: [] for e in (PE, ACT, DVE, POOL, SP)}
        for o in ops:
            by_eng[o["eng"]].append(o)

        def run(eng_obj, lst):
            for o in lst:
                for (s, v) in o["waits"]:
                    eng_obj.wait_ge(s, v)
                ins = o["fn"](eng_obj)
                if ins is None:
                    continue
                if o["dma_key"] is not None:
                    ins.then_inc(sem[("dma", o["dma_key"])], 16)
                elif o["sig"] is not None:
                    ins.then_inc(sem[o["eng"]], 1)

        final_waits = [(sem[("dma", k)], 16 * v["count"]) for k, v in self.dma_keys.items()]

        with nc.Block() as block:
            @block.sync
            def _(e):
                run(e, by_eng[SP])
                for (s_, v_) in final_waits:
                    e.wait_ge(s_, v_)

            @block.tensor
            def _(e):
                run(e, by_eng[PE])

            @block.scalar
            def _(e):
                run(e, by_eng[ACT])

            @block.vector
            def _(e):
                run(e, by_eng[DVE])

            @block.gpsimd
            def _(e):
                run(e, by_eng[POOL])
                for (s_, v_) in final_waits:
                    e.wait_ge(s_, v_)
        return {e: len(by_eng[e]) for e in by_eng}


def build_program(n_tiles, n_layers, dbg=False):
    L = n_tiles * 128
    nc = bass.Bass("TRN2", target_bir_lowering=False)

    def din(name, shape):
        return nc.dram_tensor(name, list(shape), F32, kind="ExternalInput").ap()

    x_d = din("x", [L, D])
    cfm_d = din("cfm", [128, 8])
    w_in_d = din("w_in", [n_layers, D, NW])
    w_out_d = din("w_out", [n_layers, 2048, D])
    w_ada_d = din("w_ada", [n_layers, D, 3 * D])
    pfm_d = din("pfm", [n_layers, 128, NF])
    p8_d = din("p8", [n_layers, 8, 2])
    bgate_d = din("bgate", [n_layers, 128, D])
    wgk_d = din("wgk", [n_layers, 17, 256])
    finalg_d = din("finalg", [128, D])
    cmat_d = din("cmat", [128, 5 * 128])
    sel_d = din("sel", [8, 8 * 128])
    gqk_d = din("gqk", [128, 2 * 4 * 128])
    retsc_d = din("retsc", [128, 24])
    rot_d = din("rot", [n_tiles, 128, 1024])
    y_d = nc.dram_tensor("y", [L, D], F32, kind="ExternalOutput").ap()
    xs_d = nc.dram_tensor("xs", [L, D], F32, kind="Internal").ap()
    if dbg:
        dbg_d = nc.dram_tensor("dbg", [n_layers, n_tiles, 128, 16 * 128], F32, kind="ExternalOutput").ap()

    with ExitStack() as st:
        r = Rec(nc, st)
        sb, ps = r.sb, r.ps
        w_in = sb("w_in_sb", [128, 8, NW], BF16)
        w_out = sb("w_out_sb", [128, 8 if os.environ.get("SHRINK") else 16, D], BF16)
        xt = sb("xt", [128, D], F32)
        xn = sb("xn", [128, D], BF16)
        hT = sb("hT", [128, 8, 128], BF16)
        yT = sb("yT", [128, 16, 128], BF16)
        hq = sb("hq", [128, 4, 128], BF16)
        hg = sb("hg", [128, 4, 128], BF16)
        hgB = sb("hgB", [128, 4, 128], BF16)
        vv = sb("vv", [128, 512], BF16)
        qt = sb("qt", [128, 4, 128], BF16)
        kt = sb("kt", [128, 4, 128], BF16)
        ktT = sb("ktT", [128, 4, 128], BF16)
        pt = sb("pt", [128, 4, 128], BF16)
        gm = sb("gm", [128, 128], BF16)
        sp0 = sb("sp0", [128, 512], BF16)
        sp1 = sb("sp1", [128, 512], BF16)
        F = [sb("f%d" % i, [128, 4, 128], F32) for i in range(5)]
        ubuf = sb("ubuf", [128, 8, 131], BF16)
        xc = sb("xc", [128, 8, 128], BF16)
        S_hg = sb("S_hg", [128, 4, 128], F32)
        S_rt = sb("S_rt", [128, 4, 128], F32)
        S_sd = sb("S_sd", [128, 8, 64], F32)
        S_gl = sb("S_gl", [128, 2, 128], F32)
        cmat = sb("cmat", [128, 5, 128], BF16)
        ones32 = sb("ones32", [128, 128], F32)
        sel = sb("sel", [8, 8, 128], F32)
        gqk = sb("gqk", [128, 8, 128], BF16)
        retsc = sb("retsc", [128, 4, 6], F32)
        rot = sb("rot", [128, 2, 4, 128], F32)
        xt2 = sb("xt2", [128, D], F32)
        xts = [xt, xt2]
        pfm = sb("pfm", [128, NF], F32)
        p8 = sb("p8", [8, 2], F32)
        cfm = sb("cfm", [128, 8], F32)
        cact = sb("cact", [128, 8, 2], F32)
        gs = sb("gs", [128, 8], F32)
        sh = sb("sh", [128, 8], F32)
        lbt = sb("lbt", [128, 4, 3], F32)
        sm = sb("sm", [128, 16], F32)
        sc = sb("sc", [128, 4, 6], F32)
        negm = sb("negm", [128, 4, 2], F32)
        wgk = sb("wgk", [32, 256], BF16)
        lrT = sb("lrT", [32, 128], BF16)
        d8 = sb("d8", [8, 4, 128], F32)
        a8 = sb("a8", [8, 2], F32)
        ssdT = sb("ssdT", [128, 3, 8], F32)
        ss = sb("ss", [128, 4], F32)
        P = [ps("P%d" % i, [128, 512], F32) for i in range(6)]
        TB = ps("TB", [128, 1024], BF16)
        P7 = ps("P7", [128, 512], F32)
        ident, mask01, perm, ones_bf = (cmat[:, i, :] for i in range(4))

        def act(out, in_, func, reads, writes, scale=1.0, bias=0.0, accum=None):
            kw = {}
            if accum is not None:
                kw["accum_out"] = accum
            r.op(ACT, lambda e: e.activation(out=out, in_=in_, func=func, scale=scale, bias=bias, **kw), reads, writes)

        def tt(eng, out, in0, in1, op, reads, writes):
            r.op(eng, lambda e: e.tensor_tensor(out=out, in0=in0, in1=in1, op=op), reads, writes)

        def ts(eng, out, in0, s1, s2, op0, op1, reads, writes):
            if s2 is None:
                r.op(eng, lambda e: e.tensor_scalar(out=out, in0=in0, scalar1=s1, scalar2=None, op0=op0), reads, writes)
            else:
                r.op(eng, lambda e: e.tensor_scalar(out=out, in0=in0, scalar1=s1, scalar2=s2, op0=op0, op1=op1), reads, writes)

        def stt(out, in0, scalar, in1, op0, op1, reads, writes):
            r.op(DVE, lambda e: e.scalar_tensor_tensor(out=out, in0=in0, scalar=scalar, in1=in1, op0=op0, op1=op1), reads, writes)

        def cp(eng, out, in_, reads, writes):
            r.op(eng, lambda e: e.tensor_copy(out=out, in_=in_), reads, writes)

        def mm(out, lhsT, rhs, start, stop, reads, writes):
            r.op(PE, lambda e: e.matmul(out, lhsT=lhsT, rhs=rhs, start=start, stop=stop), reads, writes)

        def tr(out, in_, reads, writes):
            r.op(PE, lambda e: e.transpose(out=out, in_=in_, identity=ident), list(reads) + ["cmat"], writes)

        def dma(eng, out, in_, reads, writes, key, mode="serial"):
            if key[:3] in SKIP.split(","):
                return
            r.op(eng, lambda e: e.dma_start(out=out, in_=in_), reads, writes, dma_key=key, dma_mode=mode)

        def proj_fm(pbank, pname, col0, ngroups, width=128, pcol0=0):
            for g in range(ngroups):
                for k in range(8):
                    mm(pbank[0:width, pcol0 + g * 128: pcol0 + (g + 1) * 128],
                       w_in[:, k, col0 + g * width: col0 + (g + 1) * width], hT[:, k, :],
                       k == 0, k == 7, ["w_in%d" % k, "hT"], [pname])

        def proj_tm(pbank, pname, col0, n):
            for k in range(8):
                mm(pbank[:, 0:n], hT[:, k, :], w_in[:, k, col0:col0 + n], k == 0, k == 7, ["w_in%d" % k, "hT"], [pname])

        Ff = [f[:].rearrange("p a b -> p (a b)") for f in F]
        dma(SP, Ff[0], cmat_d[:, 0:512], [], ["f0"], "c1", "batch")
        dma(SP, Ff[3][:, 0:128], cmat_d[:, 512:640], [], ["f3"], "c1", "batch")
        dma(SP, Ff[1], gqk_d[:, 0:512], [], ["f1"], "c1", "batch")
        dma(SP, Ff[2], gqk_d[:, 512:1024], [], ["f2"], "c1", "batch")
        dma(SP, sel[:].rearrange("p a b -> p (a b)"), sel_d, [], ["sel"], "c1", "batch")
        dma(SP, retsc[:].rearrange("p a b -> p (a b)"), retsc_d, [], ["retsc"], "c1", "batch")
        dma(SP, cfm[:], cfm_d, [], ["cfm"], "c1", "batch")
        cp(DVE, cmat[:, 0:4, :].rearrange("p a b -> p (a b)"), Ff[0], ["f0"], ["cmat"])
        cp(DVE, cmat[:, 4, :], Ff[3][:, 0:128], ["f3"], ["cmat"])
        cp(DVE, gqk[:, 0:4, :].rearrange("p a b -> p (a b)"), Ff[1], ["f1"], ["gqk"])
        cp(DVE, gqk[:, 4:8, :].rearrange("p a b -> p (a b)"), Ff[2], ["f2"], ["gqk"])
        r.op(DVE, lambda e: e.memset(ones32[:], 1.0), [], ["ones32"])
        r.op(DVE, lambda e: e.memset(lrT[:], 1.0), [], ["lrT"])
        act(sm[:, 0:8], cfm[:], AF.Silu, ["cfm"], ["sm"])
        cp(DVE, cact[:], sm[:, 0:8].unsqueeze(2).to_broadcast([128, 8, 2]), ["sm"], ["cact"])
        wl = dict(ci=0, ring=[0, 1, 2])
        cast_engs = [ACT, DVE, POOL]

        def wload(dst_ap, src_ap, npart, width, bufname):
            ci = wl["ci"]
            j = wl["ring"][ci % len(wl["ring"])]
            sname = "f%d" % j
            st_ap = Ff[j][0:npart, 0:width]
            dma(SP, st_ap, src_ap, [], [sname], "wst%d" % j)
            eng = cast_engs[ci % 3]
            if eng == ACT:
                act(dst_ap, st_ap, AF.Copy, [sname], [bufname])
            else:
                cp(eng, dst_ap, st_ap, [sname], [bufname])
            wl["ci"] = ci + 1

        try:
          for l in range(n_layers):
              last = (l == n_layers - 1)
              xin = x_d if l == 0 else xs_d
              dma(SP, pfm[:], pfm_d[l], [], ["pfm"], "prm%d" % l, "batch")
              dma(SP, p8[:], p8_d[l], [], ["p8"], "prm%d" % l, "batch")
              dma(SP, xt2[:], bgate_d[l], [], ["xt1"], "prm%d" % l, "batch")
              w_in_v = w_in_d[l].rearrange("(k p) n -> p k n", p=128)
              w_ada_v = w_ada_d[l].rearrange("(k p) n -> p k n", p=128)
              cactbc = xt[:].rearrange("p (k n) -> p k n", k=8)
              cp(DVE, cactbc, sm[:, 0:8].unsqueeze(2).to_broadcast([128, 8, 128]), ["sm"], ["xt0"])
              wl["ring"] = [0, 1, 2]

              def ada_block(blk):
                  cg, half = blk // 2, blk % 2
                  j = 3 + (blk % 2)
                  sname = "f%d" % j
                  dma(SP, F[j][:], w_ada_v[:, half * 4:(half + 1) * 4, cg * 128:(cg + 1) * 128], [], [sname], "wst%d" % j)
                  for kk in range(4):
                      k = half * 4 + kk
                      if cg < 16:
                          mm(P7[:, cg * 2:cg * 2 + 2], F[j][:, kk, :], cact[:, k, :], k == 0, k == 7, [sname, "cact"], ["P7"])
                      else:
                          g = cg - 16
                          pb, pn = (P[0], "P0") if g < 4 else (P[1], "P1")
                          mm(pb[:, (g % 4) * 128:(g % 4 + 1) * 128], cactbc[:, k, :], F[j][:, kk, :], k == 0, k == 7,
                             ["xt0", sname], [pn])

              chunks = [(k, c0) for k in range(8) for c0 in range(0, NW, 512)]
              nblk = 0
              for i, (k, c0) in enumerate(chunks):
                  wd = min(512, NW - c0)
                  wload(w_in[:, k, c0:c0 + wd], w_in_v[:, k, c0:c0 + wd], 128, wd, "w_in%d" % k)
                  while nblk < 48 and nblk < (i + 1) * 48 // len(chunks):
                      ada_block(nblk)
                      nblk += 1
              while nblk < 48:
                  ada_block(nblk)
                  nblk += 1
              wload(wgk[0:17, :], wgk_d[l], 17, 256, "wgk")
              wl["ring"] = [0, 1, 2, 3, 4]
              chk("A")
              p7v = P7[:, 0:32].rearrange("p (a b) -> p a b", b=2)
              tt(DVE, sh[:], p7v[:, 0:8, 0], pfm[:, 8:16], ALU.add, ["P7", "pfm"], ["sh"])
              tt(DVE, gs[:], p7v[:, 8:16, 0], pfm[:, 16:24], ALU.add, ["P7", "pfm"], ["gs"])
              stt(gs[:], gs[:], 1.0, pfm[:, 0:8], ALU.add, ALU.mult, ["gs", "pfm"], ["gs"])
              tt(DVE, xt2[:, 0:512], P[0][:], xt2[:, 0:512], ALU.add, ["P0", "xt1"], ["xt1"])
              tt(DVE, xt2[:, 512:1024], P[1][:], xt2[:, 512:1024], ALU.add, ["P1", "xt1"], ["xt1"])
              w_out_v = w_out_d[l].rearrange("(k p) n -> p k n", p=128)
              for k in range(16):
                  for hf in range(2):
                      ci = wl["ci"]
                      j = ci % 5
                      sname = "f%d" % j
                      dma(SP, Ff[j], w_out_v[:, k, hf * 512:(hf + 1) * 512], [], [sname], "wst%d" % j)
                      tt(DVE if ci % 2 == 0 else POOL, w_out[:, k, hf * 512:(hf + 1) * 512], Ff[j], xt2[:, hf * 512:(hf + 1) * 512],
                         ALU.mult, [sname, "xt1"], ["w_out%d" % k])
                      wl["ci"] = ci + 1
              if l == 0:
                  r.op(DVE, lambda e: e.memset(lbt[:, :, 0], 0.0), [], ["lbt"])
              else:
                  act(sm[:, 8:16], pfm[:, 24:32], AF.Exp, ["pfm"], ["sm"])
                  tt(DVE, sm[:, 8:12], sm[:, 8:12], sm[:, 12:16], ALU.add, ["sm"], ["sm"])
                  r.op(DVE, lambda e: e.reciprocal(out=sm[:, 8:12], in_=sm[:, 8:12]), ["sm"], ["sm"])
                  tt(DVE, lbt[:, :, 0], sm[:, 8:12], sm[:, 12:16], ALU.mult, ["sm"], ["lbt"])
              ts(DVE, lbt[:, :, 1], lbt[:, :, 0], -1.0, 1.0, ALU.mult, ALU.add, ["lbt"], ["lbt"])
              ts(DVE, lbt[:, :, 2], lbt[:, :, 1], -1.0, None, ALU.mult, None, ["lbt"], ["lbt"])
              act(a8[:, 0:1], p8[:, 1:2], AF.Exp, ["p8"], ["a8"])
              ts(DVE, a8[:, 1:2], a8[:, 0:1], -1.0, None, ALU.mult, None, ["a8"], ["a8"])
              for (S, nm) in ((S_hg, "S_hg"), (S_rt, "S_rt"), (S_sd, "S_sd"), (S_gl, "S_gl")):
                  r.op(POOL, lambda e, S=S: e.memset(S[:], 0.0), [], [nm])
              r.op(POOL, lambda e: e.memset(ubuf[:, :, 0:3], 0.0), [], ["ubuf"])

              chk("B")
              for ti in range(n_tiles):
                  t0 = ti * 128
                  X = xts[ti % 2]
                  xk = "xt%d" % (ti % 2)

                  def load_x(tj):
                      dma(SP, xts[tj % 2][:], xin[tj * 128:(tj + 1) * 128, :], ["xs%d" % tj] if l > 0 else [], ["xt%d" % (tj % 2)],
                          "xin%d" % (tj % 2))

                  def load_rot(tj):
                      dma(SP, rot[:].rearrange("p a h b -> p (a h b)"), rot_d[tj], [], ["rot"], "rot")

                  if ti == 0:
                      load_x(0)
                      load_rot(0)
                  act(xn[:], X[:], AF.Square, [xk], ["xn", "ss"], accum=ss[:, 0:1])
                  act(ss[:, 1:2], ss[:, 0:1], AF.Ln, ["ss"], ["ss"], scale=1.0 / D, bias=EPS)
                  act(ss[:, 2:3], ss[:, 1:2], AF.Exp, ["ss"], ["ss"], scale=-0.5)
                  ts(DVE, xn[:], X[:], ss[:, 2:3], None, ALU.mult, None, [xk, "ss"], ["xn"])
                  if ti + 1 < n_tiles:
                      load_x(ti + 1)
                  for k in range(8):
                      tr(TB[:, k * 128:(k + 1) * 128], xn[:, k * 128:(k + 1) * 128], ["xn"], ["TB"])
                  for k in range(8):
                      ts(DVE, hT[:, k, :], TB[:, k * 128:(k + 1) * 128], gs[:, k:k + 1], sh[:, k:k + 1], ALU.mult, ALU.add,
                         ["TB", "gs", "sh"], ["hT"])

                  chk("C")
                  def core(groups, heads, S, Sn, scv, scn, vcol, yslot, og_col, norm_div, qz=None, qs=1.0, gb=None, gbn="hg"):
                      nh = len(heads)
                      gb = hg if gb is None else gb
                      for h in range(4):
                          ts(POOL, gb[:, h, :], gb[:, h, :], pfm[:, og_col + h: og_col + h + 1], 1.0, ALU.mult, ALU.mult, [gbn, "pfm"], [gbn])
                      for g in range(groups):
                          tr(TB[:, g * 128:(g + 1) * 128], kt[:, g, :], ["kt"], ["TB"])
                      act(ktT[:, 0:groups, :], TB[:, 0:groups * 128].rearrange("p (a b) -> p a b", b=128), AF.Copy, ["TB"], ["ktT"])
                      for hi, (g, po, dk) in enumerate(heads):
                          if qz is None:
                              mm(P[3][:, hi * 128:(hi + 1) * 128], kt[po:po + dk, g, :], qt[po:po + dk, g, :], True, True,
                                 ["kt", "qt"], ["P3"])
                          else:
                              mm(P[3][:, hi * 128:(hi + 1) * 128], kt[:, g, :], qz[:, hi, :], True, True, ["kt", "hq"], ["P3"])
                      maskt = mask01 if qs == 1.0 else cmat[:, 4, :]
                      if os.environ.get("NEWPT"):
                          stt(pt[:, 0:nh, :], P[3][:, 0:nh * 128].rearrange("p (a b) -> p a b", b=128), 1e25,
                              maskt.unsqueeze(1).to_broadcast([128, nh, 128]), ALU.min, ALU.mult, ["P3", "cmat"], ["pt"])
                      else:
                          ts(DVE, pt[:, 0:nh, :], P[3][:, 0:nh * 128].rearrange("p (a b) -> p a b", b=128), 1e25, -1e25,
                             ALU.min, ALU.max, ["P3"], ["pt"])
                          tt(DVE, pt[:, 0:nh, :], pt[:, 0:nh, :],
                             maskt.unsqueeze(1).to_broadcast([128, nh, 128]), ALU.mult, ["pt", "cmat"], ["pt"])
                      dv = 128
                      for g in range(groups):
                          ts(POOL, sp0[:, g * dv:(g + 1) * dv], S[:, g, :], scv[:, g, 0:1], qs, ALU.mult, ALU.mult, [Sn, scn], ["sp0"])
                      for c in range(2):
                          for hi, (g, po, dk) in enumerate(heads):
                              mm(P[5][po:po + dk, g * 128:(g + 1) * 128], ktT[c * 64:(c + 1) * 64, g, po:po + dk],
                                 vv[c * 64:(c + 1) * 64, vcol + hi * 128: vcol + (hi + 1) * 128], True, True, ["ktT", "vv"], ["P5"])
                          for g in range(groups):
                              act(F[3][:, g, :], P[5][:, g * 128:(g + 1) * 128], AF.Copy, ["P5", scn], ["f3_%d" % g],
                                  scale=scv[:, g, 4 + c:5 + c])
                          for g in range(groups):
                              stt(S[:, g, :], S[:, g, :], scv[:, g, 2 + c:3 + c], F[3][:, g, :], ALU.mult, ALU.add,
                                  [Sn, scn, "f3_%d" % g], [Sn])
                          if c == 0:
                              for g in range(groups):
                                  ts(POOL, sp1[:, g * dv:(g + 1) * dv], S[:, g, :], scv[:, g, 1:2], qs, ALU.mult, ALU.mult, [Sn, scn], ["sp1"])
                              for hi, (g, po, dk) in enumerate(heads):
                                  o = P[4][:, hi * 128:(hi + 1) * 128]
                                  mm(o, vv[:, vcol + hi * 128: vcol + (hi + 1) * 128], pt[:, hi, :], True, False, ["vv", "pt"], ["P4"])
                                  if qz is None:
                                      mm(o[:, 0:64], sp0[po:po + dk, g * dv:(g + 1) * dv], qt[po:po + dk, g, 0:64], False, False,
                                         ["sp0", "qt"], ["P4"])
                                      mm(o[:, 64:128], sp1[po:po + dk, g * dv:(g + 1) * dv], qt[po:po + dk, g, 64:128], False, True,
                                         ["sp1", "qt"], ["P4"])
                                  else:
                                      mm(o[:, 0:64], sp0[:, g * dv:(g + 1) * dv], qz[:, hi, 0:64], False, False, ["sp0", "hq"], ["P4"])
                                      mm(o[:, 64:128], sp1[:, g * dv:(g + 1) * dv], qz[:, hi, 64:128], False, True, ["sp1", "hq"], ["P4"])
                      act(xn[:, 0:512], P[4][:], AF.Square, ["P4"], ["xn"])
                      mm(P7[:, 0:512], ones_bf, xn[:, 0:512], True, True, ["xn", "cmat"], ["P7"])
                      act(F[0][:].rearrange("p a b -> p (a b)"), P7[:], AF.Ln, ["P7"], ["f0"], scale=1.0 / norm_div, bias=EPS)
                      act(F[0][:].rearrange("p a b -> p (a b)"), F[0][:].rearrange("p a b -> p (a b)"), AF.Exp, ["f0"], ["f0"], scale=-0.5)
                      tt(DVE, yT[:, yslot:yslot + 4, :].rearrange("p a b -> p (a b)"), P[4][:], F[0][:].rearrange("p a b -> p (a b)"), ALU.mult,
                         ["P4", "f0"], ["yT"])
                      tt(DVE, yT[:, yslot:yslot + 4, :], yT[:, yslot:yslot + 4, :], gb[:], ALU.mult, ["yT", gbn], ["yT"])

                  def vec_decay_prep(ngr, logf_buf, logf_name):
                      for g in range(ngr):
                          r.op(DVE, lambda e, g=g: e.tensor_tensor_scan(out=F[4][:, g, :], data0=ones32[:], data1=logf_buf[:, g, :],
                                                                         initial=0.0, op0=ALU.mult, op1=ALU.add),
                               [logf_name, "ones32"], ["f4"])
                      cum = F[4]
                      ts(DVE, negm[:, 0:ngr, :], cum[:, 0:ngr, 31:128:64], -1.0, -QL, ALU.mult, ALU.add, ["f4"], ["negm"])
                      for g in range(ngr):
                          for c in range(2):
                              act(F[2][:, g, c * 64:(c + 1) * 64], cum[:, g, c * 64:(c + 1) * 64], AF.Exp, ["f4", "negm"], ["f2"],
                                  bias=negm[:, g, c:c + 1])
                      for g in range(ngr):
                          for c in range(2):
                              act(F[3][:, g, c * 64:(c + 1) * 64], cum[:, g, c * 64:(c + 1) * 64], AF.Exp, ["f4"], ["f3_%d" % g],
                                  scale=-1.0, bias=cum[:, g, 31 + 64 * c: 32 + 64 * c])
                      cp(POOL, sc[:, 0:ngr, 0], cum[:, 0:ngr, 31], ["f4"], ["sc"])
                      tt(POOL, sc[:, 0:ngr, 1], cum[:, 0:ngr, 95], cum[:, 0:ngr, 63], ALU.subtract, ["f4"], ["sc"])
                      cp(POOL, sc[:, 0:ngr, 2], cum[:, 0:ngr, 63], ["f4"], ["sc"])
                      tt(POOL, sc[:, 0:ngr, 3], cum[:, 0:ngr, 127], cum[:, 0:ngr, 63], ALU.subtract, ["f4"], ["sc"])
                      tt(POOL, sc[:, 0:ngr, 4], cum[:, 0:ngr, 63], cum[:, 0:ngr, 31], ALU.subtract, ["f4"], ["sc"])
                      tt(POOL, sc[:, 0:ngr, 5], cum[:, 0:ngr, 127], cum[:, 0:ngr, 95], ALU.subtract, ["f4"], ["sc"])
                      act(sc[:, 0:ngr, :], sc[:, 0:ngr, :], AF.Exp, ["sc"], ["sc"])

                  flat = lambda t: t[:].rearrange("p a b -> p (a b)")

                  r.op(POOL, lambda e: e.memset(yT[:], 0.0), [], ["yT"]) if BR != "hrgs" else None
                  def br_h(before_core=None):
                      proj_fm(P[0], "P0", C_AQ, 4)
                      act(flat(hq), P[0][:], AF.Silu, ["P0"], ["hq"])
                      proj_fm(P[2], "P2", C_AG, 4)
                      act(flat(hg), P[2][:], AF.Silu, ["P2"], ["hg"])
                      if GROUPG:
                          proj_fm(P[2], "P2", C_RG, 4)
                          act(flat(hgB), P[2][:], AF.Silu, ["P2"], ["hgB"])
                      proj_fm(P[1], "P1", C_AF, 4)
                      act(flat(F[0]), P[1][:], AF.Exp, ["P1"], ["f0"], scale=-1.0)
                      act(flat(F[0]), flat(F[0]), AF.Ln, ["f0"], ["f0"], bias=1.0)
                      act(flat(F[1]), flat(F[0]), AF.Exp, ["f0"], ["f1"], scale=-1.0)
                      for h in range(4):
                          act(F[0][:, h, :], F[1][:, h, :], AF.Ln, ["f1_%d" % h, "lbt"], ["f0_%d" % h], scale=lbt[:, h, 1:2], bias=lbt[:, h, 0:1])
                          ts(POOL, F[1][:, h, :], F[1][:, h, :], lbt[:, h, 2:3], lbt[:, h, 1:2], ALU.mult, ALU.add, ["f1_%d" % h, "lbt"], ["f1_%d" % h])
                      vec_decay_prep(4, F[0], "f0")
                      tt(POOL, qt[:], hq[:], F[2][:], ALU.mult, ["hq", "f2"], ["qt"])
                      tt(DVE, kt[:], F[1][:], F[3][:], ALU.mult, ["f1", "f3"], ["kt"])
                      proj_tm(P[0], "P0", C_AI, 512)
                      act(vv[:], P[0][:], AF.Copy, ["P0"], ["vv"])
                      if before_core is not None:
                          before_core()
                      core(4, [(h, 0, 128) for h in range(4)], S_hg, "S_hg", sc, "sc", 0, 0, 32, 128.0, qs=QS)

                  def pre_r():
                      proj_fm(P[1], "P1", C_RQ, 4)
                      proj_fm(P[2], "P2", C_RG, 4)
                      proj_tm(P[0], "P0", C_RV, 512)

                  def pre_g():
                      proj_fm(P[0], "P0", C_GQ, 2)
                      proj_fm(P[2], "P2", C_GG, 4)

                  def pre_s():
                      proj_fm(P[0], "P0", C_MX, 4)
                      proj_fm(P[1], "P1", C_MX + 512, 4)
                      proj_fm(P[2], "P2", C_MZ, 4)

                  PIPE = (BR == "hrgs") and bool(os.environ.get("PIPE"))
                  if "h" in BR and not PIPE:
                      br_h()
                  def br_r(before_core=None, pre=False):
                      for (ccol, tab, dst, dn) in ((C_RQ, 0, qt, "qt"), (C_RK, 4, kt, "kt")):
                          if not (pre and ccol == C_RQ):
                              proj_fm(P[1], "P1", ccol, 4)
                          act(flat(hq), P[1][:], AF.Copy, ["P1"], ["hq"])
                          chk("Ra")
                          mm(P7[:, 0:512], perm, flat(hq), True, True, ["hq", "cmat"], ["P7"])
                          chk("Rb")
                          p1v = P[1][:].rearrange("p (a b) -> p a b", b=128)
                          p7v2 = P7[:].rearrange("p (a b) -> p a b", b=128)
                          RT = os.environ.get("RTEST", "")
                          if RT == "1":
                              tt(DVE, F[0][:], p1v, F[2][:], ALU.mult, ["P1", "f2"], ["f0"])
                          elif RT == "2":
                              cp(DVE, F[0][:], p1v, ["P1"], ["f0"])
                          elif RT == "3":
                              tt(DVE, F[0][:], F[2][:], rot[:, 0, :, :], ALU.mult, ["f2", "rot"], ["f0"])
                          else:
                              tt(DVE, F[0][:], p1v, rot[:, 0, :, :], ALU.mult, ["P1", "rot"], ["f0"])
                          chk("Rc1")
                          tt(DVE, F[1][:], p7v2, rot[:, 1, :, :], ALU.mult, ["P7", "rot"], ["f1"])
                          chk("Rc")
                          tt(POOL, F[0][:], F[0][:], F[1][:], ALU.add, ["f0", "f1"], ["f0"])
                          chk("Rd")
                          tt(DVE, dst[:], F[0][:], gqk[:, tab:tab + 4, :], ALU.mult, ["f0", "gqk"], [dn])
                      if not GROUPG:
                          if not pre:
                              proj_fm(P[2], "P2", C_RG, 4)
                          act(flat(hg), P[2][:], AF.Silu, ["P2"], ["hg"])
                      if not pre:
                          proj_tm(P[0], "P0", C_RV, 512)
                      act(vv[:], P[0][:], AF.Copy, ["P0"], ["vv"])
                      chk("R1")
                      if before_core is not None:
                          before_core()
                      if GROUPG:
                          core(4, [(h, 0, 128) for h in range(4)], S_rt, "S_rt", retsc, "retsc", 0, 4, 36, 128.0, gb=hgB, gbn="hgB")
                      else:
                          core(4, [(h, 0, 128) for h in range(4)], S_rt, "S_rt", retsc, "retsc", 0, 4, 36, 128.0)

                  if "r" in BR and not PIPE:
                      br_r()
                  if not PIPE and ti + 1 < n_tiles:
                      load_rot(ti + 1)
                  def br_g(before_core=None, pre=False):
                      proj_fm(P7, "P7", C_LR, 1, width=16)
                      cp(DVE, lrT[0:16, :], P7[0:16, 0:128], ["P7"], ["lrT"])
                      for g in range(2):
                          mm(P[1][:, g * 128:(g + 1) * 128], wgk[0:17, g * 128:(g + 1) * 128], lrT[0:17, :], True, True, ["wgk", "lrT"], ["P1"])
                      act(F[0][:, 0:2, :], P[1][:, 0:256].rearrange("p (a b) -> p a b", b=128), AF.Exp, ["P1"], ["f0"], scale=-1.0)
                      act(F[0][:, 0:2, :], F[0][:, 0:2, :], AF.Ln, ["f0"], ["f0"], bias=1.0)
                      ts(DVE, F[0][:, 0:2, :], F[0][:, 0:2, :], -1.0 / 16.0, None, ALU.mult, None, ["f0"], ["f0"])
                      vec_decay_prep(2, F[0], "f0")
                      if not pre:
                          proj_fm(P[0], "P0", C_GQ, 2)
                      stt(qt[:, 0:2, :], P[0][:, 0:256].rearrange("p (a b) -> p a b", b=128), 0.125, F[2][:, 0:2, :], ALU.mult, ALU.mult,
                          ["P0", "f2"], ["qt"])
                      proj_fm(P[1], "P1", C_GK, 2)
                      tt(DVE, kt[:, 0:2, :], P[1][:, 0:256].rearrange("p (a b) -> p a b", b=128), F[3][:, 0:2, :], ALU.mult, ["P1", "f3"], ["kt"])
                      if not pre:
                          proj_fm(P[2], "P2", C_GG, 4)
                      act(flat(hg), P[2][:], AF.Silu, ["P2"], ["hg"])
                      proj_tm(P[0], "P0", C_GV, 512)
                      act(vv[:], P[0][:], AF.Copy, ["P0"], ["vv"])
                      r.op(POOL, lambda e: e.memset(hq[:], 0.0), [], ["hq"])
                      for hi, (g, po) in enumerate(((0, 0), (0, 64), (1, 0), (1, 64))):
                          cp(POOL, hq[po:po + 64, hi, :], qt[po:po + 64, g, :], ["qt"], ["hq"])
                      if before_core is not None:
                          before_core()
                      core(2, [(0, 0, 64), (0, 64, 64), (1, 0, 64), (1, 64, 64)], S_gl, "S_gl", sc, "sc", 0, 12, 44, 128.0, qz=hq, qs=QS)

                  if "g" in BR and not PIPE:
                      br_g()
                  def br_s(pre=False):
                      for half in range(2):
                          if not pre:
                              proj_fm(P[half], "P%d" % half, C_MX + half * 512, 4)
                      cp(DVE, ubuf[:, 0:4, 3:131], P[0][:].rearrange("p (a b) -> p a b", b=128), ["P0"], ["ubuf"])
                      act(ubuf[:, 4:8, 3:131], P[1][:].rearrange("p (a b) -> p a b", b=128), AF.Copy, ["P1"], ["ubuf"])
                      cacc = [F[0], F[1]]
                      for g in range(8):
                          ca = cacc[g // 4][:, g % 4, :]
                          cn = "f%d_%d" % (g // 4, g % 4)
                          ts(POOL, ca, ubuf[:, g, 0:128], pfm[:, 56 + g * 4: 57 + g * 4], pfm[:, 48 + g: 49 + g], ALU.mult, ALU.add,
                             ["ubuf", "pfm"], [cn])
                          for j in range(1, 4):
                              stt(ca, ubuf[:, g, j:j + 128], pfm[:, 56 + g * 4 + j: 57 + g * 4 + j], ca, ALU.mult, ALU.add,
                                  ["ubuf", "pfm", cn], [cn])
                      cp(POOL, ubuf[:, :, 0:3], ubuf[:, :, 128:131], ["f0", "f1", "ubuf"], ["ubuf"])
                      act(xc[:, 0:4, :], F[0][:], AF.Silu, ["f0"], ["xc"])
                      act(xc[:, 4:8, :], F[1][:], AF.Silu, ["f1"], ["xc"])
                      if not pre:
                          proj_fm(P[2], "P2", C_MZ, 4)
                      act(flat(hg), P[2][:], AF.Silu, ["P2"], ["hg"])
                      proj_fm(P7, "P7", C_DT, 1, width=8)
                      act(d8[:, 0, :], P7[0:8, 0:128], AF.Exp, ["P7", "p8"], ["d8"], bias=p8[:, 0:1])
                      act(d8[:, 1, :], d8[:, 0, :], AF.Ln, ["d8"], ["d8"], bias=1.0)
                      ts(DVE, d8[:, 0, :], d8[:, 1, :], a8[:, 1:2], None, ALU.mult, None, ["d8", "a8"], ["d8"])
                      for c in range(2):
                          r.op(DVE, lambda e, c=c: e.tensor_tensor_scan(out=d8[:, 2, c * 64:(c + 1) * 64], data0=ones32[0:8, 0:64],
                                                                         data1=d8[:, 0, c * 64:(c + 1) * 64], initial=0.0,
                                                                         op0=ALU.mult, op1=ALU.add), ["d8", "ones32"], ["d8"])
                      for c in range(2):
                          act(d8[:, 3, c * 64:(c + 1) * 64], d8[:, 2, c * 64:(c + 1) * 64], AF.Exp, ["d8"], ["d8"], scale=-1.0,
                              bias=d8[:, 2, c * 64 + 63: c * 64 + 64])
                      tt(DVE, d8[:, 3, :], d8[:, 3, :], d8[:, 1, :], ALU.mult, ["d8"], ["d8"])
                      for i, row in enumerate((1, 3, 2)):
                          r.op(PE, lambda e, i=i, row=row: e.matmul(P7[:, 256 + i * 8: 256 + (i + 1) * 8], lhsT=d8[:, row, :],
                                                                    rhs=sel[:, :, 0], start=True, stop=True), ["d8", "sel"], ["P7"])
                      cp(DVE, ssdT[:].rearrange("p a b -> p (a b)"), P7[:, 256:280], ["P7"], ["ssdT"])
                      for g in range(4):
                          tr(TB[:, g * 128:(g + 1) * 128], xc[:, g, :], ["xc"], ["TB"])
                      tbv = TB[:, 0:512].rearrange("p (a b) -> p a b", b=64)
                      tt(DVE, vv[:].rearrange("p (a b) -> p a b", b=64), tbv, ssdT[:, 0, :].unsqueeze(2).to_broadcast([128, 8, 64]), ALU.mult,
                         ["TB", "ssdT"], ["vv"])
                      tt(DVE, flat(hq).rearrange("p (a b) -> p a b", b=64), tbv, ssdT[:, 1, :].unsqueeze(2).to_broadcast([128, 8, 64]), ALU.mult,
                         ["TB", "ssdT"], ["hq"])
                      v2 = flat(hq)
                      for gi in range(2):
                          tr(TB[:, 512 + gi * 128: 512 + (gi + 1) * 128], xc[:, 4 + gi, :], ["xc"], ["TB"])
                      act(ktT[:, 0:2, :], TB[:, 512:768].rearrange("p (a b) -> p a b", b=128), AF.Copy, ["TB"], ["ktT"])
                      cp(POOL, sp0[:], S_sd[:].rearrange("p a b -> p (a b)"), ["S_sd"], ["sp0"])
                      for gi in range(2):
                          for hh in range(4):
                              h = gi * 4 + hh
                              mm(P7[:, hh * 128:(hh + 1) * 128], sel[:, h, :], d8[:, 2, :], True, True, ["sel", "d8"], ["P7"])
                          for hh in range(4):
                              h = gi * 4 + hh
                              ts(DVE, F[2][:, hh, :], P7[:, hh * 128:(hh + 1) * 128], ssdT[:, 2, h:h + 1], 0.0, ALU.subtract, ALU.min,
                                 ["P7", "ssdT"], ["f2"])
                          act(flat(kt), flat(F[2]), AF.Exp, ["f2"], ["kt"])
                          act(flat(F[3]), P7[:], AF.Exp, ["P7"], ["f3"])
                          mm(P[3][:, 0:128], xc[:, 4 + gi, :], xc[:, 6 + gi, :], True, True, ["xc"], ["P3"])
                          tt(DVE, gm[:], P[3][:, 0:128], mask01, ALU.mult, ["P3", "cmat"], ["gm"])
                          tt(POOL, pt[:], kt[:], gm[:].unsqueeze(1).to_broadcast([128, 4, 128]), ALU.mult, ["kt", "gm"], ["pt"])
                          tt(POOL, qt[:], F[3][:], xc[:, 6 + gi, :].unsqueeze(1).to_broadcast([128, 4, 128]), ALU.mult, ["f3", "xc"], ["qt"])
                          for c in range(2):
                              mm(P[5][:, 0:256], ktT[c * 64:(c + 1) * 64, gi, :], v2[c * 64:(c + 1) * 64, gi * 256:(gi + 1) * 256], True, True,
                                 ["ktT", "hq"], ["P5"])
                              for hh in range(4):
                                  h = gi * 4 + hh
                                  stt(S_sd[:, h, :], S_sd[:, h, :], F[3][:, hh, c * 64 + 63: c * 64 + 64], P[5][:, hh * 64:(hh + 1) * 64],
                                      ALU.mult, ALU.add, ["S_sd", "f3", "P5"], ["S_sd"])
                              if c == 0:
                                  cp(POOL, sp1[:, gi * 256:(gi + 1) * 256], S_sd[:, gi * 4:(gi + 1) * 4, :].rearrange("p a b -> p (a b)"),
                                     ["S_sd"], ["sp1"])
                                  for hh in range(4):
                                      h = gi * 4 + hh
                                      o = P[4][(h % 2) * 64:(h % 2) * 64 + 64, (h // 2) * 128:(h // 2 + 1) * 128]
                                      mm(o, vv[:, h * 64:(h + 1) * 64], pt[:, hh, :], True, False, ["vv", "pt"], ["P4"])
                                      mm(o[:, 0:64], sp0[:, h * 64:(h + 1) * 64], qt[:, hh, 0:64], False, False, ["sp0", "qt"], ["P4"])
                                      mm(o[:, 64:128], sp1[:, h * 64:(h + 1) * 64], qt[:, hh, 64:128], False, True, ["sp1", "qt"], ["P4"])
                      for g in range(4):
                          stt(F[0][:, g, :], xc[:, g, :], pfm[:, 88 + g: 89 + g], P[4][:, g * 128:(g + 1) * 128], ALU.mult, ALU.add,
                              ["xc", "pfm", "P4"], ["f0"])
                      tt(POOL, F[0][:], F[0][:], hg[:], ALU.mult, ["f0", "hg"], ["f0"])
                      act(xn[:, 0:512], flat(F[0]), AF.Square, ["f0"], ["xn"])
                      for gi in range(2):
                          for j in range(2):
                              mm(P7[:, gi * 128:(gi + 1) * 128], ones_bf, xn[:, (2 * gi + j) * 128:(2 * gi + j + 1) * 128], j == 0, j == 1,
                                 ["xn", "cmat"], ["P7"])
                      act(F[1][:, 0:2, :], P7[:, 0:256].rearrange("p (a b) -> p a b", b=128), AF.Ln, ["P7"], ["f1"], scale=1.0 / 256.0, bias=EPS)
                      act(F[1][:, 0:2, :], F[1][:, 0:2, :], AF.Exp, ["f1"], ["f1"], scale=-0.5)
                      for g in range(4):
                          stt(yT[:, 8 + g, :], F[0][:, g, :], pfm[:, 40 + g: 41 + g], F[1][:, g // 2, :], ALU.mult, ALU.mult,
                              ["f0", "pfm", "f1"], ["yT"])

                  if "s" in BR and not PIPE:
                      br_s()
                  if PIPE:
                      br_h(before_core=pre_r)
                      br_r(before_core=pre_g, pre=True)
                      if ti + 1 < n_tiles:
                          load_rot(ti + 1)
                      br_g(before_core=pre_s, pre=True)
                      br_s(pre=True)
                  if dbg:
                      DUMP = os.environ.get("DUMP", "")
                      if DUMP:
                          bufs = dict(qt=qt, kt=kt, hq=hq, hg=hg, pt=pt, ktT=ktT)
                          for hf, nm in enumerate(DUMP.split(",")):
                              if nm.startswith("f"):
                                  j = int(nm[1:])
                                  dma(SP, dbg_d[l, ti, :, hf * 512:(hf + 1) * 512], Ff[j], [nm], ["dbgout"], "dbg")
                              elif nm in ("vv", "sp0", "sp1"):
                                  src = dict(vv=vv, sp0=sp0, sp1=sp1)[nm]
                                  cp(POOL, Ff[hf], src[:], [nm], ["f%d" % hf])
                                  dma(SP, dbg_d[l, ti, :, hf * 512:(hf + 1) * 512], Ff[hf], ["f%d" % hf], ["dbgout"], "dbg")
                              elif nm.startswith("S"):
                                  src = dict(S_hg=S_hg, S_rt=S_rt)[nm]
                                  dma(SP, dbg_d[l, ti, :, hf * 512:(hf + 1) * 512], src[:].rearrange("p a b -> p (a b)"), [nm], ["dbgout"], "dbg")
                              else:
                                  cp(POOL, F[hf][:], bufs[nm][:], [nm], ["f%d" % hf])
                                  dma(SP, dbg_d[l, ti, :, hf * 512:(hf + 1) * 512], Ff[hf], ["f%d" % hf], ["dbgout"], "dbg")
                      else:
                          for hf in range(4):
                              cp(POOL, F[2][:], yT[:, hf * 4:(hf + 1) * 4, :], ["yT"], ["f2"])
                              dma(SP, dbg_d[l, ti, :, hf * 512:(hf + 1) * 512], flat(F[2]), ["f2"], ["dbgout"], "dbg")

                  if last:
                      for half in range(2):
                          dma(SP, flat(F[2 + half]), finalg_d[:, half * 512:(half + 1) * 512], [], ["f%d" % (2 + half)], "fg%d" % half)
                  for half in range(2):
                      for kc in range(16):
                          mm(P[half][:], yT[:, kc, :], w_out[:, kc, half * 512:(half + 1) * 512], kc == 0, kc == 15,
                             ["yT", "w_out%d" % kc], ["P%d" % half])
                      tt(DVE, X[:, half * 512:(half + 1) * 512], P[half][:], X[:, half * 512:(half + 1) * 512], ALU.add,
                         ["P%d" % half, xk], [xk])
                  if last:
                      act(xn[:], X[:], AF.Square, [xk], ["xn", "ss"], accum=ss[:, 0:1])
                      act(ss[:, 1:2], ss[:, 0:1], AF.Ln, ["ss"], ["ss"], scale=1.0 / D, bias=EPS)
                      act(ss[:, 2:3], ss[:, 1:2], AF.Exp, ["ss"], ["ss"], scale=-0.5)
                      for half in range(2):
                          stt(X[:, half * 512:(half + 1) * 512], X[:, half * 512:(half + 1) * 512], ss[:, 2:3], flat(F[2 + half]),
                              ALU.mult, ALU.mult, [xk, "ss", "f%d" % (2 + half)], [xk])
                      dma(SP, y_d[t0:t0 + 128, :], X[:], [xk], ["yout"], "xout%d" % (ti % 2))
                  else:
                      dma(SP, xs_d[t0:t0 + 128, :], X[:], [xk], ["xs%d" % ti], "xout%d" % (ti % 2))
        except _Stop:
            dma(SP, y_d[0:128, :], xt[:], ["xt0"], ["yout"], "xout0")
        r.op(SP, lambda e: None, ["yout"] + (["dbgout"] if dbg else []), [])
        stats = r.emit()
    return nc, stats


def _const_tables(n_tiles):
    L = n_tiles * 128
    ident = np.eye(128, dtype=np.float32)
    s = np.arange(128)[:, None]
    t = np.arange(128)[None, :]
    mask01 = ((s // 64 == t // 64) & (s <= t)).astype(np.float32)
    perm = np.zeros((128, 128), np.float32)
    perm[(np.arange(128) + 64) % 128, np.arange(128)] = 1.0
    ones = np.ones((128, 128), np.float32)
    cmat = np.concatenate([ident, mask01, perm, ones, mask01 * QS], axis=1)
    sel = np.zeros((8, 8, 128), np.float32)
    for h in range(8):
        sel[h, h, :] = 1.0
    gam = 1.0 - np.exp2(-(5.0 + np.arange(4, dtype=np.float64)))
    tp = (np.arange(128) % 64 + 1).astype(np.float64)
    gq = np.stack([gam[h] ** tp for h in range(4)], 0)
    gk = np.stack([gam[h] ** (-tp) * (128.0 ** -0.5) for h in range(4)], 0)
    gqk = np.concatenate([np.broadcast_to(gq[None], (128, 4, 128)), np.broadcast_to(gk[None], (128, 4, 128))], axis=1)
    gqk = np.ascontiguousarray(gqk, dtype=np.float32).reshape(128, 1024)
    retsc = np.zeros((128, 4, 6), np.float32)
    for h in range(4):
        g64 = gam[h] ** 64
        retsc[:, h, :] = [1.0, 1.0, g64, g64, g64, g64]
    retsc = retsc.reshape(128, 24)
    inv_freq = (10000.0 ** (-np.arange(0, 128, 2, dtype=np.float32) / 128)).astype(np.float32)
    ang = np.arange(L, dtype=np.float32)[:, None] * inv_freq[None, :]
    cos = np.cos(ang).astype(np.float32).T
    sin = np.sin(ang).astype(np.float32).T
    cosf = np.concatenate([cos, cos], 0)
    sinf = np.concatenate([-sin, sin], 0)
    rot = np.stack([cosf.reshape(128, n_tiles, 128), sinf.reshape(128, n_tiles, 128)], axis=2)
    rot = np.broadcast_to(rot.transpose(1, 0, 2, 3)[:, :, :, None, :], (n_tiles, 128, 2, 4, 128))
    rot = np.ascontiguousarray(rot).reshape(n_tiles, 128, 1024)
    return dict(cmat=cmat, sel=sel.reshape(8, 1024), gqk=gqk, retsc=retsc, rot=rot)


def _fm(v, n):
    return np.ascontiguousarray(np.asarray(v, np.float32).reshape(n, 128).T)


def _pack_params(inp, n_layers):
    pfm = np.zeros((n_layers, 128, NF), np.float32)
    p8 = np.zeros((n_layers, 8, 2), np.float32)
    bgate = np.zeros((n_layers, 128, D), np.float32)
    wgk = np.zeros((n_layers, 17, 256), np.float32)
    for l in range(n_layers):
        pfm[l, :, 0:8] = _fm(inp["norm_g"][l], 8)
        pfm[l, :, 8:16] = _fm(inp["b_ada"][l][0:D], 8)
        pfm[l, :, 16:24] = _fm(inp["b_ada"][l][D:2 * D], 8)
        pfm[l, :, 24:28] = _fm(inp["hgrn_lb_logits"][0], 4)
        pfm[l, :, 28:32] = _fm(inp["hgrn_lb_logits"][min(1, inp["hgrn_lb_logits"].shape[0] - 1)], 4)
        pfm[l, :, 32:36] = _fm(inp["hgrn_onorm_g"][l], 4)
        pfm[l, :, 36:40] = _fm(inp["ret_onorm_g"][l], 4)
        pfm[l, :, 40:44] = _fm(inp["ssm_norm_g"][l], 4)
        pfm[l, :, 44:48] = _fm(inp["gla_onorm_g"][l], 4)
        pfm[l, :, 48:56] = _fm(inp["ssm_conv_b"][l], 8)
        cw = np.asarray(inp["ssm_conv_w"][l], np.float32)
        pfm[l, :, 56:88] = cw.reshape(4, 8, 128).transpose(2, 1, 0).reshape(128, 32)
        pfm[l, :, 88:92] = _fm(np.repeat(np.asarray(inp["ssm_d"][l], np.float32), 64), 4)
        p8[l, :, 0] = inp["ssm_dt_bias"][l]
        p8[l, :, 1] = inp["ssm_a_log"][l]
        bgate[l] = np.broadcast_to(np.asarray(inp["b_ada"][l][2 * D:3 * D], np.float32)[None, :], (128, D))
        wgk[l, 0:16] = inp["gla_w_gk2"][l]
        wgk[l, 16] = inp["gla_b_gk2"][l]
    return pfm, p8, bgate, wgk


_CACHE = {}


def run(inputs, n_tiles, n_layers, core_batches, dbg=False):
    inp = {k: np.asarray(v) for k, v in inputs.items()}
    key = (n_tiles, n_layers, dbg)
    if key not in _CACHE:
        _CACHE[key] = build_program(n_tiles, n_layers, dbg)
    nc, stats = _CACHE[key]
    L = n_tiles * 128
    consts = _const_tables(n_tiles)
    pfm, p8, bgate, wgk = _pack_params(inp, n_layers)
    finalg = np.ascontiguousarray(np.broadcast_to(np.asarray(inp["final_g"], np.float32)[None, :], (128, D)))
    shared = dict(w_in=np.ascontiguousarray(inp["w_in"][:n_layers], dtype=np.float32),
                  w_out=np.ascontiguousarray(inp["w_out"][:n_layers], dtype=np.float32),
                  w_ada=np.ascontiguousarray(inp["w_ada"][:n_layers], dtype=np.float32),
                  pfm=pfm, p8=p8, bgate=bgate, wgk=wgk, finalg=finalg, **consts)
    in_maps = []
    for b in core_batches:
        m = dict(shared)
        m["x"] = np.ascontiguousarray(inp["x"][b, :L], dtype=np.float32)
        m["cfm"] = _fm(inp["c"][b], 8)
        in_maps.append(m)
    res = run_bass_kernel_spmd(nc, in_maps, core_ids=list(range(len(core_batches))))
    return res, stats


def kernel(**inputs):
    B, Lfull, _ = inputs["x"].shape
    n_tiles = Lfull // 128
    core_batches = [i % B for i in range(8)]
    res, _ = run(inputs, n_tiles, 2, core_batches)
    out = np.stack([np.asarray(res.results[b]["y"], dtype=np.float32) for b in range(B)], axis=0)
    return out
```
